# Optimizing a Trainium2 kernel written in Bass

```python
import math
import jax, jax.numpy as jnp
from jax import lax
import numpy as np

D_MODEL = 1024
BATCH = 2
SEQ = 8192
DEPTH = 4

N_MIXERS = 4
DN_ALPHA = (2.0 * DEPTH) ** 0.25
DN_BETA = (8.0 * DEPTH) ** -0.25
LN_EPS = 1e-5

GDN_HEADS = 8
GDN_DK = D_MODEL // GDN_HEADS
GDN_DV = D_MODEL // GDN_HEADS
GDN_CONV = 4
GDN_CHUNK = 64
RET_HEADS = 4
RET_DK = D_MODEL // RET_HEADS
RET_DV = 2 * D_MODEL // RET_HEADS
RET_CHUNK = 128
RET_ROPE_BASE = 10000.0
GMLP_CHUNK = 128
GMLP_WIDTH = 2 * D_MODEL
GMLP_GROUPS = 8
SB_HEADS = 16
SB_DH = D_MODEL // SB_HEADS
SB_BLOCK = 128
FFN_HIDDEN = ((8 * D_MODEL // 3 + 255) // 256) * 256
FFN_CONV = 3

kernel_name = 'hybrid_interleaved_gdn_ret_gmlp_sb_trunk'

F32 = jnp.float32


def _standardize(x, eps):
    xf = x.astype(F32)
    mu = jnp.mean(xf, axis=-1, keepdims=True)
    xc = xf - mu
    var = jnp.mean(xc * xc, axis=-1, keepdims=True)
    return xc * lax.rsqrt(var + eps)


def layer_norm(x, g, b):
    return (_standardize(x, LN_EPS) * g.astype(F32) + b.astype(F32)).astype(x.dtype)


def _l2norm(x, eps=1e-6):
    return x * lax.rsqrt(jnp.sum(x * x, axis=-1, keepdims=True) + eps)


def causal_dwconv(x, w):
    k_w = w.shape[0]
    s = x.shape[1]
    xp = jnp.pad(x, ((0, 0), (k_w - 1, 0), (0, 0)))
    y = xp[:, k_w - 1:k_w - 1 + s] * w[k_w - 1]
    for j in range(k_w - 1):
        y = y + xp[:, j:j + s] * w[j]
    return y


def _chunk_heads(t, n_heads, chunk):
    b, s, hd = t.shape
    return t.astype(F32).reshape(b, s // chunk, chunk, n_heads, hd // n_heads).transpose(0, 3, 1, 2, 4)


def _unchunk_heads(t):
    b, h, n, c, d = t.shape
    return t.transpose(0, 2, 3, 1, 4).reshape(b, n * c, h, d)


def gated_deltanet(h, w_in, conv_w, a_log, dt_bias, norm_w, w_out):
    H, dk, dv, C = GDN_HEADS, GDN_DK, GDN_DV, GDN_CHUNK
    b_, s, _ = h.shape
    n_qkv = 2 * H * dk + H * dv
    proj = h @ w_in
    qkv, z, a, bt = jnp.split(proj, [n_qkv, n_qkv + H * dv, n_qkv + H * dv + H], axis=-1)
    qkv = jax.nn.silu(causal_dwconv(qkv, conv_w))
    q, k, v = jnp.split(qkv, [H * dk, 2 * H * dk], axis=-1)
    q = _l2norm(_chunk_heads(q, H, C)) * (dk ** -0.5)
    k = _l2norm(_chunk_heads(k, H, C))
    v = _chunk_heads(v, H, C)
    beta = jax.nn.sigmoid(_chunk_heads(bt, H, C)[..., 0])
    g = -jnp.exp(a_log.astype(F32))[:, None, None] * jax.nn.softplus(
        _chunk_heads(a, H, C)[..., 0] + dt_bias.astype(F32)[:, None, None])
    gc = jnp.cumsum(g, axis=-1)
    idx = jnp.arange(C)
    causal = idx[:, None] >= idx[None, :]
    strict = idx[:, None] > idx[None, :]
    diff = gc[..., :, None] - gc[..., None, :]
    decay = jnp.where(causal, jnp.exp(jnp.where(causal, diff, 0.0)), 0.0)
    kb = k * beta[..., None]
    kk = jnp.where(strict, jnp.einsum('bhncd,bhnmd->bhncm', kb, k) * decay, 0.0)
    eye = jnp.eye(C, dtype=F32)
    rhs = jnp.concatenate([v * beta[..., None], kb * jnp.exp(gc)[..., None]], axis=-1)
    sol = lax.linalg.triangular_solve(kk + eye, rhs, left_side=True, lower=True, unit_diagonal=True)
    u, w = sol[..., :dv], sol[..., dv:]
    qk = jnp.where(causal, jnp.einsum('bhncd,bhnmd->bhncm', q, k) * decay, 0.0)

    def step(state, inp):
        q_n, k_n, u_n, w_n, qk_n, g_n = inp
        v_new = u_n - jnp.einsum('bhck,bhkv->bhcv', w_n, state)
        o = (jnp.einsum('bhck,bhkv->bhcv', q_n * jnp.exp(g_n)[..., None], state)
             + jnp.einsum('bhcm,bhmv->bhcv', qk_n, v_new))
        g_last = g_n[..., -1:]
        state = (state * jnp.exp(g_last)[..., None]
                 + jnp.einsum('bhck,bhcv->bhkv', k_n * jnp.exp(g_last - g_n)[..., None], v_new))
        return state, o

    xs = tuple(jnp.moveaxis(t, 2, 0) for t in (q, k, u, w, qk, gc))
    state0 = jnp.zeros((b_, H, dk, dv), F32)
    _, o = lax.scan(step, state0, xs)
    o = _unchunk_heads(jnp.moveaxis(o, 0, 2))
    o = o * lax.rsqrt(jnp.mean(o * o, axis=-1, keepdims=True) + 1e-6) * norm_w.astype(F32)
    o = o * jax.nn.silu(z.astype(F32).reshape(b_, s, H, dv))
    return o.reshape(b_, s, H * dv).astype(h.dtype) @ w_out


def retention(h, w_in, w_out):
    H, dk, dv, C = RET_HEADS, RET_DK, RET_DV, RET_CHUNK
    b_, s, _ = h.shape
    q, k, v, gate = jnp.split(h @ w_in, [H * dk, 2 * H * dk, 2 * H * dk + H * dv], axis=-1)
    pos = jnp.arange(s, dtype=F32)
    inv_freq = RET_ROPE_BASE ** (-jnp.linspace(0.0, 1.0, dk // 2, dtype=F32))
    ang = pos[:, None] * inv_freq[None, :]
    cos_a, sin_a = jnp.cos(ang)[:, None, :], jnp.sin(ang)[:, None, :]

    def rot(t):
        t = t.astype(F32).reshape(b_, s, H, dk)
        t1, t2 = t[..., :dk // 2], t[..., dk // 2:]
        return jnp.concatenate([t1 * cos_a - t2 * sin_a, t1 * sin_a + t2 * cos_a], axis=-1).reshape(b_, s, H * dk)

    q = _chunk_heads(rot(q), H, C)
    k = _chunk_heads(rot(k), H, C) * (dk ** -0.5)
    v = _chunk_heads(v, H, C)
    log_gamma = jnp.log(1.0 - jnp.power(2.0, -5.0 - jnp.arange(H, dtype=F32)))
    idx = jnp.arange(C, dtype=F32)
    rel = idx[:, None] - idx[None, :]
    dmask = jnp.where(rel >= 0, jnp.exp(jnp.maximum(rel, 0.0) * log_gamma[:, None, None]), 0.0)
    scores = jnp.einsum('bhncd,bhnmd->bhncm', q, k) * dmask[None, :, None]
    intra = jnp.einsum('bhncm,bhnmv->bhncv', scores, v)
    zeta = jnp.exp((C - 1.0 - idx)[None, :] * log_gamma[:, None])
    xi = jnp.exp((idx + 1.0)[None, :] * log_gamma[:, None])
    gamma_c = jnp.exp(C * log_gamma)

    def step(state, inp):
        q_n, k_n, v_n = inp
        o = jnp.einsum('bhck,bhkv->bhcv', q_n, state) * xi[None, :, :, None]
        state = (state * gamma_c[None, :, None, None]
                 + jnp.einsum('bhck,bhcv->bhkv', k_n * zeta[None, :, :, None], v_n))
        return state, o

    xs = tuple(jnp.moveaxis(t, 2, 0) for t in (q, k, v))
    _, inter = lax.scan(step, jnp.zeros((b_, H, dk, dv), F32), xs)
    o = _unchunk_heads(intra + jnp.moveaxis(inter, 0, 2))
    o = _standardize(o, 1e-6).reshape(b_, s, H * dv)
    o = o * jax.nn.silu(gate.astype(F32))
    return o.astype(h.dtype) @ w_out


def chunked_gmlp(h, w_in, ln_g, ln_b, w_s, b_s, w_out):
    C, G, W = GMLP_CHUNK, GMLP_GROUPS, GMLP_WIDTH
    b_, s, _ = h.shape
    u, v = jnp.split(jax.nn.gelu(h @ w_in, approximate=False), 2, axis=-1)
    v = layer_norm(v, ln_g, ln_b).reshape(b_, s // C, C, G, W // G)
    causal = jnp.tril(jnp.ones((C, C), dtype=bool))
    ws = jnp.where(causal, w_s, 0.0).astype(v.dtype)
    vs = jnp.einsum('gts,bnsgd->bntgd', ws, v) + b_s.T.astype(v.dtype)[None, None, :, :, None]
    return (u * vs.reshape(b_, s, W)) @ w_out


def stick_breaking(h, w_in, w_out):
    H, dh, T = SB_HEADS, SB_DH, SB_BLOCK
    b_, s, _ = h.shape
    nb = s // T
    q, k, v = jnp.split(h @ w_in, 3, axis=-1)
    q, k, v = (t.reshape(b_, s, H, dh).transpose(0, 2, 1, 3) for t in (q, k, v))
    qb = q.reshape(b_, H, nb, T, dh).transpose(2, 0, 1, 3, 4)
    key_pos = jnp.arange(s)
    scale = dh ** -0.5

    def block(args):
        q_blk, blk = args
        z = jnp.einsum('bhtd,bhsd->bhts', q_blk, k).astype(F32) * scale
        q_pos = blk * T + jnp.arange(T)
        strict = key_pos[None, :] < q_pos[:, None]
        log_1mb = jnp.where(strict, jax.nn.log_sigmoid(-z), 0.0)
        after = lax.cumsum(log_1mb, axis=3, reverse=True) - log_1mb
        a = jnp.where(strict, jnp.exp(jax.nn.log_sigmoid(z) + after), 0.0)
        return jnp.einsum('bhts,bhsd->bhtd', a.astype(v.dtype), v)

    o = lax.map(block, (qb, jnp.arange(nb)))
    o = o.transpose(1, 0, 3, 2, 4).reshape(b_, s, H * dh)
    return o @ w_out


def conv_ffn(h, w_up, conv_w, conv_b, w_down):
    gate, up = jnp.split(h @ w_up, 2, axis=-1)
    gate = causal_dwconv(gate, conv_w) + conv_b
    return (jax.nn.silu(gate) * up) @ w_down


def setup_inputs(seed: int = 0) -> dict:
    key = jax.random.key(seed)
    ks = jax.random.split(key, 32)
    D, F = D_MODEL, FFN_HIDDEN

    def nrm(i, shape, scale):
        return jax.random.normal(ks[i], shape, F32) * scale

    gdn_qkv = 2 * GDN_HEADS * GDN_DK + GDN_HEADS * GDN_DV
    gdn_in = gdn_qkv + GDN_HEADS * GDN_DV + 2 * GDN_HEADS
    ret_in = 2 * RET_HEADS * RET_DK + 2 * RET_HEADS * RET_DV
    a_vals = jax.random.uniform(ks[14], (GDN_HEADS,), F32, 1.0, 16.0)
    dt = jnp.exp(jax.random.uniform(ks[15], (GDN_HEADS,), F32, math.log(1e-3), math.log(1e-1)))
    return {
        'x': nrm(0, (BATCH, SEQ, D), 1.0),
        'c': nrm(1, (BATCH, D), 1.0),
        'cond_w': nrm(2, (D, D), D ** -0.5),
        'cond_b': nrm(3, (D,), 0.01),
        'ada_w': nrm(4, (DEPTH, D, 6 * D), 0.1 * D ** -0.5),
        'ada_b': nrm(5, (DEPTH, 6 * D), 0.01),
        'ln_g': 1.0 + nrm(6, (DEPTH, 2, D), 0.02),
        'ln_b': nrm(7, (DEPTH, 2, D), 0.02),
        'ffn_up': nrm(8, (DEPTH, D, 2 * F), D ** -0.5),
        'ffn_conv_w': nrm(9, (DEPTH, FFN_CONV, F), FFN_CONV ** -0.5),
        'ffn_conv_b': nrm(10, (DEPTH, F), 0.01),
        'ffn_down': nrm(11, (DEPTH, F, D), DN_BETA * F ** -0.5),
        'gdn_w_in': nrm(12, (D, gdn_in), D ** -0.5),
        'gdn_conv_w': nrm(13, (GDN_CONV, gdn_qkv), GDN_CONV ** -0.5),
        'gdn_a_log': jnp.log(a_vals),
        'gdn_dt_bias': dt + jnp.log(-jnp.expm1(-dt)),
        'gdn_norm_w': 1.0 + nrm(16, (GDN_DV,), 0.02),
        'gdn_w_out': nrm(17, (GDN_HEADS * GDN_DV, D), DN_BETA * (GDN_HEADS * GDN_DV) ** -0.5),
        'ret_w_in': nrm(18, (D, ret_in), D ** -0.5),
        'ret_w_out': nrm(19, (RET_HEADS * RET_DV, D), DN_BETA * (RET_HEADS * RET_DV) ** -0.5),
        'gmlp_w_in': nrm(20, (D, 2 * GMLP_WIDTH), D ** -0.5),
        'gmlp_ln_g': 1.0 + nrm(21, (GMLP_WIDTH,), 0.02),
        'gmlp_ln_b': nrm(22, (GMLP_WIDTH,), 0.02),
        'gmlp_w_s': nrm(23, (GMLP_GROUPS, GMLP_CHUNK, GMLP_CHUNK), GMLP_CHUNK ** -0.5),
        'gmlp_b_s': 1.0 + nrm(24, (GMLP_GROUPS, GMLP_CHUNK), 0.01),
        'gmlp_w_out': nrm(25, (GMLP_WIDTH, D), DN_BETA * GMLP_WIDTH ** -0.5),
        'sb_w_in': nrm(26, (D, 3 * D), D ** -0.5),
        'sb_w_out': nrm(27, (D, D), DN_BETA * D ** -0.5),
    }


def reference(x, c, cond_w, cond_b, ada_w, ada_b, ln_g, ln_b, ffn_up, ffn_conv_w, ffn_conv_b, ffn_down,
              gdn_w_in, gdn_conv_w, gdn_a_log, gdn_dt_bias, gdn_norm_w, gdn_w_out,
              ret_w_in, ret_w_out,
              gmlp_w_in, gmlp_ln_g, gmlp_ln_b, gmlp_w_s, gmlp_b_s, gmlp_w_out,
              sb_w_in, sb_w_out):
    mixers = (
        lambda t: gated_deltanet(t, gdn_w_in, gdn_conv_w, gdn_a_log, gdn_dt_bias, gdn_norm_w, gdn_w_out),
        lambda t: retention(t, ret_w_in, ret_w_out),
        lambda t: chunked_gmlp(t, gmlp_w_in, gmlp_ln_g, gmlp_ln_b, gmlp_w_s, gmlp_b_s, gmlp_w_out),
        lambda t: stick_breaking(t, sb_w_in, sb_w_out),
    )
    e = jax.nn.silu(c @ cond_w + cond_b)
    for i in range(DEPTH):
        mod = (e @ ada_w[i] + ada_b[i])[:, None, :]
        sh1, sc1, g1, sh2, sc2, g2 = jnp.split(mod, 6, axis=-1)
        y = mixers[i % N_MIXERS](x * (1.0 + sc1) + sh1)
        x = layer_norm(DN_ALPHA * x + (1.0 + g1) * y, ln_g[i, 0], ln_b[i, 0])
        y = conv_ffn(x * (1.0 + sc2) + sh2, ffn_up[i], ffn_conv_w[i], ffn_conv_b[i], ffn_down[i])
        x = layer_norm(DN_ALPHA * x + (1.0 + g2) * y, ln_g[i, 1], ln_b[i, 1])
    return x
```

```python
import contextlib
import math

import numpy as np
import ml_dtypes

import concourse.bass as bass
import concourse.mybir as mybir
from concourse.bass_utils import run_bass_kernel_spmd

F32 = mybir.dt.float32
BF16 = mybir.dt.bfloat16
AF = mybir.ActivationFunctionType
ALU = mybir.AluOpType
AX = mybir.AxisListType

NCORES = 8
D = 1024
B = 2
S = 8192
DEPTH = 4
FH = 2816
ALPHA = (2.0 * DEPTH) ** 0.25
LN_EPS = 1e-5

ENGS = ("pe", "act", "dve", "pool", "sp")


def _prod(xs):
    r = 1
    for v in xs:
        r *= int(v)
    return r


class Prog:
    N_DMA_SLOTS = 12

    def __init__(self, nc):
        self.nc = nc
        self.stack = contextlib.ExitStack()
        self.streams = {e: [] for e in ENGS}
        self.cnt = {e: 0 for e in ENGS}
        self.seen = {e: {} for e in ENGS}
        self.acc = {}
        self.esem = {e: self.stack.enter_context(nc.semaphore("sem_" + e)) for e in ENGS}
        self.semobj = {("e", e): self.esem[e] for e in ENGS}
        self.dma_slots = {}
        self.dma_next = {}
        for q in ("sp", "pool", "act"):
            self.dma_slots[q] = []
            self.dma_next[q] = 0
        self.out_dmas = []
        self.psum_names = set()

    def sb(self, name, shape, dtype=F32):
        return self.stack.enter_context(self.nc.sbuf_tensor(name, list(shape), dtype))

    def ps(self, name, shape, dtype=F32):
        self.psum_names.add(name)
        return self.stack.enter_context(self.nc.psum_tensor(name, list(shape), dtype))

    def dram(self, name, shape, dtype, kind):
        return self.nc.dram_tensor(name, list(shape), dtype, kind=kind).ap()

    @staticmethod
    def region(ap):
        t = ap.tensor
        pairs = [(int(s), int(c)) for s, c in ap.ap]
        off = int(ap.offset)
        kind = type(t).__name__
        if kind.startswith("DRam"):
            ext = sum((c - 1) * abs(s) for s, c in pairs)
            return (t.name, 0, 0, off, off + ext)
        fsz = _prod(t.shape[1:])
        p0, f0 = divmod(off, fsz)
        ps_, pc = pairs[0]
        pstep = ps_ // fsz if ps_ else 0
        ext = sum((c - 1) * abs(s) for s, c in pairs[1:])
        f1 = f0 + ext
        if kind.startswith("PSum"):
            epb = 2048 // (2 if t.dtype == BF16 else 4)
            f0 = (f0 // epb) * epb
            f1 = (f1 // epb) * epb + epb - 1
        return (t.name, p0, p0 + (pc - 1) * pstep, f0, f1)

    def _deps(self, eng, reads, writes, rec_key):
        need = {}

        def add(k, v):
            if need.get(k, 0) < v:
                need[k] = v

        rr = [self.region(a) for a in reads]
        ww = [self.region(a) for a in writes]
        for (name, pl, ph, fl, fh) in rr:
            ps_rar = name in self.psum_names
            for rec in self.acc.get(name, ()):
                if (rec[5] or (ps_rar and rec[4] != eng)) and not (rec[1] < pl or rec[0] > ph or rec[3] < fl or rec[2] > fh):
                    add(rec[6], rec[7])
        for (name, pl, ph, fl, fh) in ww:
            for rec in self.acc.get(name, ()):
                if not (rec[1] < pl or rec[0] > ph or rec[3] < fl or rec[2] > fh):
                    add(rec[6], rec[7])
        for (name, pl, ph, fl, fh) in ww:
            lst = self.acc.setdefault(name, [])
            lst[:] = [r for r in lst if not (r[0] >= pl and r[1] <= ph and r[2] >= fl and r[3] <= fh)]
            lst.append((pl, ph, fl, fh, eng, True, rec_key[0], rec_key[1]))
        for (name, pl, ph, fl, fh) in rr:
            lst = self.acc.setdefault(name, [])
            lst[:] = [r for r in lst if not ((not r[5]) and r[6] == rec_key[0]
                                             and r[0] >= pl and r[1] <= ph and r[2] >= fl and r[3] <= fh)]
            lst.append((pl, ph, fl, fh, eng, False, rec_key[0], rec_key[1]))
        waits = []
        own = ("e", eng)
        for k, v in need.items():
            if k == own:
                if eng == "pe" or v > self.cnt[eng]:
                    continue
            if self.seen[eng].get(k, 0) >= v:
                continue
            self.seen[eng][k] = v
            waits.append((k, v))
        return waits

    def op(self, eng, name, args, kwargs, reads, writes, inc=True):
        own = ("e", eng)
        val = self.cnt[eng] + 1
        waits = self._deps(eng, reads, writes, (own, val))
        if inc:
            self.cnt[eng] = val
        self.streams[eng].append((name, args, kwargs, waits, own if inc else None, 1))

    def dma(self, q, out, in_, is_output=False, **kwargs):
        slots = self.dma_slots[q]
        if len(slots) < self.N_DMA_SLOTS:
            key = ("d", q, len(slots))
            sem = self.stack.enter_context(self.nc.semaphore("dsem_%s_%d" % (q, len(slots))))
            self.semobj[key] = sem
            slots.append([key, 0])
            slot = slots[-1]
        else:
            slot = slots[self.dma_next[q] % self.N_DMA_SLOTS]
        self.dma_next[q] += 1
        key, uses = slot
        val = 16 * (uses + 1)
        waits = self._deps(q, [in_], [out], (key, val))
        if uses > 0 and self.seen[q].get(key, 0) < 16 * uses:
            self.seen[q][key] = 16 * uses
            waits.append((key, 16 * uses))
        slot[1] = uses + 1
        self.streams[q].append(("dma_start", (), dict(out=out, in_=in_, **kwargs), waits, key, 16))
        if is_output:
            self.out_dmas.append((q, key, val))

    def mm(self, out, lhsT, rhs, start=True, stop=True, inc=None):
        if inc is None:
            inc = stop
        self.op("pe", "matmul", (out, lhsT, rhs), dict(start=start, stop=stop), [lhsT, rhs], [out], inc=inc)

    def transpose(self, out, in_, ident, inc=True):
        self.op("pe", "transpose", (out, in_, ident), {}, [in_, ident], [out], inc=inc)

    def act(self, out, in_, func, bias=None, scale=None, accum_out=None, eng="act"):
        kw = {}
        reads = [in_]
        writes = [out]
        if bias is not None:
            kw["bias"] = bias
            if not isinstance(bias, (int, float)):
                reads.append(bias)
        if scale is not None:
            kw["scale"] = scale
            if not isinstance(scale, (int, float)):
                reads.append(scale)
        if accum_out is not None:
            kw["accum_out"] = accum_out
            writes.append(accum_out)
        self.op(eng, "activation", (out, in_, func), kw, reads, writes)

    def tt(self, eng, out, in0, in1, op):
        self.op(eng, "tensor_tensor", (out, in0, in1, op), {}, [in0, in1], [out])

    def ts(self, eng, out, in0, s1, s2, op0, op1=None, accum_out=None):
        reads = [in0]
        for s in (s1, s2):
            if s is not None and not isinstance(s, (int, float)):
                reads.append(s)
        kw = {}
        writes = [out]
        if accum_out is not None:
            kw["accum_out"] = accum_out
            writes.append(accum_out)
        if op1 is None:
            self.op(eng, "tensor_scalar", (out, in0, s1, None, op0), kw, reads, writes)
        else:
            self.op(eng, "tensor_scalar", (out, in0, s1, s2, op0, op1), kw, reads, writes)

    def stt(self, eng, out, in0, scalar, in1, op0, op1):
        reads = [in0, in1]
        if not isinstance(scalar, (int, float)):
            reads.append(scalar)
        self.op(eng, "scalar_tensor_tensor", (out, in0, scalar, in1, op0, op1), {}, reads, [out])

    def copy(self, eng, out, in_):
        if eng == "act":
            self.op(eng, "copy", (out, in_), {}, [in_], [out])
        else:
            self.op(eng, "tensor_copy", (out, in_), {}, [in_], [out])

    def memset(self, eng, ap, val):
        self.op(eng, "memset", (ap, val), {}, [], [ap])

    def generic(self, eng, name, args, reads, writes, **kwargs):
        self.op(eng, name, tuple(args), kwargs, reads, writes)

    def check(self):
        counts = {}
        ptr = {e: 0 for e in ENGS}
        while True:
            progress = False
            for e in ENGS:
                st = self.streams[e]
                while ptr[e] < len(st):
                    name, args, kwargs, waits, inc, amt = st[ptr[e]]
                    if any(counts.get(k, 0) < v for k, v in waits):
                        break
                    if inc is not None:
                        counts[inc] = counts.get(inc, 0) + amt
                    ptr[e] += 1
                    progress = True
            if all(ptr[e] == len(self.streams[e]) for e in ENGS):
                return
            if not progress:
                msg = []
                for e in ENGS:
                    if ptr[e] < len(self.streams[e]):
                        name, args, kwargs, waits, inc, amt = self.streams[e][ptr[e]]
                        bad = [(k, v, counts.get(k, 0)) for k, v in waits if counts.get(k, 0) < v]
                        msg.append("%s stuck at %d/%d (%s) waiting %s" % (e, ptr[e], len(self.streams[e]), name, bad))
                raise RuntimeError("DEADLOCK in recorded program:\n" + "\n".join(msg))

    def emit(self):
        nc = self.nc
        self.check()
        tail = {}
        for q, key, val in self.out_dmas:
            d = tail.setdefault(q, {})
            d[key] = max(d.get(key, 0), val)

        def replay(e, eng):
            for (name, args, kwargs, waits, inc, amt) in self.streams[eng]:
                for k, v in waits:
                    e.wait_ge(self.semobj[k], v)
                ins = getattr(e, name)(*args, **kwargs)
                if inc is not None:
                    ins.then_inc(self.semobj[inc], amt)
            for k, v in tail.get(eng, {}).items():
                e.wait_ge(self.semobj[k], v)

        with nc.Block() as block:
            @block.tensor
            def _(e):
                replay(e, "pe")

            @block.scalar
            def _(e):
                replay(e, "act")

            @block.vector
            def _(e):
                replay(e, "dve")

            @block.gpsimd
            def _(e):
                replay(e, "pool")

            @block.sync
            def _(e):
                replay(e, "sp")
        self.stack.close()

    def stats(self):
        return {e: len(self.streams[e]) for e in ENGS}


def build_C():
    nc = bass.Bass("TRN2", target_bir_lowering=False)
    P = Prog(nc)
    cT_d = P.dram("cT", [128, 8, 2], F32, "ExternalInput")
    cw_d = P.dram("cond_w", [128, 8, 1024], F32, "ExternalInput")
    cb_d = P.dram("cond_b", [128, 8], F32, "ExternalInput")
    aw_d = P.dram("ada_w", [128, 8, 3072], F32, "ExternalInput")
    ab_d = P.dram("ada_b", [128, 24], F32, "ExternalInput")
    out_d = P.dram("modpm", [128, 24, 2], F32, "ExternalOutput")

    cT = P.sb("cT_sb", [128, 8, 2])
    cw = P.sb("cw_sb", [128, 8, 1024])
    cb = P.sb("cb_sb", [128, 8])
    aw = P.sb("aw_sb", [128, 8, 3072])
    ab = P.sb("ab_sb", [128, 24])
    eT = P.sb("eT_sb", [128, 8, 2])
    mo = P.sb("mo_sb", [128, 24, 2])
    e_ps = P.ps("e_ps", [128, 8, 2])
    m_ps = P.ps("m_ps", [128, 24, 2])

    P.dma("sp", cT[:], cT_d)
    P.dma("sp", cb[:], cb_d)
    P.dma("sp", ab[:], ab_d)
    for kc in range(8):
        P.dma("sp", cw[:, kc, :], cw_d[:, kc, :])
    for kc in range(8):
        P.dma("sp", aw[:, kc, :], aw_d[:, kc, :])
    for jc in range(8):
        for kc in range(8):
            P.mm(e_ps[:, jc, :], cw[:, kc, jc * 128:(jc + 1) * 128], cT[:, kc, :], start=(kc == 0), stop=(kc == 7))
        P.act(eT[:, jc, :], e_ps[:, jc, :], AF.Silu, bias=cb[:, jc:jc + 1])
    for jc in range(24):
        for kc in range(8):
            P.mm(m_ps[:, jc, :], aw[:, kc, jc * 128:(jc + 1) * 128], eT[:, kc, :], start=(kc == 0), stop=(kc == 7))
        P.act(mo[:, jc, :], m_ps[:, jc, :], AF.Identity, bias=ab[:, jc:jc + 1])
    P.dma("sp", out_d, mo[:], is_output=True)
    P.emit()
    return nc


def pm(v, n=128):
    v = np.asarray(v)
    k = v.shape[-1] // n
    return np.ascontiguousarray(np.moveaxis(v.reshape(v.shape[:-1] + (k, n)), -1, 0))


def wtile(w):
    w = np.asarray(w)
    K, N = w.shape
    return np.ascontiguousarray(w.reshape(K // 128, 128, N).transpose(1, 0, 2))


def run_C(inp):
    nc = build_C()
    ada_flat = np.asarray(inp["ada_w"]).transpose(1, 0, 2).reshape(1024, 4 * 6144)
    adab_flat = np.asarray(inp["ada_b"]).reshape(4 * 6144)
    cT = pm(inp["c"])
    cT = np.ascontiguousarray(cT.transpose(0, 2, 1))
    cw = wtile(inp["cond_w"])
    cb = pm(inp["cond_b"])
    in_maps = []
    for core in range(NCORES):
        sl = slice(core * 3072, (core + 1) * 3072)
        in_maps.append({
            "cT": cT, "cond_w": cw, "cond_b": cb,
            "ada_w": wtile(ada_flat[:, sl]),
            "ada_b": pm(adab_flat[sl]),
        })
    res = run_bass_kernel_spmd(nc, in_maps, core_ids=list(range(NCORES)))
    mod = np.zeros((2, 4 * 6144), np.float32)
    for core in range(NCORES):
        o = np.asarray(res.results[core]["modpm"])
        mod[:, core * 3072:(core + 1) * 3072] = o.transpose(2, 1, 0).reshape(2, 3072)
    return mod.reshape(2, 4, 6144)


NT = 16
TBT = 8
GH = 2
NG = 22 // GH


def ln_tile(P, y_ps, x_in, Gt, lng, lnb, x_out, xhat_bf, sc):
    tmp, stats, mv, rstd, xhat = sc["tmp"], sc["stats"], sc["mv"], sc["rstd"], sc["xhat"]
    P.tt("dve", tmp[:], y_ps, Gt, ALU.mult)
    P.stt("dve", tmp[:], x_in, ALPHA, tmp[:], ALU.mult, ALU.add)
    for h in range(2):
        P.generic("dve", "bn_stats", (stats[:, h, :], tmp[:, h * 512:(h + 1) * 512]),
                  [tmp[:, h * 512:(h + 1) * 512]], [stats[:, h, :]])
    P.generic("dve", "bn_aggr", (mv[:], stats[:].rearrange("p a b -> p (a b)")), [stats[:]], [mv[:]])
    P.act(rstd[:], mv[:, 1:2], AF.Sqrt, bias=sc["eps"][:])
    P.generic("dve", "reciprocal", (rstd[:], rstd[:]), [rstd[:]], [rstd[:]])
    P.ts("dve", xhat[:], tmp[:], mv[:, 0:1], rstd[:], ALU.subtract, ALU.mult)
    if xhat_bf is not None:
        P.copy("act", xhat_bf, xhat[:])
    P.tt("pool", x_out, xhat[:], lng, ALU.mult)
    P.tt("pool", x_out, x_out, lnb, ALU.add)


def build_F(W):
    KO = W // 128
    nc = bass.Bass("TRN2", target_bir_lowering=False)
    P = Prog(nc)
    NTT = NT + 1
    x_d = P.dram("xin", [NTT * 128, 1024], F32, "ExternalInput")
    oT_d = P.dram("oT", [128, KO, NTT * 128], BF16, "ExternalInput")
    wo_d = P.dram("w_out", [128, KO * 1024], F32, "ExternalInput")
    wu_d = P.dram("w_up", [NG, 128, 8 * 2 * GH * 128], F32, "ExternalInput")
    wd_d = P.dram("w_dn", [NG, 128, GH * 1024], F32, "ExternalInput")
    cw_d = P.dram("conv_w", [128, 22, 3], F32, "ExternalInput")
    cb_d = P.dram("conv_b", [128, 22], F32, "ExternalInput")
    rows_d = P.dram("rows", [128, 6, 1024], F32, "ExternalInput")
    pmv_d = P.dram("pmv", [128, 4, 8], F32, "ExternalInput")
    flag_d = P.dram("flag", [128, 1], F32, "ExternalInput")
    id_d = P.dram("ident", [128, 128], BF16, "ExternalInput")
    out_d = P.dram("xout", [NT * 128, 1024], F32, "ExternalOutput")

    big = P.sb("big", [128, 22 * 1024], BF16)
    wup = [P.sb("wup%d" % i, [128, 8, 2, GH * 128], BF16) for i in range(3)]
    wdn = [P.sb("wdn%d" % i, [128, GH, 1024], BF16) for i in range(3)]
    x1 = P.sb("x1", [128, TBT, 1024])
    xin = [P.sb("xin%d" % i, [128, 1024]) for i in range(2)]
    oT = [P.sb("oT%d" % i, [128, KO, 128], BF16) for i in range(3)]
    hT = P.sb("hT", [128, 8, TBT * 128], BF16)
    hTs = P.sb("hTs", [128, 8, 128], BF16)
    rows = P.sb("rows_sb", [128, 6, 1024])
    pmv = P.sb("pmv_sb", [128, 4, 8])
    A2 = P.sb("A2", [128, 8])
    B2 = P.sb("B2", [128, 8])
    cw = P.sb("cw_sb", [128, 22, 3])
    cb = P.sb("cb_sb", [128, 22])
    flag = P.sb("flag_sb", [128, 1])
    ident = P.sb("ident_sb", [128, 128], BF16)
    sc = dict(tmp=P.sb("tmp", [128, 1024]), stats=P.sb("stats", [128, 2, 6]), mv=P.sb("mv", [128, 2]),
              rstd=P.sb("rstd", [128, 1]), xhat=P.sb("xhat", [128, 1024]), eps=P.sb("eps", [128, 1]))
    P.memset("dve", sc["eps"][:], LN_EPS)
    xhb = [P.sb("xhb%d" % i, [128, 1024], BF16) for i in range(4)]
    xo = [P.sb("xo%d" % i, [128, 1024]) for i in range(2)]
    gsb = [P.sb("gsb%d" % i, [128, 2 + 512]) for i in range(2)]
    gc = [P.sb("gc%d" % i, [128, 512]) for i in range(2)]
    gs_rot = [0]
    ghalo = P.sb("ghalo", [128, 22, 2])
    PA = P.ps("PA", [128, 6, 512])
    PT = P.ps("PT", [128, 2, 1024], BF16)

    G1, G2 = rows[:, 0, :], rows[:, 1, :]
    lng0, lnb0, lng1, lnb1 = rows[:, 2, :], rows[:, 3, :], rows[:, 4, :], rows[:, 5, :]

    P.dma("sp", rows[:], rows_d)
    P.dma("sp", pmv[:], pmv_d)
    P.dma("sp", cw[:], cw_d)
    P.dma("sp", cb[:], cb_d)
    P.dma("sp", flag[:], flag_d)
    P.dma("sp", ident[:], id_d)
    P.ts("dve", rows[:, 0:2, :], rows[:, 0:2, :], 1.0, None, ALU.add)
    P.ts("dve", pmv[:, 0, :], pmv[:, 0, :], 1.0, None, ALU.add)
    P.tt("dve", A2[:], pmv[:, 2, :], pmv[:, 0, :], ALU.mult)
    P.tt("dve", B2[:], pmv[:, 3, :], pmv[:, 0, :], ALU.mult)
    P.tt("dve", B2[:], B2[:], pmv[:, 1, :], ALU.add)

    ps_rot = [0]

    def next_y():
        i = ps_rot[0] % 3
        ps_rot[0] += 1
        return PA[:, 2 * i:2 * i + 2, :].rearrange("p a b -> p (a b)"), (PA[:, 2 * i, :], PA[:, 2 * i + 1, :])

    wout_v = big[:, 0:KO * 1024].rearrange("p (k n) -> p k n", k=KO)
    uT = big[:].rearrange("p (j t) -> p j t", j=22)

    xin_rot = [0]
    oT_rot = [0]
    xhb_rot = [0]
    pt_rot = [0]

    def phase_A(tiles, dst_x1, dst_hT):
        n = len(tiles)
        xh_list = []
        for i, gt in enumerate(tiles):
            ob = oT[oT_rot[0] % 3]
            oT_rot[0] += 1
            P.dma("sp", ob[:], oT_d[:, :, gt * 128:(gt + 1) * 128])
            xb = xin[xin_rot[0] % 2]
            xin_rot[0] += 1
            P.dma("sp", xb[:], x_d[gt * 128:(gt + 1) * 128, :])
            yfull, (y0, y1) = next_y()
            oc = 0
            for kc in range(KO):
                P.mm(y0, ob[:, kc, oc:oc + 128], wout_v[:, kc, 0:512], start=(kc == 0), stop=(kc == KO - 1), inc=False)
                P.mm(y1, ob[:, kc, oc:oc + 128], wout_v[:, kc, 512:1024], start=(kc == 0), stop=(kc == KO - 1),
                     inc=(kc == KO - 1))
            xh = xhb[xhb_rot[0] % 4]
            xhb_rot[0] += 1
            d = dst_x1(i)
            if d is None:
                d = xo[0][:]
            ln_tile(P, yfull, xb[:], G1, lng0, lnb0, d, xh[:], sc)
            xh_list.append(xh)
            if len(xh_list) == 4 or i == n - 1:
                m = len(xh_list)
                base = (i - m + 1) * 128
                for j in range(8):
                    pt = PT[:, pt_rot[0] % 2, :]
                    pt_rot[0] += 1
                    for q in range(m):
                        P.transpose(pt[:, q * 128:(q + 1) * 128], xh_list[q][:, j * 128:(j + 1) * 128], ident[:],
                                    inc=(q == m - 1))
                    P.act(dst_hT[:, j, base:base + m * 128], pt[:, 0:m * 128], AF.Identity,
                          bias=B2[:, j:j + 1], scale=A2[:, j:j + 1])
                xh_list = []

    wu_i = [0]
    wd_i = [0]

    def load_wup(g):
        b = wup[wu_i[0] % 3]
        wu_i[0] += 1
        P.dma("pool", b[:].rearrange("p a b c -> p (a b c)"), wu_d[g])
        return b

    def load_wdn(g):
        b = wdn[wd_i[0] % 3]
        wd_i[0] += 1
        P.dma("pool", b[:].rearrange("p a b -> p (a b)"), wd_d[g])
        return b

    for tb in range(NT // TBT):
        for k4 in range(0, KO, 4):
            P.dma("pool", big[:, k4 * 1024:(k4 + 4) * 1024], wo_d[:, k4 * 1024:(k4 + 4) * 1024])
        if tb == 0:
            phase_A([0], lambda i: None, hTs)
        phase_A([1 + tb * TBT + i for i in range(TBT)], lambda i: x1[:, i, :], hT)

        wq = [load_wup(0), load_wup(1)]
        pb = 0
        for g in range(NG):
            if g + 2 < NG:
                wq.append(load_wup(g + 2))
            wb = wq[g]
            for jj in range(GH):
                j = g * GH + jj
                gs_prev = None
                for half in range(2):
                    gs = gsb[gs_rot[0] % 2]
                    g_c = gc[gs_rot[0] % 2]
                    gs_rot[0] += 1
                    if half == 1:
                        P.copy("pool", gs[:, 0:2], gs_prev[:, 512:514])
                    elif tb == 0:
                        hp = PA[:, (pb % 3) * 2, 0:2]
                        for kc in range(8):
                            P.mm(hp, wb[:, kc, 0, jj * 128:(jj + 1) * 128], hTs[:, kc, 126:128], start=(kc == 0), stop=(kc == 7))
                        P.ts("dve", gs[:, 0:2], hp, flag[:, 0:1], None, ALU.mult)
                        pb += 1
                    else:
                        P.copy("pool", gs[:, 0:2], ghalo[:, j, :])
                    i3 = pb % 3
                    pb += 1
                    gp, up = PA[:, 2 * i3, :], PA[:, 2 * i3 + 1, :]
                    tok = slice(half * 512, (half + 1) * 512)
                    for kc in range(8):
                        P.mm(gp, wb[:, kc, 0, jj * 128:(jj + 1) * 128], hT[:, kc, tok], start=(kc == 0), stop=(kc == 7))
                    for kc in range(8):
                        P.mm(up, wb[:, kc, 1, jj * 128:(jj + 1) * 128], hT[:, kc, tok], start=(kc == 0), stop=(kc == 7))
                    P.copy("act", gs[:, 2:514], gp)
                    P.ts("dve", g_c[:], gs[:, 0:512], cw[:, j, 0:1], cb[:, j:j + 1], ALU.mult, ALU.add)
                    P.stt("dve", g_c[:], gs[:, 1:513], cw[:, j, 1:2], g_c[:], ALU.mult, ALU.add)
                    P.stt("dve", g_c[:], gs[:, 2:514], cw[:, j, 2:3], g_c[:], ALU.mult, ALU.add)
                    P.act(g_c[:], g_c[:], AF.Silu)
                    P.tt("dve", uT[:, j, tok], g_c[:], up, ALU.mult)
                    gs_prev = gs
                if tb + 1 < NT // TBT:
                    P.copy("pool", ghalo[:, j, :], gs_prev[:, 512:514])

        dq = [load_wdn(0), load_wdn(1)]
        t0 = 0
        while t0 < TBT:
            tl = list(range(t0, min(t0 + 3, TBT)))
            ys = [next_y() for _ in tl]
            for g in range(NG):
                if g + 2 < NG:
                    dq.append(load_wdn(g + 2))
                elif t0 + 3 < TBT:
                    dq.append(load_wdn(g + 2 - NG))
                db = dq.pop(0)
                for jj in range(GH):
                    j = g * GH + jj
                    for ti, t in enumerate(tl):
                        y0, y1 = ys[ti][1]
                        last = (j == 21)
                        P.mm(y0, uT[:, j, t * 128:(t + 1) * 128], db[:, jj, 0:512], start=(j == 0), stop=last, inc=False)
                        P.mm(y1, uT[:, j, t * 128:(t + 1) * 128], db[:, jj, 512:1024], start=(j == 0), stop=last,
                             inc=(last or (jj == GH - 1 and ti == len(tl) - 1)))
            for ti, t in enumerate(tl):
                xb = xo[(tb * TBT + t) % 2]
                ln_tile(P, ys[ti][0], x1[:, t, :], G2, lng1, lnb1, xb[:], None, sc)
                gt = tb * TBT + t
                P.dma("sp", out_d[gt * 128:(gt + 1) * 128, :], xb[:], is_output=True)
            t0 += 3
    P.emit()
    return nc


_CACHE = {}


def bf16(a):
    return np.asarray(a).astype(ml_dtypes.bfloat16)


def run_F(x_full, oT_cores, layer, mod, inp, w_out):
    W = w_out.shape[0]
    KO = W // 128
    key = ("F", W)
    if key not in _CACHE:
        _CACHE[key] = build_F(W)
    nc = _CACHE[key]
    wup = np.asarray(inp["ffn_up"][layer])
    wt = wtile(wup)
    wu_l = np.empty((NG, 128, 8, 2, GH * 128), np.float32)
    for g in range(NG):
        wu_l[g, :, :, 0, :] = wt[:, :, g * GH * 128:(g + 1) * GH * 128]
        wu_l[g, :, :, 1, :] = wt[:, :, FH + g * GH * 128:FH + (g + 1) * GH * 128]
    wu_l = wu_l.reshape(NG, 128, 8 * 2 * GH * 128)
    wd = wtile(inp["ffn_down"][layer])
    wd_l = np.ascontiguousarray(wd.reshape(128, NG, GH * 1024).transpose(1, 0, 2))
    wo_l = wtile(w_out).reshape(128, KO * 1024)
    cw = np.ascontiguousarray(pm(inp["ffn_conv_w"][layer]).transpose(0, 2, 1))
    cb = pm(inp["ffn_conv_b"][layer])
    ident = np.eye(128, dtype=np.float32).astype(ml_dtypes.bfloat16)
    lng = np.asarray(inp["ln_g"][layer])
    lnb = np.asarray(inp["ln_b"][layer])
    in_maps = []
    for core in range(NCORES):
        b, q = divmod(core, 4)
        m = mod[b, layer]
        g1, sh2, sc2, g2 = m[2048:3072], m[3072:4096], m[4096:5120], m[5120:6144]
        rows = np.stack([g1, g2, lng[0], lnb[0], lng[1], lnb[1]], 0)
        rows = np.ascontiguousarray(np.broadcast_to(rows[None], (128, 6, 1024))).astype(np.float32)
        pmv = np.ascontiguousarray(np.stack([pm(sc2), pm(sh2), pm(lng[0]), pm(lnb[0])], 1)).astype(np.float32)
        t0 = q * 2048
        xin = np.zeros((17 * 128, 1024), np.float32)
        if q > 0:
            xin[:] = x_full[b, t0 - 128:t0 + 2048]
        else:
            xin[128:] = x_full[b, 0:2048]
        in_maps.append({
            "xin": xin, "oT": oT_cores[core], "w_out": wo_l, "w_up": wu_l, "w_dn": wd_l,
            "conv_w": cw, "conv_b": cb, "rows": rows, "pmv": pmv,
            "flag": np.full((128, 1), 0.0 if q == 0 else 1.0, np.float32), "ident": ident,
        })
    res = run_bass_kernel_spmd(nc, in_maps, core_ids=list(range(NCORES)))
    out = np.empty((2, 8192, 1024), np.float32)
    for core in range(NCORES):
        b, q = divmod(core, 4)
        out[b, q * 2048:(q + 1) * 2048] = np.asarray(res.results[core]["xout"])
    return out


def oT_from_tokenmajor(o_full):
    W = o_full.shape[-1]
    KO = W // 128
    outs = []
    for core in range(NCORES):
        b, q = divmod(core, 4)
        t0 = q * 2048
        seg = np.zeros((17 * 128, W), ml_dtypes.bfloat16)
        if q > 0:
            seg[:] = o_full[b, t0 - 128:t0 + 2048]
        else:
            seg[128:] = o_full[b, 0:2048]
        outs.append(np.ascontiguousarray(seg.T.reshape(KO, 128, 17 * 128).transpose(1, 0, 2)))
    return outs


def build_Mgmlp():
    nc = bass.Bass("TRN2", target_bir_lowering=False)
    P = Prog(nc)
    TOK = 2048
    xT_d = P.dram("xT", [128, 8, TOK], F32, "ExternalInput")
    pm1_d = P.dram("pm1", [128, 2, 8], F32, "ExternalInput")
    win_d = P.dram("w_in", [128, 8, 4096], F32, "ExternalInput")
    rows_d = P.dram("rows", [128, 2, 2048], F32, "ExternalInput")
    wsT_d = P.dram("wsT", [128, 8, 128], F32, "ExternalInput")
    mask_d = P.dram("mask", [128, 128], F32, "ExternalInput")
    bs_d = P.dram("bs", [1, 1024], F32, "ExternalInput")
    out_d = P.dram("oT", [128, 16, TOK], BF16, "ExternalOutput")

    wu = P.sb("wu", [128, 8, 2048], BF16)
    wv = P.sb("wv", [128, 8, 2048], BF16)
    rows = P.sb("rows_sb", [128, 2, 2048])
    pm1 = P.sb("pm1_sb", [128, 2, 8])
    wsT = P.sb("wsT_sb", [128, 8, 128])
    mask = P.sb("mask_sb", [128, 128])
    wsTm = P.sb("wsTm", [128, 8, 128], BF16)
    bs = P.sb("bs_sb", [1, 1024])
    ones1 = P.sb("ones1", [1, 128])
    eps = P.sb("eps", [128, 1])
    xs = [P.sb("xs%d" % i, [128, 512]) for i in range(3)]
    hT = P.sb("hT", [128, 8, 512], BF16)
    uT = P.sb("uT", [128, 16, 512])
    vsb = [P.sb("vsb%d" % i, [128, 2048]) for i in range(2)]
    vln = [P.sb("vln%d" % i, [128, 2048], BF16) for i in range(2)]
    uvb = [P.sb("uvb%d" % i, [128, 16, 512], BF16) for i in range(2)]
    stats = P.sb("stats", [128, 4, 6])
    mv = P.sb("mv", [128, 2])
    rstd = P.sb("rstd", [128, 1])
    pu = P.ps("pu", [128, 2, 512])
    pv = P.ps("pv", [128, 2, 512])
    psp = P.ps("psp", [128, 16, 128])

    P.dma("sp", pm1[:], pm1_d)
    P.dma("sp", wsT[:], wsT_d)
    P.dma("sp", mask[:], mask_d)
    P.dma("sp", bs[:], bs_d)
    P.dma("sp", rows[:], rows_d)
    P.memset("dve", ones1[:], 1.0)
    P.memset("dve", eps[:], LN_EPS)
    P.ts("dve", pm1[:, 0, :], pm1[:, 0, :], 1.0, None, ALU.add)
    for g in range(8):
        P.tt("dve", wsTm[:, g, :], wsT[:, g, :], mask[:], ALU.mult)
    for kc in range(8):
        P.dma("pool", wu[:, kc, :], win_d[:, kc, 0:2048])
    for kc in range(8):
        P.dma("pool", wv[:, kc, :], win_d[:, kc, 2048:4096])

    xi = 0
    vi = 0
    for blk in range(TOK // 512):
        t0 = blk * 512
        for kc in range(8):
            xb = xs[xi % 3]
            xi += 1
            P.dma("sp", xb[:], xT_d[:, kc, t0:t0 + 512])
            P.act(hT[:, kc, :], xb[:], AF.Identity, bias=pm1[:, 1, kc:kc + 1], scale=pm1[:, 0, kc:kc + 1])
        for uc in range(16):
            pt = pu[:, uc % 2, :]
            for kc in range(8):
                P.mm(pt, wu[:, kc, uc * 128:(uc + 1) * 128], hT[:, kc, :], start=(kc == 0), stop=(kc == 7))
            P.act(uT[:, uc, :], pt, AF.Gelu)
        ob = uvb[blk % 2]
        for ch in range(4):
            tok = slice(ch * 128, (ch + 1) * 128)
            vb = vsb[vi % 2]
            vl = vln[vi % 2]
            vi += 1
            for half in range(2):
                for q in range(2):
                    c0 = (half * 2 + q) * 512
                    for kc in range(8):
                        P.mm(pv[:, q, :], hT[:, kc, tok], wv[:, kc, c0:c0 + 512], start=(kc == 0), stop=(kc == 7))
                P.act(vb[:, half * 1024:(half + 1) * 1024], pv[:].rearrange("p a b -> p (a b)"), AF.Gelu)
            for q in range(4):
                P.generic("dve", "bn_stats", (stats[:, q, :], vb[:, q * 512:(q + 1) * 512]),
                          [vb[:, q * 512:(q + 1) * 512]], [stats[:, q, :]])
            P.generic("dve", "bn_aggr", (mv[:], stats[:].rearrange("p a b -> p (a b)")), [stats[:]], [mv[:]])
            P.act(rstd[:], mv[:, 1:2], AF.Sqrt, bias=eps[:])
            P.generic("dve", "reciprocal", (rstd[:], rstd[:]), [rstd[:]], [rstd[:]])
            P.ts("dve", vb[:], vb[:], mv[:, 0:1], rstd[:], ALU.subtract, ALU.mult)
            P.tt("pool", vb[:], vb[:], rows[:, 0, :], ALU.mult)
            P.tt("pool", vl[:], vb[:], rows[:, 1, :], ALU.add)
            for cc in range(16):
                g = cc // 2
                P.mm(psp[:, cc, :], vl[:, cc * 128:(cc + 1) * 128], wsTm[:, g, :], start=True, stop=False, inc=False)
                P.mm(psp[:, cc, :], ones1[0:1, :], bs[0:1, g * 128:(g + 1) * 128], start=False, stop=True,
                     inc=(cc % 4 == 3))
            P.tt("dve", ob[:, :, tok], psp[:], uT[:, :, tok], ALU.mult)
        P.dma("sp", out_d[:, :, t0:t0 + 512], ob[:], is_output=True)
    P.emit()
    return nc


def xT_layout(xseg):
    T = xseg.shape[0]
    return np.ascontiguousarray(np.asarray(xseg).T.reshape(8, 128, T).transpose(1, 0, 2))


def run_Mgmlp(x_full, mod, inp):
    if "Mgmlp" not in _CACHE:
        _CACHE["Mgmlp"] = build_Mgmlp()
    nc = _CACHE["Mgmlp"]
    layer = 2
    win = wtile(inp["gmlp_w_in"])
    rows = np.stack([inp["gmlp_ln_g"], inp["gmlp_ln_b"]], 0)
    rows = np.ascontiguousarray(np.broadcast_to(rows[None], (128, 2, 2048))).astype(np.float32)
    wsT = np.ascontiguousarray(np.asarray(inp["gmlp_w_s"]).transpose(2, 0, 1))
    idx = np.arange(128)
    mask = (idx[:, None] <= idx[None, :]).astype(np.float32)
    bs = np.asarray(inp["gmlp_b_s"]).reshape(1, 1024).astype(np.float32)
    in_maps = []
    for core in range(NCORES):
        b, q = divmod(core, 4)
        m = mod[b, layer]
        sh1, sc1 = m[0:1024], m[1024:2048]
        pm1 = np.ascontiguousarray(np.stack([pm(sc1), pm(sh1)], 1)).astype(np.float32)
        in_maps.append({"xT": xT_layout(x_full[b, q * 2048:(q + 1) * 2048]), "pm1": pm1, "w_in": win,
                        "rows": rows, "wsT": wsT, "mask": mask, "bs": bs})
    res = run_bass_kernel_spmd(nc, in_maps, core_ids=list(range(NCORES)))
    oTs = [np.asarray(res.results[c]["oT"]) for c in range(NCORES)]
    outs = []
    for core in range(NCORES):
        b, q = divmod(core, 4)
        sh = np.zeros((128, 16, 128), ml_dtypes.bfloat16) if q == 0 else oTs[core - 1][:, :, -128:]
        outs.append(np.ascontiguousarray(np.concatenate([sh, oTs[core]], axis=2)))
    return outs


def build_Mret():
    nc = bass.Bass("TRN2", target_bir_lowering=False)
    P = Prog(nc)
    xT_d = P.dram("xT", [128, 8, S], F32, "ExternalInput")
    pm1_d = P.dram("pm1", [128, 2, 8], F32, "ExternalInput")
    wq_d = P.dram("wq", [128, 8, 256], F32, "ExternalInput")
    wk_d = P.dram("wk", [128, 8, 256], F32, "ExternalInput")
    wv_d = P.dram("wv", [128, 8, 512], F32, "ExternalInput")
    wg_d = P.dram("wg", [128, 8, 512], F32, "ExternalInput")
    cos_d = P.dram("cosT", [128, S], F32, "ExternalInput")
    sin_d = P.dram("sinT", [128, S], F32, "ExternalInput")
    xi_d = P.dram("xi_row", [128, 512], F32, "ExternalInput")
    dm_d = P.dram("dmaskT", [128, 128], F32, "ExternalInput")
    col_d = P.dram("cols", [128, 2], F32, "ExternalInput")
    id_d = P.dram("ident", [128, 128], BF16, "ExternalInput")
    out_d = P.dram("o", [S, 512], BF16, "ExternalOutput")

    wq = P.sb("wq_sb", [128, 8, 256], BF16)
    wk = P.sb("wk_sb", [128, 8, 256], BF16)
    wv = P.sb("wv_sb", [128, 8, 512], BF16)
    wg = P.sb("wg_sb", [128, 8, 512], BF16)
    pm1 = P.sb("pm1_sb", [128, 2, 8])
    xi_row = P.sb("xi_sb", [128, 512])
    dmT = P.sb("dm_sb", [128, 128])
    cols = P.sb("cols_sb", [128, 2])
    ident = P.sb("ident_sb", [128, 128], BF16)
    eps = P.sb("eps", [128, 1])
    xs = [P.sb("xs%d" % i, [128, 512]) for i in range(3)]
    hT = P.sb("hT", [128, 8, 512], BF16)
    cs = [P.sb("cs%d" % i, [128, 2, 512]) for i in range(2)]
    tmp = [P.sb("tmp%d" % i, [128, 512]) for i in range(4)]
    qT = P.sb("qT", [128, 2, 512], BF16)
    qxT = P.sb("qxT", [128, 2, 512], BF16)
    kT = P.sb("kT", [128, 2, 512], BF16)
    vsb = [P.sb("vsb%d" % i, [128, 512], BF16) for i in range(2)]
    sg = [P.sb("sg%d" % i, [128, 512]) for i in range(2)]
    kz = [P.sb("kz%d" % i, [128, 256], BF16) for i in range(2)]
    sT = [P.sb("sT%d" % i, [128, 128], BF16) for i in range(2)]
    on = [P.sb("on%d" % i, [128, 512]) for i in range(2)]
    ob = [P.sb("ob%d" % i, [128, 512], BF16) for i in range(2)]
    state = P.sb("state", [128, 2, 512])
    state_bf = P.sb("state_bf", [128, 2, 512], BF16)
    stats = P.sb("stats", [128, 6])
    mv = P.sb("mv", [128, 2])
    rstd = P.sb("rstd", [128, 1])
    pq = P.ps("pq", [128, 2, 512])
    pvg = P.ps("pvg", [128, 512])
    pS = P.ps("pS", [128, 512])
    pkt = P.ps("pkt", [128, 1024], BF16)
    po = P.ps("po", [128, 512])
    pst = P.ps("pst", [128, 2, 512])

    for dst, src in ((pm1, pm1_d), (xi_row, xi_d), (dmT, dm_d), (cols, col_d), (ident, id_d)):
        P.dma("sp", dst[:], src)
    P.memset("dve", eps[:], 1e-6)
    P.memset("dve", state[:], 0.0)
    P.memset("pool", state_bf[:], 0.0)
    P.ts("dve", pm1[:, 0, :], pm1[:, 0, :], 1.0, None, ALU.add)
    for dst, src in ((wq, wq_d), (wk, wk_d), (wv, wv_d), (wg, wg_d)):
        P.dma("pool", dst[:], src)

    xi_ = 0
    ci = 0
    for blk in range(S // 512):
        t0 = blk * 512
        cb = cs[blk % 2]
        P.dma("sp", cb[:, 0, :], cos_d[:, t0:t0 + 512])
        P.dma("sp", cb[:, 1, :], sin_d[:, t0:t0 + 512])
        for kc in range(8):
            xb = xs[xi_ % 3]
            xi_ += 1
            P.dma("sp", xb[:], xT_d[:, kc, t0:t0 + 512])
            P.act(hT[:, kc, :], xb[:], AF.Identity, bias=pm1[:, 1, kc:kc + 1], scale=pm1[:, 0, kc:kc + 1])
        for which, w_sb in (("q", wq), ("k", wk)):
            for dkc in range(2):
                for kc in range(8):
                    P.mm(pq[:, dkc, :], w_sb[:, kc, dkc * 128:(dkc + 1) * 128], hT[:, kc, :], start=(kc == 0), stop=(kc == 7))
            P.tt("dve", tmp[0][:], pq[:, 0, :], cb[:, 0, :], ALU.mult)
            P.tt("dve", tmp[1][:], pq[:, 1, :], cb[:, 1, :], ALU.mult)
            P.tt("dve", tmp[2][:], pq[:, 0, :], cb[:, 1, :], ALU.mult)
            P.tt("dve", tmp[3][:], pq[:, 1, :], cb[:, 0, :], ALU.mult)
            P.tt("pool", tmp[0][:], tmp[0][:], tmp[1][:], ALU.subtract)
            P.tt("pool", tmp[2][:], tmp[2][:], tmp[3][:], ALU.add)
            dst = qT if which == "q" else kT
            P.copy("act", dst[:, 0, :], tmp[0][:])
            P.copy("act", dst[:, 1, :], tmp[2][:])
            if which == "q":
                P.tt("pool", qxT[:, 0, :], tmp[0][:], xi_row[:], ALU.mult)
                P.tt("pool", qxT[:, 1, :], tmp[2][:], xi_row[:], ALU.mult)
        for ch in range(4):
            tok = slice(ch * 128, (ch + 1) * 128)
            vb, sgb, kzb, sTb, onb, obb = vsb[ci % 2], sg[ci % 2], kz[ci % 2], sT[ci % 2], on[ci % 2], ob[ci % 2]
            ci += 1
            for kc in range(8):
                P.mm(pvg[:], hT[:, kc, tok], wv[:, kc, :], start=(kc == 0), stop=(kc == 7))
            P.copy("act", vb[:], pvg[:])
            for kc in range(8):
                P.mm(pvg[:], hT[:, kc, tok], wg[:, kc, :], start=(kc == 0), stop=(kc == 7))
            P.act(sgb[:], pvg[:], AF.Silu)
            for dkc in range(2):
                P.transpose(pkt[:, dkc * 128:(dkc + 1) * 128], kT[:, dkc, tok], ident[:], inc=(dkc == 1))
            P.ts("dve", kzb[:], pkt[:, 0:256], cols[:, 0:1], None, ALU.mult)
            for dkc in range(2):
                P.mm(pS[:, 0:128], kT[:, dkc, tok], qT[:, dkc, tok], start=(dkc == 0), stop=(dkc == 1))
            P.tt("dve", sTb[:], pS[:, 0:128], dmT[:], ALU.mult)
            P.mm(po[:], sTb[:], vb[:], start=True, stop=False, inc=False)
            P.mm(po[:], qxT[:, 0, tok], state_bf[:, 0, :], start=False, stop=False, inc=False)
            P.mm(po[:], qxT[:, 1, tok], state_bf[:, 1, :], start=False, stop=True)
            for dkc in range(2):
                P.mm(pst[:, dkc, :], kzb[:, dkc * 128:(dkc + 1) * 128], vb[:], start=True, stop=True)
            P.stt("dve", state[:].rearrange("p a b -> p (a b)"), state[:].rearrange("p a b -> p (a b)"), cols[:, 1:2],
                  pst[:].rearrange("p a b -> p (a b)"), ALU.mult, ALU.add)
            P.copy("act", state_bf[:].rearrange("p a b -> p (a b)"), state[:].rearrange("p a b -> p (a b)"))
            P.generic("dve", "bn_stats", (stats[:], po[:]), [po[:]], [stats[:]])
            P.generic("dve", "bn_aggr", (mv[:], stats[:]), [stats[:]], [mv[:]])
            P.act(rstd[:], mv[:, 1:2], AF.Sqrt, bias=eps[:])
            P.generic("dve", "reciprocal", (rstd[:], rstd[:]), [rstd[:]], [rstd[:]])
            P.ts("dve", onb[:], po[:], mv[:, 0:1], rstd[:], ALU.subtract, ALU.mult)
            P.tt("pool", obb[:], onb[:], sgb[:], ALU.mult)
            P.dma("sp", out_d[t0 + ch * 128:t0 + (ch + 1) * 128, :], obb[:], is_output=True)
    P.emit()
    return nc


def run_Mret(x_full, mod, inp):
    if "Mret" not in _CACHE:
        _CACHE["Mret"] = build_Mret()
    nc = _CACHE["Mret"]
    layer = 1
    H, dk, dv, C = 4, 256, 512, 128
    w = np.asarray(inp["ret_w_in"])
    pos = np.arange(S, dtype=np.float32)
    inv_freq = (np.float32(10000.0) ** (-np.linspace(0.0, 1.0, dk // 2, dtype=np.float32))).astype(np.float32)
    ang = (pos[:, None] * inv_freq[None, :]).astype(np.float32)
    cosT = np.ascontiguousarray(np.cos(ang).T.astype(np.float32))
    sinT = np.ascontiguousarray(np.sin(ang).T.astype(np.float32))
    ident = np.eye(128, dtype=np.float32).astype(ml_dtypes.bfloat16)
    idx = np.arange(C, dtype=np.float32)
    in_maps = []
    xTs = [xT_layout(x_full[b]) for b in range(2)]
    for core in range(NCORES):
        b, h = divmod(core, 4)
        m = mod[b, layer]
        sh1, sc1 = m[0:1024], m[1024:2048]
        pm1 = np.ascontiguousarray(np.stack([pm(sc1), pm(sh1)], 1)).astype(np.float32)
        lg = np.log(np.float32(1.0) - np.power(np.float32(2.0), np.float32(-5.0 - h))).astype(np.float32)
        rel = idx[None, :] - idx[:, None]
        dmT = np.where(rel >= 0, np.exp(np.maximum(rel, 0.0) * lg), 0.0).astype(np.float32) * np.float32(dk ** -0.5)
        zeta = np.exp((C - 1.0 - idx) * lg).astype(np.float32) * np.float32(dk ** -0.5)
        xi = np.exp((idx + 1.0) * lg).astype(np.float32)
        gam = np.exp(np.float32(C) * lg).astype(np.float32)
        cols = np.stack([zeta, np.full(128, gam, np.float32)], 1).astype(np.float32)
        xi_row = np.ascontiguousarray(np.broadcast_to(np.tile(xi, 4)[None], (128, 512))).astype(np.float32)
        in_maps.append({
            "xT": xTs[b], "pm1": pm1,
            "wq": wtile(w[:, h * dk:(h + 1) * dk]), "wk": wtile(w[:, H * dk + h * dk:H * dk + (h + 1) * dk]),
            "wv": wtile(w[:, 2 * H * dk + h * dv:2 * H * dk + (h + 1) * dv]),
            "wg": wtile(w[:, 2 * H * dk + H * dv + h * dv:2 * H * dk + H * dv + (h + 1) * dv]),
            "cosT": cosT, "sinT": sinT, "xi_row": xi_row, "dmaskT": np.ascontiguousarray(dmT), "cols": cols, "ident": ident,
        })
    res = run_bass_kernel_spmd(nc, in_maps, core_ids=list(range(NCORES)))
    o_full = np.empty((2, S, H * dv), ml_dtypes.bfloat16)
    for core in range(NCORES):
        b, h = divmod(core, 4)
        o_full[b, :, h * dv:(h + 1) * dv] = np.asarray(res.results[core]["o"])
    return oT_from_tokenmajor(o_full)


def build_Msb():
    nc = bass.Bass("TRN2", target_bir_lowering=False)
    P = Prog(nc)
    NQB = S // 128
    xT_d = P.dram("xT", [128, 8, S], F32, "ExternalInput")
    xTr_d = P.dram("xTr", [128, 8, S], F32, "ExternalInput")
    pm1_d = P.dram("pm1", [128, 2, 8], F32, "ExternalInput")
    wq_d = P.dram("wq", [128, 8, 256], F32, "ExternalInput")
    wk_d = P.dram("wk", [128, 8, 256], F32, "ExternalInput")
    wv_d = P.dram("wv", [128, 8, 256], F32, "ExternalInput")
    mneg_d = P.dram("mneg", [128, 128], BF16, "ExternalInput")
    id_d = P.dram("ident", [128, 128], BF16, "ExternalInput")
    out_d = P.dram("o", [128, NQB, 256], BF16, "ExternalOutput")

    qT = P.sb("qT_all", [128, 2, S], BF16)
    kT = P.sb("kT_all", [128, 2, S], BF16)
    v_all = P.sb("v_all", [128, NQB, 256], BF16)
    o_all = P.sb("o_all", [128, NQB, 256], BF16)
    wq = P.sb("wq_sb", [128, 8, 256], BF16)
    wk = P.sb("wk_sb", [128, 8, 256], BF16)
    wv = P.sb("wv_sb", [128, 8, 256], BF16)
    pm1 = P.sb("pm1_sb", [128, 2, 8])
    mneg = P.sb("mneg_sb", [128, 128], BF16)
    ident = P.sb("ident_sb", [128, 128], BF16)
    zeros = P.sb("zeros", [128, 1024])
    xs = [P.sb("xs%d" % i, [128, 512]) for i in range(3)]
    hT = P.sb("hT", [128, 8, 512], BF16)
    hTr = P.sb("hTr", [128, 8, 512], BF16)
    gb = [P.sb("gb%d" % i, [128, 1024]) for i in range(2)]
    Pb = [P.sb("Pb%d" % i, [128, 1025]) for i in range(2)]
    Ab = [P.sb("Ab%d" % i, [128, 1024], BF16) for i in range(2)]
    ATb = [P.sb("ATb%d" % i, [128, 1024], BF16) for i in range(2)]
    pz = P.ps("pz", [128, 2, 1024])
    pT = P.ps("pT", [128, 2, 1024], BF16)
    po = P.ps("po", [128, 2, 64])
    pp = pz

    for dst, src in ((pm1, pm1_d), (mneg, mneg_d), (ident, id_d)):
        P.dma("sp", dst[:], src)
    P.memset("dve", zeros[:], 0.0)
    P.ts("dve", pm1[:, 0, :], pm1[:, 0, :], 1.0, None, ALU.add)
    for dst, src in ((wq, wq_d), (wk, wk_d), (wv, wv_d)):
        P.dma("pool", dst[:], src)

    xi_ = 0
    for blk in range(S // 512):
        t0 = blk * 512
        for (src_d, dst) in ((xT_d, hT), (xTr_d, hTr)):
            for kc in range(8):
                xb = xs[xi_ % 3]
                xi_ += 1
                P.dma("sp", xb[:], src_d[:, kc, t0:t0 + 512])
                P.act(dst[:, kc, :], xb[:], AF.Identity, bias=pm1[:, 1, kc:kc + 1], scale=pm1[:, 0, kc:kc + 1])
        for pair in range(2):
            pq_ = pp[:, 0, pair * 512:(pair + 1) * 512]
            for kc in range(8):
                P.mm(pq_, wq[:, kc, pair * 128:(pair + 1) * 128], hT[:, kc, :], start=(kc == 0), stop=(kc == 7))
            P.act(qT[:, pair, t0:t0 + 512], pq_, AF.Copy, scale=0.125)
            pk_ = pp[:, 1, pair * 512:(pair + 1) * 512]
            for kc in range(8):
                P.mm(pk_, wk[:, kc, pair * 128:(pair + 1) * 128], hTr[:, kc, :], start=(kc == 0), stop=(kc == 7))
            P.copy("dve", kT[:, pair, t0:t0 + 512], pk_)
        for ch in range(4):
            pv_ = pp[:, ch % 2, 0:256] if False else pz[:, ch % 2, 0:256]
            for kc in range(8):
                P.mm(pv_, hTr[:, kc, ch * 128:(ch + 1) * 128], wv[:, kc, :], start=(kc == 0), stop=(kc == 7))
            P.copy("act" if ch % 2 else "dve", v_all[:, blk * 4 + ch, :], pv_)

    si = 0
    for head in range(4):
        pair, par = divmod(head, 2)
        prt = slice(par * 64, par * 64 + 64)
        for qb in range(NQB):
            t0 = qb * 128
            nb = qb + 1
            i0 = (NQB - 1 - qb) * 128
            pob = po[:, qb % 2, :]
            nseg = (nb + 7) // 8
            prevP = None
            for sg_ in range(nseg):
                kb0 = sg_ * 8
                nk = min(8, nb - kb0)
                n = nk * 128
                b2 = si % 2
                si += 1
                zt = pz[:, b2, :]
                for c0 in range(0, n, 512):
                    m_ = min(512, n - c0)
                    first_diag = (sg_ == 0 and c0 == 0)
                    ks = i0 + kb0 * 128 + c0
                    P.mm(zt[:, c0:c0 + m_], qT[prt, pair, t0:t0 + 128], kT[prt, pair, ks:ks + m_],
                         start=True, stop=not first_diag)
                    if first_diag:
                        P.mm(zt[:, 0:128], ident[:], mneg[:], start=False, stop=True)
                g_, P_, A_, AT_ = gb[b2], Pb[b2], Ab[b2], ATb[b2]
                P.act(g_[:, 0:n], zt[:, 0:n], AF.Sigmoid, scale=-1.0)
                if prevP is None:
                    P.memset("pool", P_[:, 0:1], 1.0)
                    init = 1.0
                    P.generic("dve", "tensor_tensor_scan", (P_[:, 1:1 + n], g_[:, 0:n], zeros[:, 0:n], init, ALU.mult, ALU.add),
                              [g_[:, 0:n], zeros[:, 0:n]], [P_[:, 1:1 + n]])
                else:
                    pp_, pn = prevP
                    P.copy("pool", P_[:, 0:1], pp_[:, pn:pn + 1])
                    P.generic("dve", "tensor_tensor_scan", (P_[:, 1:1 + n], g_[:, 0:n], zeros[:, 0:n], pp_[:, pn:pn + 1], ALU.mult, ALU.add),
                              [g_[:, 0:n], zeros[:, 0:n], pp_[:, pn:pn + 1]], [P_[:, 1:1 + n]])
                prevP = (P_, n)
                P.tt("pool", A_[:, 0:n], P_[:, 0:n], P_[:, 1:1 + n], ALU.subtract)
                ptb = pT[:, b2, :]
                for j in range(nk):
                    P.transpose(ptb[:, j * 128:(j + 1) * 128], A_[:, j * 128:(j + 1) * 128], ident[:], inc=(j == nk - 1))
                P.copy("act" if b2 else "dve", AT_[:, 0:n], ptb[:, 0:n])
                for j in range(nk):
                    rb = (NQB - 1 - qb) + kb0 + j
                    first = (sg_ == 0 and j == 0)
                    last = (sg_ == nseg - 1 and j == nk - 1)
                    P.mm(pob, AT_[:, j * 128:(j + 1) * 128], v_all[:, rb, head * 64:(head + 1) * 64],
                         start=first, stop=last, inc=(last or j == nk - 1))
            P.copy("act", o_all[:, qb, head * 64:(head + 1) * 64], pob)
    for q4 in range(4):
        P.dma("sp", out_d[:, q4 * 16:(q4 + 1) * 16, :], o_all[:, q4 * 16:(q4 + 1) * 16, :], is_output=True)
    P.emit()
    return nc


def run_Msb(x_full, mod, inp):
    if "Msb" not in _CACHE:
        _CACHE["Msb"] = build_Msb()
    nc = _CACHE["Msb"]
    layer = 3
    w = np.asarray(inp["sb_w_in"])
    ident = np.eye(128, dtype=np.float32).astype(ml_dtypes.bfloat16)
    idx = np.arange(128)
    mneg = np.where(idx[:, None] + idx[None, :] <= 127, -30000.0, 0.0).astype(np.float32).astype(ml_dtypes.bfloat16)
    xTs = [xT_layout(x_full[b]) for b in range(2)]
    xTrs = [np.ascontiguousarray(t[:, :, ::-1]) for t in xTs]
    in_maps = []
    for core in range(NCORES):
        b, hg = divmod(core, 4)
        m = mod[b, layer]
        sh1, sc1 = m[0:1024], m[1024:2048]
        pm1 = np.ascontiguousarray(np.stack([pm(sc1), pm(sh1)], 1)).astype(np.float32)
        in_maps.append({
            "xT": xTs[b], "xTr": xTrs[b], "pm1": pm1,
            "wq": wtile(w[:, hg * 256:(hg + 1) * 256]),
            "wk": wtile(w[:, 1024 + hg * 256:1024 + (hg + 1) * 256]),
            "wv": wtile(w[:, 2048 + hg * 256:2048 + (hg + 1) * 256]),
            "mneg": mneg, "ident": ident,
        })
    res = run_bass_kernel_spmd(nc, in_maps, core_ids=list(range(NCORES)))
    o_full = np.empty((2, S, 1024), ml_dtypes.bfloat16)
    for core in range(NCORES):
        b, hg = divmod(core, 4)
        o = np.asarray(res.results[core]["o"])
        o_full[b, :, hg * 256:(hg + 1) * 256] = o.transpose(1, 0, 2).reshape(S, 256)
    return oT_from_tokenmajor(o_full)


def build_Mgdn(nblk=None):
    nc = bass.Bass("TRN2", target_bir_lowering=False)
    P = Prog(nc)
    NTL = S // 128
    xT_d = P.dram("xT", [128, 8, S], F32, "ExternalInput")
    pm1_d = P.dram("pm1", [128, 2, 8], F32, "ExternalInput")
    w_d = P.dram("w", [128, 8, 1024], F32, "ExternalInput")
    wab_d = P.dram("wab", [128, 8, 4], F32, "ExternalInput")
    cw_d = P.dram("cw", [128, 6, 4], F32, "ExternalInput")
    cst_d = P.dram("cst", [128, 8, 128], F32, "ExternalInput")
    hs_d = P.dram("hs", [128, 4], F32, "ExternalInput")
    nw_d = P.dram("nw", [128, 128], F32, "ExternalInput")
    idb_d = P.dram("identb", [128, 128], BF16, "ExternalInput")
    out_d = P.dram("o", [128, NTL, 256], BF16, "ExternalOutput")

    w = P.sb("w_sb", [128, 8, 1024], BF16)
    wab = P.sb("wab_sb", [128, 8, 4])
    pm1 = P.sb("pm1_sb", [128, 2, 8])
    cw = P.sb("cw_sb", [128, 6, 4])
    cst = P.sb("cst_sb", [128, 8, 128])
    hs = P.sb("hs_sb", [128, 4])
    nw = P.sb("nw_sb", [128, 128])
    identb = P.sb("identb_sb", [128, 128], BF16)
    onesb = P.sb("onesb", [128, 128], BF16)
    negA = P.sb("negA", [128, 2])
    eps6 = P.sb("eps6", [128, 1])
    eps6q = P.sb("eps6q", [128, 1])
    one1 = P.sb("one1", [128, 1])
    ident_f, onesBD, triBD, blk0, blk1, posmask, posmaskT, negstrict = [cst[:, i, :] for i in range(8)]

    xs = [P.sb("xs%d" % i, [128, 512]) for i in range(3)]
    hT = P.sb("hT", [128, 8, 512], BF16)
    hTf = P.sb("hTf", [128, 8, 512])
    xc = [P.sb("xc%d" % i, [128, 515]) for i in range(6)]
    cv = [P.sb("cv%d" % i, [128, 512]) for i in range(2)]
    sq = [P.sb("sq%d" % i, [128, 512], BF16) for i in range(2)]
    rn = [P.sb("rn%d" % i, [128, 512]) for i in range(2)]
    qhT = [P.sb("qhT%d" % i, [128, 512], BF16) for i in range(2)]
    khT = [P.sb("khT%d" % i, [128, 512], BF16) for i in range(2)]
    vcT = [P.sb("vcT%d" % i, [128, 512], BF16) for i in range(2)]
    ktok = [P.sb("ktok%d" % i, [128, 4, 128]) for i in range(2)]
    vtok = [P.sb("vtok%d" % i, [128, 4, 128]) for i in range(2)]
    nz = [P.sb("nz%d" % i, [128, 4, 128]) for i in range(2)]
    gat = {}
    for nm in ("g", "beta", "gc", "glo", "glb0", "glb1", "egc", "edl", "egl0", "egl1", "bg", "e1"):
        gat[nm] = [P.sb("%s%d" % (nm, i), [128, 4]) for i in range(2)]
    Dg = [P.sb("Dg%d" % i, [128, 128]) for i in range(4)]
    dec = [P.sb("dec%d" % i, [128, 128]) for i in range(4)]
    decT = [P.sb("decT%d" % i, [128, 128]) for i in range(4)]
    tmpE = [P.sb("tmpE%d" % i, [128, 128]) for i in range(4)]
    Yb = [[P.sb("Y%d_%d" % (i, k), [128, 128]) for k in range(2)] for i in range(4)]
    Zb = [[P.sb("Z%d_%d" % (i, k), [128, 128]) for k in range(2)] for i in range(4)]
    Ttb = [[P.sb("Tt%d_%d" % (i, k), [128, 128]) for k in range(2)] for i in range(4)]
    Tmb = [[P.sb("Tm%d_%d" % (i, k), [128, 128]) for k in range(2)] for i in range(4)]
    Ttbf = [P.sb("Ttbf%d" % i, [128, 128], BF16) for i in range(4)]
    vbt = [P.sb("vbt%d" % i, [128, 128], BF16) for i in range(4)]
    kbe = [P.sb("kbe%d" % i, [128, 128], BF16) for i in range(4)]
    kd = [P.sb("kd%d" % i, [128, 128], BF16) for i in range(4)]
    u_sb = [P.sb("u%d" % i, [128, 128]) for i in range(4)]
    wTA = [P.sb("wTA%d" % i, [128, 128], BF16) for i in range(4)]
    wTB = [P.sb("wTB%d" % i, [128, 128], BF16) for i in range(4)]
    qkT = [P.sb("qkT%d" % i, [128, 128], BF16) for i in range(4)]
    vn = [P.sb("vn%d" % i, [128, 128], BF16) for i in range(2)]
    o1 = [P.sb("o1_%d" % i, [128, 128]) for i in range(2)]
    osum = [P.sb("osum%d" % i, [128, 128]) for i in range(2)]
    osq = P.sb("osq", [128, 128])
    ssq = P.sb("ssq", [128, 1])
    rinv = P.sb("rinv", [128, 1])
    St = [P.sb("S%d" % i, [128, 128]) for i in range(2)]
    Sbf = [P.sb("Sbf%d" % i, [128, 128], BF16) for i in range(2)]
    o_all = P.sb("o_all", [128, NTL, 256], BF16)

    PB = P.ps("PB", [128, 4, 512])
    PTr = P.ps("PTr", [128, 1024], BF16)
    PG = P.ps("PG", [128, 512])
    PSC = P.ps("PSC", [128, 2, 512])

    def slot(i):
        return PB[:, i // 4, (i % 4) * 128:(i % 4 + 1) * 128]

    for dst, src in ((pm1, pm1_d), (wab, wab_d), (cw, cw_d), (cst, cst_d), (hs, hs_d), (nw, nw_d), (identb, idb_d)):
        P.dma("sp", dst[:], src)
    for kc in range(8):
        P.dma("pool", w[:, kc, :], w_d[:, kc, :])
    P.memset("dve", onesb[:], 1.0)
    P.memset("dve", eps6[:], 1e-6)
    P.memset("dve", eps6q[:], 128e-6)
    P.memset("dve", one1[:], 1.0)
    for i in range(6):
        P.memset("pool", xc[i][:, 0:3], 0.0)
    for i in range(4):
        P.memset("pool", wTA[i][:], 0.0)
        P.memset("pool", wTB[i][:], 0.0)
    for h2 in range(2):
        P.memset("dve", St[h2][:], 0.0)
        P.memset("dve", Sbf[h2][:], 0.0)
    P.ts("dve", pm1[:, 0, :], pm1[:, 0, :], 1.0, None, ALU.add)
    if nblk:
        P.memset("pool", o_all[:], 0.0)
    P.act(negA[:], hs[:, 2:4], AF.Exp)
    P.ts("dve", negA[:], negA[:], -1.0, None, ALU.mult)

    xi_ = 0
    for blk in range(nblk or (S // 512)):
        t0 = blk * 512
        for kc in range(8):
            xb = xs[xi_ % 3]
            xi_ += 1
            P.dma("sp", xb[:], xT_d[:, kc, t0:t0 + 512])
            P.act(hT[:, kc, :], xb[:], AF.Identity, bias=pm1[:, 1, kc:kc + 1], scale=pm1[:, 0, kc:kc + 1])
            P.ts("dve", hTf[:, kc, :], xb[:], pm1[:, 0, kc:kc + 1], pm1[:, 1, kc:kc + 1], ALU.mult, ALU.add)
        for tl in range(4):
            for kc in range(8):
                P.mm(PG[:, tl * 4:tl * 4 + 4], hTf[:, kc, tl * 128:(tl + 1) * 128], wab[:, kc, :], start=(kc == 0), stop=(kc == 7))
        pgv = PG[:, 0:16].rearrange("p (t c) -> p t c", c=4)
        for h2 in range(2):
            G = {k: v[h2] for k, v in gat.items()}
            P.act(G["e1"][:], pgv[:, :, h2], AF.Exp, bias=hs[:, h2:h2 + 1])
            P.act(G["e1"][:], G["e1"][:], AF.Ln, bias=one1[:])
            P.ts("dve", G["g"][:], G["e1"][:], negA[:, h2:h2 + 1], None, ALU.mult)
            P.act(G["beta"][:], pgv[:, :, 2 + h2], AF.Sigmoid)
            for nm, cm, off in (("gc", triBD, 16), ("glo", onesBD, 20), ("glb0", blk0, 24), ("glb1", blk1, 28)):
                o_ = off + h2 * 16
                P.mm(PG[:, 32 + o_ - 16:32 + o_ - 12], cm, G["g"][:], start=True, stop=True)
                P.copy("dve", G[nm][:], PG[:, 32 + o_ - 16:32 + o_ - 12])
            P.act(G["egc"][:], G["gc"][:], AF.Exp)
            P.tt("dve", G["edl"][:], G["glo"][:], G["gc"][:], ALU.subtract)
            P.act(G["edl"][:], G["edl"][:], AF.Exp)
            P.act(G["egl0"][:], G["glb0"][:], AF.Exp)
            P.act(G["egl1"][:], G["glb1"][:], AF.Exp)
            P.tt("dve", G["bg"][:], G["beta"][:], G["egc"][:], ALU.mult)
        for h2 in range(2):
            for i in range(3):
                ci = h2 * 3 + i
                pp_ = PB[:, ci % 4, :]
                c0 = h2 * 384 + i * 128
                for kc in range(8):
                    P.mm(pp_, w[:, kc, c0:c0 + 128], hT[:, kc, :], start=(kc == 0), stop=(kc == 7))
                xcb = xc[ci]
                if blk > 0:
                    P.copy("pool", xcb[:, 0:3], xcb[:, 512:515])
                P.copy("act", xcb[:, 3:515], pp_)
                cvb = cv[ci % 2]
                P.ts("dve", cvb[:], xcb[:, 0:512], cw[:, ci, 0:1], None, ALU.mult)
                for tap in range(1, 4):
                    P.stt("dve", cvb[:], xcb[:, tap:tap + 512], cw[:, ci, tap:tap + 1], cvb[:], ALU.mult, ALU.add)
                if i == 2:
                    P.act(vcT[h2][:], cvb[:], AF.Silu)
                else:
                    P.act(cvb[:], cvb[:], AF.Silu)
                    sqb, rnb = sq[ci % 2], rn[ci % 2]
                    P.act(sqb[:], cvb[:], AF.Square)
                    pn = PB[:, (ci + 2) % 4, :]
                    P.mm(pn, onesb[:], sqb[:], start=True, stop=True)
                    if i == 0:
                        P.act(rnb[:], pn, AF.Sqrt, bias=eps6q[:], scale=128.0)
                    else:
                        P.act(rnb[:], pn, AF.Sqrt, bias=eps6[:])
                    P.generic("dve", "reciprocal", (rnb[:], rnb[:]), [rnb[:]], [rnb[:]])
                    P.tt("dve", (qhT if i == 0 else khT)[h2][:], cvb[:], rnb[:], ALU.mult)
            for tl in range(4):
                tok = slice(tl * 128, (tl + 1) * 128)
                P.transpose(PTr[:, 0:128], khT[h2][:, tok], identb[:], inc=False)
                P.transpose(PTr[:, 128:256], vcT[h2][:, tok], identb[:])
                P.copy("act", ktok[h2][:, tl, :], PTr[:, 0:128])
                P.copy("dve", vtok[h2][:, tl, :], PTr[:, 128:256])
                pz_ = PB[:, tl % 4, 0:128]
                for kc in range(8):
                    P.mm(pz_, hT[:, kc, tok], w[:, kc, 768 + h2 * 128:768 + (h2 + 1) * 128], start=(kc == 0), stop=(kc == 7))
                P.act(nz[h2][:, tl, :], pz_, AF.Silu)
                P.tt("pool", nz[h2][:, tl, :], nz[h2][:, tl, :], nw[:], ALU.mult)

        for h2 in range(2):
            G = {k: v[h2] for k, v in gat.items()}
            for tl in range(4):
                tok = slice(tl * 128, (tl + 1) * 128)
                gcc = G["gc"][:, tl:tl + 1]
                P.ts("dve", Dg[tl][:], ident_f, gcc, None, ALU.mult)
                P.mm(slot(tl), onesBD, Dg[tl][:], start=True, stop=True)
                P.stt("dve", tmpE[tl][:], slot(tl), gcc, posmask, ALU.subtract, ALU.add)
                P.act(dec[tl][:], tmpE[tl][:], AF.Exp, scale=-1.0)
                P.stt("dve", tmpE[tl][:], slot(tl), gcc, posmaskT, ALU.subtract, ALU.subtract)
                P.act(decT[tl][:], tmpE[tl][:], AF.Exp)
                P.mm(slot(4 + tl), khT[h2][:, tok], khT[h2][:, tok], start=True, stop=True)
                P.tt("dve", tmpE[tl][:], slot(4 + tl), dec[tl][:], ALU.mult)
                P.stt("dve", Zb[tl][0][:], tmpE[tl][:], G["beta"][:, tl:tl + 1], negstrict, ALU.mult, ALU.mult)
                P.mm(slot(8 + tl), khT[h2][:, tok], qhT[h2][:, tok], start=True, stop=True)
                P.tt("dve", qkT[tl][:], slot(8 + tl), decT[tl][:], ALU.mult)
                P.ts("dve", vbt[tl][:], vtok[h2][:, tl, :], G["beta"][:, tl:tl + 1], None, ALU.mult)
                P.act(kbe[tl][:], ktok[h2][:, tl, :], AF.Identity, scale=G["bg"][:, tl:tl + 1])
                P.act(kd[tl][:], ktok[h2][:, tl, :], AF.Identity, scale=G["edl"][:, tl:tl + 1])
                P.mm(slot(12 + tl), Zb[tl][0][:], ident_f, start=True, stop=True)
                P.copy("act", Yb[tl][0][:], slot(12 + tl))
                P.tt("dve", Ttb[tl][0][:], slot(12 + tl), ident_f, ALU.add)
                P.tt("pool", Tmb[tl][0][:], Zb[tl][0][:], ident_f, ALU.add)
            for st in range(1, 6):
                a_, b_ = (st - 1) % 2, st % 2
                last = (st == 5)
                for tl in range(4):
                    P.mm(slot(tl), Zb[tl][a_][:], Yb[tl][a_][:], start=True, stop=True)
                    if not last:
                        P.mm(slot(4 + tl), Yb[tl][a_][:], Zb[tl][a_][:], start=True, stop=True)
                for tl in range(4):
                    P.copy("act", Yb[tl][b_][:], slot(tl))
                    if not last:
                        P.copy("act", Zb[tl][b_][:], slot(4 + tl))
                for tl in range(4):
                    P.mm(slot(8 + tl), Tmb[tl][a_][:], Yb[tl][b_][:], start=True, stop=True)
                    if not last:
                        P.mm(slot(12 + tl), Ttb[tl][a_][:], Zb[tl][b_][:], start=True, stop=True)
                for tl in range(4):
                    if last:
                        P.tt("dve", Ttbf[tl][:], slot(8 + tl), Ttb[tl][a_][:], ALU.add)
                    else:
                        P.tt("dve", Ttb[tl][b_][:], slot(8 + tl), Ttb[tl][a_][:], ALU.add)
                        P.tt("dve", Tmb[tl][b_][:], slot(12 + tl), Tmb[tl][a_][:], ALU.add)
            for tl in range(4):
                P.mm(slot(tl), Ttbf[tl][:], vbt[tl][:], start=True, stop=True)
                P.mm(slot(4 + tl), kbe[tl][:], Ttbf[tl][:], start=True, stop=True)
                P.copy("act", u_sb[tl][:], slot(tl))
                P.copy("dve", wTA[tl][:, 0:64], slot(4 + tl)[:, 0:64])
                P.copy("act", wTB[tl][:, 64:128], slot(4 + tl)[:, 64:128])
            S_, Sb_ = St[h2], Sbf[h2]
            for tl in range(4):
                tok = slice(tl * 128, (tl + 1) * 128)
                gt = blk * 4 + tl
                vnb, o1b, osb = vn[tl % 2], o1[tl % 2], osum[tl % 2]
                for j in range(2):
                    pr = slice(j * 64, j * 64 + 64)
                    pws = PSC[:, j, 0:128]
                    po1 = PSC[:, j, 128:256]
                    psu = PSC[:, j, 256:384]
                    P.mm(pws, (wTA if j == 0 else wTB)[tl][:], Sb_[:], start=True, stop=True)
                    P.mm(po1, qhT[h2][:, tok], Sb_[:], start=True, stop=True)
                    P.tt("dve", vnb[pr, :], u_sb[tl][pr, :], pws[pr, :], ALU.subtract)
                    P.act(o1b[pr, :], po1[pr, :], AF.Identity, scale=G["egc"][pr, tl:tl + 1])
                    P.mm(psu, kd[tl][pr, :], vnb[pr, :], start=True, stop=True)
                    P.stt("dve", S_[:], S_[:], G["egl%d" % j][:, tl:tl + 1], psu, ALU.mult, ALU.add)
                    P.copy("act", Sb_[:], S_[:])
                po2 = PSC[:, 0, 384:512]
                P.mm(po2, qkT[tl][:], vnb[:], start=True, stop=True)
                P.tt("dve", osb[:], o1b[:], po2, ALU.add)
                P.tt("dve", osq[:], osb[:], osb[:], ALU.mult)
                P.generic("dve", "reduce_sum", (ssq[:], osq[:]), [osq[:]], [ssq[:]], axis=AX.X)
                P.act(rinv[:], ssq[:], AF.Sqrt, bias=eps6[:], scale=1.0 / 128.0)
                P.generic("dve", "reciprocal", (rinv[:], rinv[:]), [rinv[:]], [rinv[:]])
                P.stt("dve", o_all[:, gt, h2 * 128:(h2 + 1) * 128], osb[:], rinv[:], nz[h2][:, tl, :], ALU.mult, ALU.mult)
    for q4 in range(4):
        P.dma("sp", out_d[:, q4 * 16:(q4 + 1) * 16, :], o_all[:, q4 * 16:(q4 + 1) * 16, :], is_output=True)
    P.emit()
    return nc


def run_Mgdn(x_full, mod, inp):
    if "Mgdn" not in _CACHE:
        _CACHE["Mgdn"] = build_Mgdn()
    nc = _CACHE["Mgdn"]
    layer = 0
    wi = np.asarray(inp["gdn_w_in"])
    cwf = np.asarray(inp["gdn_conv_w"])
    p = np.arange(128)
    same = (p[:, None] // 64) == (p[None, :] // 64)
    ident = np.eye(128, dtype=np.float32)
    onesBD = same.astype(np.float32)
    triBD = (same & (p[:, None] <= p[None, :])).astype(np.float32)
    blk0 = np.broadcast_to((p[:, None] < 64), (128, 128)).astype(np.float32)
    blk1 = np.broadcast_to((p[:, None] >= 64), (128, 128)).astype(np.float32)
    posmask = np.where(same & (p[None, :] <= p[:, None]), 0.0, 1e4).astype(np.float32)
    posmaskT = np.where(same & (p[None, :] >= p[:, None]), 0.0, 1e4).astype(np.float32)
    negstrict = np.where(same & (p[None, :] < p[:, None]), -1.0, 0.0).astype(np.float32)
    cst = np.ascontiguousarray(np.stack([ident, onesBD, triBD, blk0, blk1, posmask, posmaskT, negstrict], 1))
    nwr = np.ascontiguousarray(np.broadcast_to(np.asarray(inp["gdn_norm_w"])[None, :], (128, 128))).astype(np.float32)
    identb = ident.astype(ml_dtypes.bfloat16)
    xTs = [xT_layout(x_full[b]) for b in range(2)]
    in_maps = []
    for core in range(NCORES):
        b, hp = divmod(core, 4)
        m = mod[b, layer]
        sh1, sc1 = m[0:1024], m[1024:2048]
        pm1 = np.ascontiguousarray(np.stack([pm(sc1), pm(sh1)], 1)).astype(np.float32)
        cols = []
        cws = []
        for h2 in range(2):
            hd = hp * 2 + h2
            for i in range(3):
                cols.append(wi[:, i * 1024 + hd * 128:i * 1024 + (hd + 1) * 128])
                cws.append(cwf[:, i * 1024 + hd * 128:i * 1024 + (hd + 1) * 128].T)
        for h2 in range(2):
            hd = hp * 2 + h2
            cols.append(wi[:, 3072 + hd * 128:3072 + (hd + 1) * 128])
        wcat = np.concatenate(cols, axis=1)
        h0, h1 = hp * 2, hp * 2 + 1
        wab = np.stack([wi[:, 4096 + h0], wi[:, 4096 + h1], wi[:, 4104 + h0], wi[:, 4104 + h1]], 1)
        hs = np.array([inp["gdn_dt_bias"][h0], inp["gdn_dt_bias"][h1], inp["gdn_a_log"][h0], inp["gdn_a_log"][h1]], np.float32)
        in_maps.append({
            "xT": xTs[b], "pm1": pm1, "w": wtile(wcat), "wab": wtile(wab),
            "cw": np.ascontiguousarray(np.stack(cws, 1)).astype(np.float32),
            "cst": cst, "hs": np.ascontiguousarray(np.broadcast_to(hs[None], (128, 4))), "nw": nwr, "identb": identb,
        })
    res = run_bass_kernel_spmd(nc, in_maps, core_ids=list(range(NCORES)))
    o_full = np.empty((2, S, 1024), ml_dtypes.bfloat16)
    for core in range(NCORES):
        b, hp = divmod(core, 4)
        o = np.asarray(res.results[core]["o"])
        o_full[b, :, hp * 256:(hp + 1) * 256] = o.transpose(1, 0, 2).reshape(S, 256)
    return oT_from_tokenmajor(o_full)


def kernel(**inputs):
    inp = {k: np.asarray(v) for k, v in inputs.items()}
    mod = run_C(inp)
    x = np.ascontiguousarray(inp["x"], dtype=np.float32)
    mixers = (run_Mgdn, run_Mret, run_Mgmlp, run_Msb)
    wouts = (inp["gdn_w_out"], inp["ret_w_out"], inp["gmlp_w_out"], inp["sb_w_out"])
    for layer in range(DEPTH):
        oT = mixers[layer](x, mod, inp)
        x = run_F(x, oT, layer, mod, inp, wouts[layer])
    return x.astype(np.float32)
```

```python
import contextlib
import math

import numpy as np
import ml_dtypes

import concourse.bass as bass
import concourse.mybir as mybir
from concourse.bass_utils import run_bass_kernel_spmd

F32 = mybir.dt.float32
BF16 = mybir.dt.bfloat16
AF = mybir.ActivationFunctionType
ALU = mybir.AluOpType
AX = mybir.AxisListType

NCORES = 8
D = 1024
B = 2
S = 8192
DEPTH = 4
FH = 2816
ALPHA = (2.0 * DEPTH) ** 0.25
LN_EPS = 1e-5

ENGS = ("pe", "act", "dve", "pool", "sp")


def _prod(xs):
    r = 1
    for v in xs:
        r *= int(v)
    return r


class Prog:
    N_DMA_SLOTS = 12

    def __init__(self, nc):
        self.nc = nc
        self.stack = contextlib.ExitStack()
        self.streams = {e: [] for e in ENGS}
        self.cnt = {e: 0 for e in ENGS}
        self.seen = {e: {} for e in ENGS}
        self.acc = {}
        self.esem = {e: self.stack.enter_context(nc.semaphore("sem_" + e)) for e in ENGS}
        self.semobj = {("e", e): self.esem[e] for e in ENGS}
        self.dma_slots = {}
        self.dma_next = {}
        for q in ("sp", "pool", "act"):
            self.dma_slots[q] = []
            self.dma_next[q] = 0
        self.out_dmas = []
        self.psum_names = set()

    def sb(self, name, shape, dtype=F32):
        return self.stack.enter_context(self.nc.sbuf_tensor(name, list(shape), dtype))

    def ps(self, name, shape, dtype=F32):
        self.psum_names.add(name)
        return self.stack.enter_context(self.nc.psum_tensor(name, list(shape), dtype))

    def dram(self, name, shape, dtype, kind):
        return self.nc.dram_tensor(name, list(shape), dtype, kind=kind).ap()

    @staticmethod
    def region(ap):
        t = ap.tensor
        pairs = [(int(s), int(c)) for s, c in ap.ap]
        off = int(ap.offset)
        kind = type(t).__name__
        if kind.startswith("DRam"):
            ext = sum((c - 1) * abs(s) for s, c in pairs)
            return (t.name, 0, 0, off, off + ext)
        fsz = _prod(t.shape[1:])
        p0, f0 = divmod(off, fsz)
        ps_, pc = pairs[0]
        pstep = ps_ // fsz if ps_ else 0
        ext = sum((c - 1) * abs(s) for s, c in pairs[1:])
        f1 = f0 + ext
        if kind.startswith("PSum"):
            epb = 2048 // (2 if t.dtype == BF16 else 4)
            f0 = (f0 // epb) * epb
            f1 = (f1 // epb) * epb + epb - 1
        return (t.name, p0, p0 + (pc - 1) * pstep, f0, f1)

    def _deps(self, eng, reads, writes, rec_key):
        need = {}

        def add(k, v):
            if need.get(k, 0) < v:
                need[k] = v

        rr = [self.region(a) for a in reads]
        ww = [self.region(a) for a in writes]
        for (name, pl, ph, fl, fh) in rr:
            ps_rar = name in self.psum_names
            for rec in self.acc.get(name, ()):
                if (rec[5] or (ps_rar and rec[4] != eng)) and not (rec[1] < pl or rec[0] > ph or rec[3] < fl or rec[2] > fh):
                    add(rec[6], rec[7])
        for (name, pl, ph, fl, fh) in ww:
            for rec in self.acc.get(name, ()):
                if not (rec[1] < pl or rec[0] > ph or rec[3] < fl or rec[2] > fh):
                    add(rec[6], rec[7])
        for (name, pl, ph, fl, fh) in ww:
            lst = self.acc.setdefault(name, [])
            lst[:] = [r for r in lst if not (r[0] >= pl and r[1] <= ph and r[2] >= fl and r[3] <= fh)]
            lst.append((pl, ph, fl, fh, eng, True, rec_key[0], rec_key[1]))
        for (name, pl, ph, fl, fh) in rr:
            lst = self.acc.setdefault(name, [])
            lst[:] = [r for r in lst if not ((not r[5]) and r[6] == rec_key[0]
                                             and r[0] >= pl and r[1] <= ph and r[2] >= fl and r[3] <= fh)]
            lst.append((pl, ph, fl, fh, eng, False, rec_key[0], rec_key[1]))
        waits = []
        own = ("e", eng)
        for k, v in need.items():
            if k == own:
                if eng == "pe" or v > self.cnt[eng]:
                    continue
            if self.seen[eng].get(k, 0) >= v:
                continue
            self.seen[eng][k] = v
            waits.append((k, v))
        return waits

    def op(self, eng, name, args, kwargs, reads, writes, inc=True):
        own = ("e", eng)
        val = self.cnt[eng] + 1
        waits = self._deps(eng, reads, writes, (own, val))
        if inc:
            self.cnt[eng] = val
        self.streams[eng].append((name, args, kwargs, waits, own if inc else None, 1))

    def dma(self, q, out, in_, is_output=False, **kwargs):
        slots = self.dma_slots[q]
        if len(slots) < self.N_DMA_SLOTS:
            key = ("d", q, len(slots))
            sem = self.stack.enter_context(self.nc.semaphore("dsem_%s_%d" % (q, len(slots))))
            self.semobj[key] = sem
            slots.append([key, 0])
            slot = slots[-1]
        else:
            slot = slots[self.dma_next[q] % self.N_DMA_SLOTS]
        self.dma_next[q] += 1
        key, uses = slot
        val = 16 * (uses + 1)
        waits = self._deps(q, [in_], [out], (key, val))
        if uses > 0 and self.seen[q].get(key, 0) < 16 * uses:
            self.seen[q][key] = 16 * uses
            waits.append((key, 16 * uses))
        slot[1] = uses + 1
        self.streams[q].append(("dma_start", (), dict(out=out, in_=in_, **kwargs), waits, key, 16))
        if is_output:
            self.out_dmas.append((q, key, val))

    def collective(self, kind, out, in_, groups, amt=16, op=None):
        q = "pool"
        key = ("c", len(self.semobj))
        sem = self.stack.enter_context(self.nc.semaphore("csem_%d" % len(self.semobj)))
        self.semobj[key] = sem
        waits = self._deps(q, [in_], [out], (key, amt))
        self.streams[q].append(("collective_compute", (kind, op if op is not None else ALU.bypass),
                                dict(replica_groups=groups, ins=[in_], outs=[out]), waits, key, amt))

    def mm(self, out, lhsT, rhs, start=True, stop=True, inc=None):
        if inc is None:
            inc = stop
        self.op("pe", "matmul", (out, lhsT, rhs), dict(start=start, stop=stop), [lhsT, rhs], [out], inc=inc)

    def transpose(self, out, in_, ident, inc=True):
        self.op("pe", "transpose", (out, in_, ident), {}, [in_, ident], [out], inc=inc)

    def act(self, out, in_, func, bias=None, scale=None, accum_out=None, eng="act"):
        kw = {}
        reads = [in_]
        writes = [out]
        if bias is not None:
            kw["bias"] = bias
            if not isinstance(bias, (int, float)):
                reads.append(bias)
        if scale is not None:
            kw["scale"] = scale
            if not isinstance(scale, (int, float)):
                reads.append(scale)
        if accum_out is not None:
            kw["accum_out"] = accum_out
            writes.append(accum_out)
        self.op(eng, "activation", (out, in_, func), kw, reads, writes)

    def tt(self, eng, out, in0, in1, op):
        self.op(eng, "tensor_tensor", (out, in0, in1, op), {}, [in0, in1], [out])

    def ts(self, eng, out, in0, s1, s2, op0, op1=None, accum_out=None):
        reads = [in0]
        for s in (s1, s2):
            if s is not None and not isinstance(s, (int, float)):
                reads.append(s)
        kw = {}
        writes = [out]
        if accum_out is not None:
            kw["accum_out"] = accum_out
            writes.append(accum_out)
        if op1 is None:
            self.op(eng, "tensor_scalar", (out, in0, s1, None, op0), kw, reads, writes)
        else:
            self.op(eng, "tensor_scalar", (out, in0, s1, s2, op0, op1), kw, reads, writes)

    def stt(self, eng, out, in0, scalar, in1, op0, op1):
        reads = [in0, in1]
        if not isinstance(scalar, (int, float)):
            reads.append(scalar)
        self.op(eng, "scalar_tensor_tensor", (out, in0, scalar, in1, op0, op1), {}, reads, [out])

    def copy(self, eng, out, in_):
        if eng == "act":
            self.op(eng, "copy", (out, in_), {}, [in_], [out])
        else:
            self.op(eng, "tensor_copy", (out, in_), {}, [in_], [out])

    def memset(self, eng, ap, val):
        self.op(eng, "memset", (ap, val), {}, [], [ap])

    def generic(self, eng, name, args, reads, writes, **kwargs):
        self.op(eng, name, tuple(args), kwargs, reads, writes)

    def check(self):
        counts = {}
        ptr = {e: 0 for e in ENGS}
        while True:
            progress = False
            for e in ENGS:
                st = self.streams[e]
                while ptr[e] < len(st):
                    name, args, kwargs, waits, inc, amt = st[ptr[e]]
                    if any(counts.get(k, 0) < v for k, v in waits):
                        break
                    if inc is not None:
                        counts[inc] = counts.get(inc, 0) + amt
                    ptr[e] += 1
                    progress = True
            if all(ptr[e] == len(self.streams[e]) for e in ENGS):
                return
            if not progress:
                msg = []
                for e in ENGS:
                    if ptr[e] < len(self.streams[e]):
                        name, args, kwargs, waits, inc, amt = self.streams[e][ptr[e]]
                        bad = [(k, v, counts.get(k, 0)) for k, v in waits if counts.get(k, 0) < v]
                        msg.append("%s stuck at %d/%d (%s) waiting %s" % (e, ptr[e], len(self.streams[e]), name, bad))
                raise RuntimeError("DEADLOCK in recorded program:\n" + "\n".join(msg))

    def emit(self):
        nc = self.nc
        self.check()
        tail = {}
        for q, key, val in self.out_dmas:
            d = tail.setdefault(q, {})
            d[key] = max(d.get(key, 0), val)

        def replay(e, eng):
            for (name, args, kwargs, waits, inc, amt) in self.streams[eng]:
                for k, v in waits:
                    e.wait_ge(self.semobj[k], v)
                ins = getattr(e, name)(*args, **kwargs)
                if inc is not None:
                    ins.then_inc(self.semobj[inc], amt)
            for k, v in tail.get(eng, {}).items():
                e.wait_ge(self.semobj[k], v)

        with nc.Block() as block:
            @block.tensor
            def _(e):
                replay(e, "pe")

            @block.scalar
            def _(e):
                replay(e, "act")

            @block.vector
            def _(e):
                replay(e, "dve")

            @block.gpsimd
            def _(e):
                replay(e, "pool")

            @block.sync
            def _(e):
                replay(e, "sp")
        self.stack.close()

    def stats(self):
        return {e: len(self.streams[e]) for e in ENGS}


def build_C():
    nc = bass.Bass("TRN2", target_bir_lowering=False)
    P = Prog(nc)
    cT_d = P.dram("cT", [128, 8, 2], F32, "ExternalInput")
    cw_d = P.dram("cond_w", [128, 8, 1024], F32, "ExternalInput")
    cb_d = P.dram("cond_b", [128, 8], F32, "ExternalInput")
    aw_d = P.dram("ada_w", [128, 8, 3072], F32, "ExternalInput")
    ab_d = P.dram("ada_b", [128, 24], F32, "ExternalInput")
    out_d = P.dram("modpm", [128, 24, 2], F32, "ExternalOutput")

    cT = P.sb("cT_sb", [128, 8, 2])
    cw = P.sb("cw_sb", [128, 8, 1024])
    cb = P.sb("cb_sb", [128, 8])
    aw = P.sb("aw_sb", [128, 8, 3072])
    ab = P.sb("ab_sb", [128, 24])
    eT = P.sb("eT_sb", [128, 8, 2])
    mo = P.sb("mo_sb", [128, 24, 2])
    e_ps = P.ps("e_ps", [128, 8, 2])
    m_ps = P.ps("m_ps", [128, 24, 2])

    P.dma("sp", cT[:], cT_d)
    P.dma("sp", cb[:], cb_d)
    P.dma("sp", ab[:], ab_d)
    for kc in range(8):
        P.dma("sp", cw[:, kc, :], cw_d[:, kc, :])
    for kc in range(8):
        P.dma("sp", aw[:, kc, :], aw_d[:, kc, :])
    for jc in range(8):
        for kc in range(8):
            P.mm(e_ps[:, jc, :], cw[:, kc, jc * 128:(jc + 1) * 128], cT[:, kc, :], start=(kc == 0), stop=(kc == 7))
        P.act(eT[:, jc, :], e_ps[:, jc, :], AF.Silu, bias=cb[:, jc:jc + 1])
    for jc in range(24):
        for kc in range(8):
            P.mm(m_ps[:, jc, :], aw[:, kc, jc * 128:(jc + 1) * 128], eT[:, kc, :], start=(kc == 0), stop=(kc == 7))
        P.act(mo[:, jc, :], m_ps[:, jc, :], AF.Identity, bias=ab[:, jc:jc + 1])
    P.dma("sp", out_d, mo[:], is_output=True)
    P.emit()
    return nc


def pm(v, n=128):
    v = np.asarray(v)
    k = v.shape[-1] // n
    return np.ascontiguousarray(np.moveaxis(v.reshape(v.shape[:-1] + (k, n)), -1, 0))


def wtile(w):
    w = np.asarray(w)
    K, N = w.shape
    return np.ascontiguousarray(w.reshape(K // 128, 128, N).transpose(1, 0, 2))


def run_C(inp):
    nc = build_C()
    ada_flat = np.asarray(inp["ada_w"]).transpose(1, 0, 2).reshape(1024, 4 * 6144)
    adab_flat = np.asarray(inp["ada_b"]).reshape(4 * 6144)
    cT = pm(inp["c"])
    cT = np.ascontiguousarray(cT.transpose(0, 2, 1))
    cw = wtile(inp["cond_w"])
    cb = pm(inp["cond_b"])
    in_maps = []
    for core in range(NCORES):
        sl = slice(core * 3072, (core + 1) * 3072)
        in_maps.append({
            "cT": cT, "cond_w": cw, "cond_b": cb,
            "ada_w": wtile(ada_flat[:, sl]),
            "ada_b": pm(adab_flat[sl]),
        })
    res = run_bass_kernel_spmd(nc, in_maps, core_ids=list(range(NCORES)))
    mod = np.zeros((2, 4 * 6144), np.float32)
    for core in range(NCORES):
        o = np.asarray(res.results[core]["modpm"])
        mod[:, core * 3072:(core + 1) * 3072] = o.transpose(2, 1, 0).reshape(2, 3072)
    return mod.reshape(2, 4, 6144)


NT = 16
TBT = 8
GH = 2
NG = 22 // GH


def ln_tile(P, y_ps, x_in, Gt, lng, lnb, x_out, xhat_bf, sc):
    tmp, stats, mv, rstd, xhat = sc["tmp"], sc["stats"], sc["mv"], sc["rstd"], sc["xhat"]
    P.tt("dve", tmp[:], y_ps, Gt, ALU.mult)
    P.stt("dve", tmp[:], x_in, ALPHA, tmp[:], ALU.mult, ALU.add)
    for h in range(2):
        P.generic("dve", "bn_stats", (stats[:, h, :], tmp[:, h * 512:(h + 1) * 512]),
                  [tmp[:, h * 512:(h + 1) * 512]], [stats[:, h, :]])
    P.generic("dve", "bn_aggr", (mv[:], stats[:].rearrange("p a b -> p (a b)")), [stats[:]], [mv[:]])
    P.act(rstd[:], mv[:, 1:2], AF.Sqrt, bias=sc["eps"][:])
    P.generic("dve", "reciprocal", (rstd[:], rstd[:]), [rstd[:]], [rstd[:]])
    P.ts("dve", xhat[:], tmp[:], mv[:, 0:1], rstd[:], ALU.subtract, ALU.mult)
    if xhat_bf is not None:
        P.copy("act", xhat_bf, xhat[:])
    P.tt("pool", x_out, xhat[:], lng, ALU.mult)
    P.tt("pool", x_out, x_out, lnb, ALU.add)


def build_F(W):
    KO = W // 128
    nc = bass.Bass("TRN2", target_bir_lowering=False)
    P = Prog(nc)
    NTT = NT + 1
    x_d = P.dram("xin", [NTT * 128, 1024], F32, "ExternalInput")
    oT_d = P.dram("oT", [128, KO, NTT * 128], BF16, "ExternalInput")
    wo_d = P.dram("w_out", [128, KO * 1024], F32, "ExternalInput")
    wu_d = P.dram("w_up", [NG, 128, 8 * 2 * GH * 128], F32, "ExternalInput")
    wd_d = P.dram("w_dn", [NG, 128, GH * 1024], F32, "ExternalInput")
    cw_d = P.dram("conv_w", [128, 22, 3], F32, "ExternalInput")
    cb_d = P.dram("conv_b", [128, 22], F32, "ExternalInput")
    rows_d = P.dram("rows", [128, 6, 1024], F32, "ExternalInput")
    pmv_d = P.dram("pmv", [128, 4, 8], F32, "ExternalInput")
    flag_d = P.dram("flag", [128, 1], F32, "ExternalInput")
    id_d = P.dram("ident", [128, 128], BF16, "ExternalInput")
    out_d = P.dram("xout", [NT * 128, 1024], F32, "ExternalOutput")

    big = P.sb("big", [128, 22 * 1024], BF16)
    wup = [P.sb("wup%d" % i, [128, 8, 2, GH * 128], BF16) for i in range(3)]
    wdn = [P.sb("wdn%d" % i, [128, GH, 1024], BF16) for i in range(3)]
    x1 = P.sb("x1", [128, TBT, 1024])
    xin = [P.sb("xin%d" % i, [128, 1024]) for i in range(2)]
    oT = [P.sb("oT%d" % i, [128, KO, 128], BF16) for i in range(3)]
    hT = P.sb("hT", [128, 8, TBT * 128], BF16)
    hTs = P.sb("hTs", [128, 8, 128], BF16)
    rows = P.sb("rows_sb", [128, 6, 1024])
    pmv = P.sb("pmv_sb", [128, 4, 8])
    A2 = P.sb("A2", [128, 8])
    B2 = P.sb("B2", [128, 8])
    cw = P.sb("cw_sb", [128, 22, 3])
    cb = P.sb("cb_sb", [128, 22])
    flag = P.sb("flag_sb", [128, 1])
    ident = P.sb("ident_sb", [128, 128], BF16)
    sc = dict(tmp=P.sb("tmp", [128, 1024]), stats=P.sb("stats", [128, 2, 6]), mv=P.sb("mv", [128, 2]),
              rstd=P.sb("rstd", [128, 1]), xhat=P.sb("xhat", [128, 1024]), eps=P.sb("eps", [128, 1]))
    P.memset("dve", sc["eps"][:], LN_EPS)
    xhb = [P.sb("xhb%d" % i, [128, 1024], BF16) for i in range(4)]
    xo = [P.sb("xo%d" % i, [128, 1024]) for i in range(2)]
    gsb = [P.sb("gsb%d" % i, [128, 2 + 512]) for i in range(2)]
    gc = [P.sb("gc%d" % i, [128, 512]) for i in range(2)]
    gs_rot = [0]
    ghalo = P.sb("ghalo", [128, 22, 2])
    PA = P.ps("PA", [128, 6, 512])
    PT = P.ps("PT", [128, 2, 1024], BF16)

    G1, G2 = rows[:, 0, :], rows[:, 1, :]
    lng0, lnb0, lng1, lnb1 = rows[:, 2, :], rows[:, 3, :], rows[:, 4, :], rows[:, 5, :]

    P.dma("sp", rows[:], rows_d)
    P.dma("sp", pmv[:], pmv_d)
    P.dma("sp", cw[:], cw_d)
    P.dma("sp", cb[:], cb_d)
    P.dma("sp", flag[:], flag_d)
    P.dma("sp", ident[:], id_d)
    P.ts("dve", rows[:, 0:2, :], rows[:, 0:2, :], 1.0, None, ALU.add)
    P.ts("dve", pmv[:, 0, :], pmv[:, 0, :], 1.0, None, ALU.add)
    P.tt("dve", A2[:], pmv[:, 2, :], pmv[:, 0, :], ALU.mult)
    P.tt("dve", B2[:], pmv[:, 3, :], pmv[:, 0, :], ALU.mult)
    P.tt("dve", B2[:], B2[:], pmv[:, 1, :], ALU.add)

    ps_rot = [0]

    def next_y():
        i = ps_rot[0] % 3
        ps_rot[0] += 1
        return PA[:, 2 * i:2 * i + 2, :].rearrange("p a b -> p (a b)"), (PA[:, 2 * i, :], PA[:, 2 * i + 1, :])

    wout_v = big[:, 0:KO * 1024].rearrange("p (k n) -> p k n", k=KO)
    uT = big[:].rearrange("p (j t) -> p j t", j=22)

    xin_rot = [0]
    oT_rot = [0]
    xhb_rot = [0]
    pt_rot = [0]

    def phase_A(tiles, dst_x1, dst_hT):
        n = len(tiles)
        xh_list = []
        for i, gt in enumerate(tiles):
            ob = oT[oT_rot[0] % 3]
            oT_rot[0] += 1
            P.dma("sp", ob[:], oT_d[:, :, gt * 128:(gt + 1) * 128])
            xb = xin[xin_rot[0] % 2]
            xin_rot[0] += 1
            P.dma("sp", xb[:], x_d[gt * 128:(gt + 1) * 128, :])
            yfull, (y0, y1) = next_y()
            oc = 0
            for kc in range(KO):
                P.mm(y0, ob[:, kc, oc:oc + 128], wout_v[:, kc, 0:512], start=(kc == 0), stop=(kc == KO - 1), inc=False)
                P.mm(y1, ob[:, kc, oc:oc + 128], wout_v[:, kc, 512:1024], start=(kc == 0), stop=(kc == KO - 1),
                     inc=(kc == KO - 1))
            xh = xhb[xhb_rot[0] % 4]
            xhb_rot[0] += 1
            d = dst_x1(i)
            if d is None:
                d = xo[0][:]
            ln_tile(P, yfull, xb[:], G1, lng0, lnb0, d, xh[:], sc)
            xh_list.append(xh)
            if len(xh_list) == 4 or i == n - 1:
                m = len(xh_list)
                base = (i - m + 1) * 128
                for j in range(8):
                    pt = PT[:, pt_rot[0] % 2, :]
                    pt_rot[0] += 1
                    for q in range(m):
                        P.transpose(pt[:, q * 128:(q + 1) * 128], xh_list[q][:, j * 128:(j + 1) * 128], ident[:],
                                    inc=(q == m - 1))
                    P.act(dst_hT[:, j, base:base + m * 128], pt[:, 0:m * 128], AF.Identity,
                          bias=B2[:, j:j + 1], scale=A2[:, j:j + 1])
                xh_list = []

    wu_i = [0]
    wd_i = [0]

    def load_wup(g):
        b = wup[wu_i[0] % 3]
        wu_i[0] += 1
        P.dma("pool", b[:].rearrange("p a b c -> p (a b c)"), wu_d[g])
        return b

    def load_wdn(g):
        b = wdn[wd_i[0] % 3]
        wd_i[0] += 1
        P.dma("pool", b[:].rearrange("p a b -> p (a b)"), wd_d[g])
        return b

    for tb in range(NT // TBT):
        for k4 in range(0, KO, 4):
            P.dma("pool", big[:, k4 * 1024:(k4 + 4) * 1024], wo_d[:, k4 * 1024:(k4 + 4) * 1024])
        if tb == 0:
            phase_A([0], lambda i: None, hTs)
        phase_A([1 + tb * TBT + i for i in range(TBT)], lambda i: x1[:, i, :], hT)

        wq = [load_wup(0), load_wup(1)]
        pb = 0
        for g in range(NG):
            if g + 2 < NG:
                wq.append(load_wup(g + 2))
            wb = wq[g]
            for jj in range(GH):
                j = g * GH + jj
                gs_prev = None
                for half in range(2):
                    gs = gsb[gs_rot[0] % 2]
                    g_c = gc[gs_rot[0] % 2]
                    gs_rot[0] += 1
                    if half == 1:
                        P.copy("pool", gs[:, 0:2], gs_prev[:, 512:514])
                    elif tb == 0:
                        hp = PA[:, (pb % 3) * 2, 0:2]
                        for kc in range(8):
                            P.mm(hp, wb[:, kc, 0, jj * 128:(jj + 1) * 128], hTs[:, kc, 126:128], start=(kc == 0), stop=(kc == 7))
                        P.ts("dve", gs[:, 0:2], hp, flag[:, 0:1], None, ALU.mult)
                        pb += 1
                    else:
                        P.copy("pool", gs[:, 0:2], ghalo[:, j, :])
                    i3 = pb % 3
                    pb += 1
                    gp, up = PA[:, 2 * i3, :], PA[:, 2 * i3 + 1, :]
                    tok = slice(half * 512, (half + 1) * 512)
                    for kc in range(8):
                        P.mm(gp, wb[:, kc, 0, jj * 128:(jj + 1) * 128], hT[:, kc, tok], start=(kc == 0), stop=(kc == 7))
                    for kc in range(8):
                        P.mm(up, wb[:, kc, 1, jj * 128:(jj + 1) * 128], hT[:, kc, tok], start=(kc == 0), stop=(kc == 7))
                    P.copy("act", gs[:, 2:514], gp)
                    P.ts("dve", g_c[:], gs[:, 0:512], cw[:, j, 0:1], cb[:, j:j + 1], ALU.mult, ALU.add)
                    P.stt("dve", g_c[:], gs[:, 1:513], cw[:, j, 1:2], g_c[:], ALU.mult, ALU.add)
                    P.stt("dve", g_c[:], gs[:, 2:514], cw[:, j, 2:3], g_c[:], ALU.mult, ALU.add)
                    P.act(g_c[:], g_c[:], AF.Silu)
                    P.tt("dve", uT[:, j, tok], g_c[:], up, ALU.mult)
                    gs_prev = gs
                if tb + 1 < NT // TBT:
                    P.copy("pool", ghalo[:, j, :], gs_prev[:, 512:514])

        dq = [load_wdn(0), load_wdn(1)]
        t0 = 0
        while t0 < TBT:
            tl = list(range(t0, min(t0 + 3, TBT)))
            ys = [next_y() for _ in tl]
            for g in range(NG):
                if g + 2 < NG:
                    dq.append(load_wdn(g + 2))
                elif t0 + 3 < TBT:
                    dq.append(load_wdn(g + 2 - NG))
                db = dq.pop(0)
                for jj in range(GH):
                    j = g * GH + jj
                    for ti, t in enumerate(tl):
                        y0, y1 = ys[ti][1]
                        last = (j == 21)
                        P.mm(y0, uT[:, j, t * 128:(t + 1) * 128], db[:, jj, 0:512], start=(j == 0), stop=last, inc=False)
                        P.mm(y1, uT[:, j, t * 128:(t + 1) * 128], db[:, jj, 512:1024], start=(j == 0), stop=last,
                             inc=(last or (jj == GH - 1 and ti == len(tl) - 1)))
            for ti, t in enumerate(tl):
                xb = xo[(tb * TBT + t) % 2]
                ln_tile(P, ys[ti][0], x1[:, t, :], G2, lng1, lnb1, xb[:], None, sc)
                gt = tb * TBT + t
                P.dma("sp", out_d[gt * 128:(gt + 1) * 128, :], xb[:], is_output=True)
            t0 += 3
    P.emit()
    return nc


_CACHE = {}


def bf16(a):
    return np.asarray(a).astype(ml_dtypes.bfloat16)


def run_F(x_full, oT_cores, layer, mod, inp, w_out):
    W = w_out.shape[0]
    KO = W // 128
    key = ("F", W)
    if key not in _CACHE:
        _CACHE[key] = build_F(W)
    nc = _CACHE[key]
    wup = np.asarray(inp["ffn_up"][layer])
    wt = wtile(wup)
    wu_l = np.empty((NG, 128, 8, 2, GH * 128), np.float32)
    for g in range(NG):
        wu_l[g, :, :, 0, :] = wt[:, :, g * GH * 128:(g + 1) * GH * 128]
        wu_l[g, :, :, 1, :] = wt[:, :, FH + g * GH * 128:FH + (g + 1) * GH * 128]
    wu_l = wu_l.reshape(NG, 128, 8 * 2 * GH * 128)
    wd = wtile(inp["ffn_down"][layer])
    wd_l = np.ascontiguousarray(wd.reshape(128, NG, GH * 1024).transpose(1, 0, 2))
    wo_l = wtile(w_out).reshape(128, KO * 1024)
    cw = np.ascontiguousarray(pm(inp["ffn_conv_w"][layer]).transpose(0, 2, 1))
    cb = pm(inp["ffn_conv_b"][layer])
    ident = np.eye(128, dtype=np.float32).astype(ml_dtypes.bfloat16)
    lng = np.asarray(inp["ln_g"][layer])
    lnb = np.asarray(inp["ln_b"][layer])
    in_maps = []
    for core in range(NCORES):
        b, q = divmod(core, 4)
        m = mod[b, layer]
        g1, sh2, sc2, g2 = m[2048:3072], m[3072:4096], m[4096:5120], m[5120:6144]
        rows = np.stack([g1, g2, lng[0], lnb[0], lng[1], lnb[1]], 0)
        rows = np.ascontiguousarray(np.broadcast_to(rows[None], (128, 6, 1024))).astype(np.float32)
        pmv = np.ascontiguousarray(np.stack([pm(sc2), pm(sh2), pm(lng[0]), pm(lnb[0])], 1)).astype(np.float32)
        t0 = q * 2048
        xin = np.zeros((17 * 128, 1024), np.float32)
        if q > 0:
            xin[:] = x_full[b, t0 - 128:t0 + 2048]
        else:
            xin[128:] = x_full[b, 0:2048]
        in_maps.append({
            "xin": xin, "oT": oT_cores[core], "w_out": wo_l, "w_up": wu_l, "w_dn": wd_l,
            "conv_w": cw, "conv_b": cb, "rows": rows, "pmv": pmv,
            "flag": np.full((128, 1), 0.0 if q == 0 else 1.0, np.float32), "ident": ident,
        })
    res = run_bass_kernel_spmd(nc, in_maps, core_ids=list(range(NCORES)))
    out = np.empty((2, 8192, 1024), np.float32)
    for core in range(NCORES):
        b, q = divmod(core, 4)
        out[b, q * 2048:(q + 1) * 2048] = np.asarray(res.results[core]["xout"])
    return out


def oT_from_tokenmajor(o_full):
    W = o_full.shape[-1]
    KO = W // 128
    outs = []
    for core in range(NCORES):
        b, q = divmod(core, 4)
        t0 = q * 2048
        seg = np.zeros((17 * 128, W), ml_dtypes.bfloat16)
        if q > 0:
            seg[:] = o_full[b, t0 - 128:t0 + 2048]
        else:
            seg[128:] = o_full[b, 0:2048]
        outs.append(np.ascontiguousarray(seg.T.reshape(KO, 128, 17 * 128).transpose(1, 0, 2)))
    return outs


def build_Mgmlp():
    nc = bass.Bass("TRN2", target_bir_lowering=False)
    P = Prog(nc)
    TOK = 2048
    xT_d = P.dram("xT", [128, 8, TOK], F32, "ExternalInput")
    pm1_d = P.dram("pm1", [128, 2, 8], F32, "ExternalInput")
    win_d = P.dram("w_in", [128, 8, 4096], F32, "ExternalInput")
    rows_d = P.dram("rows", [128, 2, 2048], F32, "ExternalInput")
    wsT_d = P.dram("wsT", [128, 8, 128], F32, "ExternalInput")
    mask_d = P.dram("mask", [128, 128], F32, "ExternalInput")
    bs_d = P.dram("bs", [1, 1024], F32, "ExternalInput")
    out_d = P.dram("oT", [128, 16, TOK], BF16, "ExternalOutput")

    wu = P.sb("wu", [128, 8, 2048], BF16)
    wv = P.sb("wv", [128, 8, 2048], BF16)
    rows = P.sb("rows_sb", [128, 2, 2048])
    pm1 = P.sb("pm1_sb", [128, 2, 8])
    wsT = P.sb("wsT_sb", [128, 8, 128])
    mask = P.sb("mask_sb", [128, 128])
    wsTm = P.sb("wsTm", [128, 8, 128], BF16)
    bs = P.sb("bs_sb", [1, 1024])
    ones1 = P.sb("ones1", [1, 128])
    eps = P.sb("eps", [128, 1])
    xs = [P.sb("xs%d" % i, [128, 512]) for i in range(3)]
    hT = P.sb("hT", [128, 8, 512], BF16)
    uT = P.sb("uT", [128, 16, 512])
    vsb = [P.sb("vsb%d" % i, [128, 2048]) for i in range(2)]
    vln = [P.sb("vln%d" % i, [128, 2048], BF16) for i in range(2)]
    uvb = [P.sb("uvb%d" % i, [128, 16, 512], BF16) for i in range(2)]
    stats = P.sb("stats", [128, 4, 6])
    mv = P.sb("mv", [128, 2])
    rstd = P.sb("rstd", [128, 1])
    pu = P.ps("pu", [128, 2, 512])
    pv = P.ps("pv", [128, 2, 512])
    psp = P.ps("psp", [128, 16, 128])

    P.dma("sp", pm1[:], pm1_d)
    P.dma("sp", wsT[:], wsT_d)
    P.dma("sp", mask[:], mask_d)
    P.dma("sp", bs[:], bs_d)
    P.dma("sp", rows[:], rows_d)
    P.memset("dve", ones1[:], 1.0)
    P.memset("dve", eps[:], LN_EPS)
    P.ts("dve", pm1[:, 0, :], pm1[:, 0, :], 1.0, None, ALU.add)
    for g in range(8):
        P.tt("dve", wsTm[:, g, :], wsT[:, g, :], mask[:], ALU.mult)
    for kc in range(8):
        P.dma("pool", wu[:, kc, :], win_d[:, kc, 0:2048])
    for kc in range(8):
        P.dma("pool", wv[:, kc, :], win_d[:, kc, 2048:4096])

    xi = 0
    vi = 0
    for blk in range(TOK // 512):
        t0 = blk * 512
        for kc in range(8):
            xb = xs[xi % 3]
            xi += 1
            P.dma("sp", xb[:], xT_d[:, kc, t0:t0 + 512])
            P.act(hT[:, kc, :], xb[:], AF.Identity, bias=pm1[:, 1, kc:kc + 1], scale=pm1[:, 0, kc:kc + 1])
        for uc in range(16):
            pt = pu[:, uc % 2, :]
            for kc in range(8):
                P.mm(pt, wu[:, kc, uc * 128:(uc + 1) * 128], hT[:, kc, :], start=(kc == 0), stop=(kc == 7))
            P.act(uT[:, uc, :], pt, AF.Gelu)
        ob = uvb[blk % 2]
        for ch in range(4):
            tok = slice(ch * 128, (ch + 1) * 128)
            vb = vsb[vi % 2]
            vl = vln[vi % 2]
            vi += 1
            for half in range(2):
                for q in range(2):
                    c0 = (half * 2 + q) * 512
                    for kc in range(8):
                        P.mm(pv[:, q, :], hT[:, kc, tok], wv[:, kc, c0:c0 + 512], start=(kc == 0), stop=(kc == 7))
                P.act(vb[:, half * 1024:(half + 1) * 1024], pv[:].rearrange("p a b -> p (a b)"), AF.Gelu)
            for q in range(4):
                P.generic("dve", "bn_stats", (stats[:, q, :], vb[:, q * 512:(q + 1) * 512]),
                          [vb[:, q * 512:(q + 1) * 512]], [stats[:, q, :]])
            P.generic("dve", "bn_aggr", (mv[:], stats[:].rearrange("p a b -> p (a b)")), [stats[:]], [mv[:]])
            P.act(rstd[:], mv[:, 1:2], AF.Sqrt, bias=eps[:])
            P.generic("dve", "reciprocal", (rstd[:], rstd[:]), [rstd[:]], [rstd[:]])
            P.ts("dve", vb[:], vb[:], mv[:, 0:1], rstd[:], ALU.subtract, ALU.mult)
            P.tt("pool", vb[:], vb[:], rows[:, 0, :], ALU.mult)
            P.tt("pool", vl[:], vb[:], rows[:, 1, :], ALU.add)
            for cc in range(16):
                g = cc // 2
                P.mm(psp[:, cc, :], vl[:, cc * 128:(cc + 1) * 128], wsTm[:, g, :], start=True, stop=False, inc=False)
                P.mm(psp[:, cc, :], ones1[0:1, :], bs[0:1, g * 128:(g + 1) * 128], start=False, stop=True,
                     inc=(cc % 4 == 3))
            P.tt("dve", ob[:, :, tok], psp[:], uT[:, :, tok], ALU.mult)
        P.dma("sp", out_d[:, :, t0:t0 + 512], ob[:], is_output=True)
    P.emit()
    return nc


def xT_layout(xseg):
    T = xseg.shape[0]
    return np.ascontiguousarray(np.asarray(xseg).T.reshape(8, 128, T).transpose(1, 0, 2))


def run_Mgmlp(x_full, mod, inp):
    if "Mgmlp" not in _CACHE:
        _CACHE["Mgmlp"] = build_Mgmlp()
    nc = _CACHE["Mgmlp"]
    layer = 2
    win = wtile(inp["gmlp_w_in"])
    rows = np.stack([inp["gmlp_ln_g"], inp["gmlp_ln_b"]], 0)
    rows = np.ascontiguousarray(np.broadcast_to(rows[None], (128, 2, 2048))).astype(np.float32)
    wsT = np.ascontiguousarray(np.asarray(inp["gmlp_w_s"]).transpose(2, 0, 1))
    idx = np.arange(128)
    mask = (idx[:, None] <= idx[None, :]).astype(np.float32)
    bs = np.asarray(inp["gmlp_b_s"]).reshape(1, 1024).astype(np.float32)
    in_maps = []
    for core in range(NCORES):
        b, q = divmod(core, 4)
        m = mod[b, layer]
        sh1, sc1 = m[0:1024], m[1024:2048]
        pm1 = np.ascontiguousarray(np.stack([pm(sc1), pm(sh1)], 1)).astype(np.float32)
        in_maps.append({"xT": xT_layout(x_full[b, q * 2048:(q + 1) * 2048]), "pm1": pm1, "w_in": win,
                        "rows": rows, "wsT": wsT, "mask": mask, "bs": bs})
    res = run_bass_kernel_spmd(nc, in_maps, core_ids=list(range(NCORES)))
    oTs = [np.asarray(res.results[c]["oT"]) for c in range(NCORES)]
    outs = []
    for core in range(NCORES):
        b, q = divmod(core, 4)
        sh = np.zeros((128, 16, 128), ml_dtypes.bfloat16) if q == 0 else oTs[core - 1][:, :, -128:]
        outs.append(np.ascontiguousarray(np.concatenate([sh, oTs[core]], axis=2)))
    return outs


def build_Mret():
    nc = bass.Bass("TRN2", target_bir_lowering=False)
    P = Prog(nc)
    xT_d = P.dram("xT", [128, 8, S], F32, "ExternalInput")
    pm1_d = P.dram("pm1", [128, 2, 8], F32, "ExternalInput")
    wq_d = P.dram("wq", [128, 8, 256], F32, "ExternalInput")
    wk_d = P.dram("wk", [128, 8, 256], F32, "ExternalInput")
    wv_d = P.dram("wv", [128, 8, 512], F32, "ExternalInput")
    wg_d = P.dram("wg", [128, 8, 512], F32, "ExternalInput")
    cos_d = P.dram("cosT", [128, S], F32, "ExternalInput")
    sin_d = P.dram("sinT", [128, S], F32, "ExternalInput")
    xi_d = P.dram("xi_row", [128, 512], F32, "ExternalInput")
    dm_d = P.dram("dmaskT", [128, 128], F32, "ExternalInput")
    col_d = P.dram("cols", [128, 2], F32, "ExternalInput")
    id_d = P.dram("ident", [128, 128], BF16, "ExternalInput")
    out_d = P.dram("o", [S, 512], BF16, "ExternalOutput")

    wq = P.sb("wq_sb", [128, 8, 256], BF16)
    wk = P.sb("wk_sb", [128, 8, 256], BF16)
    wv = P.sb("wv_sb", [128, 8, 512], BF16)
    wg = P.sb("wg_sb", [128, 8, 512], BF16)
    pm1 = P.sb("pm1_sb", [128, 2, 8])
    xi_row = P.sb("xi_sb", [128, 512])
    dmT = P.sb("dm_sb", [128, 128])
    cols = P.sb("cols_sb", [128, 2])
    ident = P.sb("ident_sb", [128, 128], BF16)
    eps = P.sb("eps", [128, 1])
    xs = [P.sb("xs%d" % i, [128, 512]) for i in range(3)]
    hT = P.sb("hT", [128, 8, 512], BF16)
    cs = [P.sb("cs%d" % i, [128, 2, 512]) for i in range(2)]
    tmp = [P.sb("tmp%d" % i, [128, 512]) for i in range(4)]
    qT = P.sb("qT", [128, 2, 512], BF16)
    qxT = P.sb("qxT", [128, 2, 512], BF16)
    kT = P.sb("kT", [128, 2, 512], BF16)
    vsb = [P.sb("vsb%d" % i, [128, 512], BF16) for i in range(2)]
    sg = [P.sb("sg%d" % i, [128, 512]) for i in range(2)]
    kz = [P.sb("kz%d" % i, [128, 256], BF16) for i in range(2)]
    sT = [P.sb("sT%d" % i, [128, 128], BF16) for i in range(2)]
    on = [P.sb("on%d" % i, [128, 512]) for i in range(2)]
    ob = [P.sb("ob%d" % i, [128, 512], BF16) for i in range(2)]
    state = P.sb("state", [128, 2, 512])
    state_bf = P.sb("state_bf", [128, 2, 512], BF16)
    stats = P.sb("stats", [128, 6])
    mv = P.sb("mv", [128, 2])
    rstd = P.sb("rstd", [128, 1])
    pq = P.ps("pq", [128, 2, 512])
    pvg = P.ps("pvg", [128, 512])
    pS = P.ps("pS", [128, 512])
    pkt = P.ps("pkt", [128, 1024], BF16)
    po = P.ps("po", [128, 512])
    pst = P.ps("pst", [128, 2, 512])

    for dst, src in ((pm1, pm1_d), (xi_row, xi_d), (dmT, dm_d), (cols, col_d), (ident, id_d)):
        P.dma("sp", dst[:], src)
    P.memset("dve", eps[:], 1e-6)
    P.memset("dve", state[:], 0.0)
    P.memset("pool", state_bf[:], 0.0)
    P.ts("dve", pm1[:, 0, :], pm1[:, 0, :], 1.0, None, ALU.add)
    for dst, src in ((wq, wq_d), (wk, wk_d), (wv, wv_d), (wg, wg_d)):
        P.dma("pool", dst[:], src)

    xi_ = 0
    ci = 0
    for blk in range(S // 512):
        t0 = blk * 512
        cb = cs[blk % 2]
        P.dma("sp", cb[:, 0, :], cos_d[:, t0:t0 + 512])
        P.dma("sp", cb[:, 1, :], sin_d[:, t0:t0 + 512])
        for kc in range(8):
            xb = xs[xi_ % 3]
            xi_ += 1
            P.dma("sp", xb[:], xT_d[:, kc, t0:t0 + 512])
            P.act(hT[:, kc, :], xb[:], AF.Identity, bias=pm1[:, 1, kc:kc + 1], scale=pm1[:, 0, kc:kc + 1])
        for which, w_sb in (("q", wq), ("k", wk)):
            for dkc in range(2):
                for kc in range(8):
                    P.mm(pq[:, dkc, :], w_sb[:, kc, dkc * 128:(dkc + 1) * 128], hT[:, kc, :], start=(kc == 0), stop=(kc == 7))
            P.tt("dve", tmp[0][:], pq[:, 0, :], cb[:, 0, :], ALU.mult)
            P.tt("dve", tmp[1][:], pq[:, 1, :], cb[:, 1, :], ALU.mult)
            P.tt("dve", tmp[2][:], pq[:, 0, :], cb[:, 1, :], ALU.mult)
            P.tt("dve", tmp[3][:], pq[:, 1, :], cb[:, 0, :], ALU.mult)
            P.tt("pool", tmp[0][:], tmp[0][:], tmp[1][:], ALU.subtract)
            P.tt("pool", tmp[2][:], tmp[2][:], tmp[3][:], ALU.add)
            dst = qT if which == "q" else kT
            P.copy("act", dst[:, 0, :], tmp[0][:])
            P.copy("act", dst[:, 1, :], tmp[2][:])
            if which == "q":
                P.tt("pool", qxT[:, 0, :], tmp[0][:], xi_row[:], ALU.mult)
                P.tt("pool", qxT[:, 1, :], tmp[2][:], xi_row[:], ALU.mult)
        for ch in range(4):
            tok = slice(ch * 128, (ch + 1) * 128)
            vb, sgb, kzb, sTb, onb, obb = vsb[ci % 2], sg[ci % 2], kz[ci % 2], sT[ci % 2], on[ci % 2], ob[ci % 2]
            ci += 1
            for kc in range(8):
                P.mm(pvg[:], hT[:, kc, tok], wv[:, kc, :], start=(kc == 0), stop=(kc == 7))
            P.copy("act", vb[:], pvg[:])
            for kc in range(8):
                P.mm(pvg[:], hT[:, kc, tok], wg[:, kc, :], start=(kc == 0), stop=(kc == 7))
            P.act(sgb[:], pvg[:], AF.Silu)
            for dkc in range(2):
                P.transpose(pkt[:, dkc * 128:(dkc + 1) * 128], kT[:, dkc, tok], ident[:], inc=(dkc == 1))
            P.ts("dve", kzb[:], pkt[:, 0:256], cols[:, 0:1], None, ALU.mult)
            for dkc in range(2):
                P.mm(pS[:, 0:128], kT[:, dkc, tok], qT[:, dkc, tok], start=(dkc == 0), stop=(dkc == 1))
            P.tt("dve", sTb[:], pS[:, 0:128], dmT[:], ALU.mult)
            P.mm(po[:], sTb[:], vb[:], start=True, stop=False, inc=False)
            P.mm(po[:], qxT[:, 0, tok], state_bf[:, 0, :], start=False, stop=False, inc=False)
            P.mm(po[:], qxT[:, 1, tok], state_bf[:, 1, :], start=False, stop=True)
            for dkc in range(2):
                P.mm(pst[:, dkc, :], kzb[:, dkc * 128:(dkc + 1) * 128], vb[:], start=True, stop=True)
            P.stt("dve", state[:].rearrange("p a b -> p (a b)"), state[:].rearrange("p a b -> p (a b)"), cols[:, 1:2],
                  pst[:].rearrange("p a b -> p (a b)"), ALU.mult, ALU.add)
            P.copy("act", state_bf[:].rearrange("p a b -> p (a b)"), state[:].rearrange("p a b -> p (a b)"))
            P.generic("dve", "bn_stats", (stats[:], po[:]), [po[:]], [stats[:]])
            P.generic("dve", "bn_aggr", (mv[:], stats[:]), [stats[:]], [mv[:]])
            P.act(rstd[:], mv[:, 1:2], AF.Sqrt, bias=eps[:])
            P.generic("dve", "reciprocal", (rstd[:], rstd[:]), [rstd[:]], [rstd[:]])
            P.ts("dve", onb[:], po[:], mv[:, 0:1], rstd[:], ALU.subtract, ALU.mult)
            P.tt("pool", obb[:], onb[:], sgb[:], ALU.mult)
            P.dma("sp", out_d[t0 + ch * 128:t0 + (ch + 1) * 128, :], obb[:], is_output=True)
    P.emit()
    return nc


def run_Mret(x_full, mod, inp):
    if "Mret" not in _CACHE:
        _CACHE["Mret"] = build_Mret()
    nc = _CACHE["Mret"]
    layer = 1
    H, dk, dv, C = 4, 256, 512, 128
    w = np.asarray(inp["ret_w_in"])
    pos = np.arange(S, dtype=np.float32)
    inv_freq = (np.float32(10000.0) ** (-np.linspace(0.0, 1.0, dk // 2, dtype=np.float32))).astype(np.float32)
    ang = (pos[:, None] * inv_freq[None, :]).astype(np.float32)
    cosT = np.ascontiguousarray(np.cos(ang).T.astype(np.float32))
    sinT = np.ascontiguousarray(np.sin(ang).T.astype(np.float32))
    ident = np.eye(128, dtype=np.float32).astype(ml_dtypes.bfloat16)
    idx = np.arange(C, dtype=np.float32)
    in_maps = []
    xTs = [xT_layout(x_full[b]) for b in range(2)]
    for core in range(NCORES):
        b, h = divmod(core, 4)
        m = mod[b, layer]
        sh1, sc1 = m[0:1024], m[1024:2048]
        pm1 = np.ascontiguousarray(np.stack([pm(sc1), pm(sh1)], 1)).astype(np.float32)
        lg = np.log(np.float32(1.0) - np.power(np.float32(2.0), np.float32(-5.0 - h))).astype(np.float32)
        rel = idx[None, :] - idx[:, None]
        dmT = np.where(rel >= 0, np.exp(np.maximum(rel, 0.0) * lg), 0.0).astype(np.float32) * np.float32(dk ** -0.5)
        zeta = np.exp((C - 1.0 - idx) * lg).astype(np.float32) * np.float32(dk ** -0.5)
        xi = np.exp((idx + 1.0) * lg).astype(np.float32)
        gam = np.exp(np.float32(C) * lg).astype(np.float32)
        cols = np.stack([zeta, np.full(128, gam, np.float32)], 1).astype(np.float32)
        xi_row = np.ascontiguousarray(np.broadcast_to(np.tile(xi, 4)[None], (128, 512))).astype(np.float32)
        in_maps.append({
            "xT": xTs[b], "pm1": pm1,
            "wq": wtile(w[:, h * dk:(h + 1) * dk]), "wk": wtile(w[:, H * dk + h * dk:H * dk + (h + 1) * dk]),
            "wv": wtile(w[:, 2 * H * dk + h * dv:2 * H * dk + (h + 1) * dv]),
            "wg": wtile(w[:, 2 * H * dk + H * dv + h * dv:2 * H * dk + H * dv + (h + 1) * dv]),
            "cosT": cosT, "sinT": sinT, "xi_row": xi_row, "dmaskT": np.ascontiguousarray(dmT), "cols": cols, "ident": ident,
        })
    res = run_bass_kernel_spmd(nc, in_maps, core_ids=list(range(NCORES)))
    o_full = np.empty((2, S, H * dv), ml_dtypes.bfloat16)
    for core in range(NCORES):
        b, h = divmod(core, 4)
        o_full[b, :, h * dv:(h + 1) * dv] = np.asarray(res.results[core]["o"])
    return oT_from_tokenmajor(o_full)


def build_Msb():
    nc = bass.Bass("TRN2", target_bir_lowering=False)
    P = Prog(nc)
    NQB = S // 128
    xT_d = P.dram("xT", [128, 8, S], F32, "ExternalInput")
    xTr_d = P.dram("xTr", [128, 8, S], F32, "ExternalInput")
    pm1_d = P.dram("pm1", [128, 2, 8], F32, "ExternalInput")
    wq_d = P.dram("wq", [128, 8, 256], F32, "ExternalInput")
    wk_d = P.dram("wk", [128, 8, 256], F32, "ExternalInput")
    wv_d = P.dram("wv", [128, 8, 256], F32, "ExternalInput")
    mneg_d = P.dram("mneg", [128, 128], BF16, "ExternalInput")
    id_d = P.dram("ident", [128, 128], BF16, "ExternalInput")
    out_d = P.dram("o", [128, NQB, 256], BF16, "ExternalOutput")

    qT = P.sb("qT_all", [128, 2, S], BF16)
    kT = P.sb("kT_all", [128, 2, S], BF16)
    v_all = P.sb("v_all", [128, NQB, 256], BF16)
    o_qb = [P.sb("o_qb%d" % i, [128, 256], BF16) for i in range(2)]
    wq = P.sb("wq_sb", [128, 8, 256], BF16)
    wk = P.sb("wk_sb", [128, 8, 256], BF16)
    wv = P.sb("wv_sb", [128, 8, 256], BF16)
    pm1 = P.sb("pm1_sb", [128, 2, 8])
    mneg = P.sb("mneg_sb", [128, 128], BF16)
    ident = P.sb("ident_sb", [128, 128], BF16)
    zeros = P.sb("zeros", [128, 512])
    xs = [P.sb("xs%d" % i, [128, 512]) for i in range(3)]
    hT = P.sb("hT", [128, 8, 512], BF16)
    hTr = P.sb("hTr", [128, 8, 512], BF16)
    NSET = 4
    gb = [P.sb("gb%d" % i, [128, 512]) for i in range(NSET)]
    Pb = [P.sb("Pb%d" % i, [128, 513]) for i in range(NSET)]
    Ab = [P.sb("Ab%d" % i, [128, 512], BF16) for i in range(NSET)]
    ATb = [P.sb("ATb%d" % i, [128, 512], BF16) for i in range(NSET)]
    pz = P.ps("pz", [128, 4, 512])
    pT = P.ps("pT", [128, 2, 1024], BF16)
    po = P.ps("po", [128, 2, 64])
    pp = pz

    for dst, src in ((pm1, pm1_d), (mneg, mneg_d), (ident, id_d)):
        P.dma("sp", dst[:], src)
    P.memset("dve", zeros[:], 0.0)
    P.ts("dve", pm1[:, 0, :], pm1[:, 0, :], 1.0, None, ALU.add)
    for dst, src in ((wq, wq_d), (wk, wk_d), (wv, wv_d)):
        P.dma("pool", dst[:], src)

    xi_ = 0
    for blk in range(S // 512):
        t0 = blk * 512
        for (src_d, dst) in ((xT_d, hT), (xTr_d, hTr)):
            for kc in range(8):
                xb = xs[xi_ % 3]
                xi_ += 1
                P.dma("sp", xb[:], src_d[:, kc, t0:t0 + 512])
                P.act(dst[:, kc, :], xb[:], AF.Identity, bias=pm1[:, 1, kc:kc + 1], scale=pm1[:, 0, kc:kc + 1])
        for pair in range(2):
            pq_ = pp[:, pair, :]
            for kc in range(8):
                P.mm(pq_, wq[:, kc, pair * 128:(pair + 1) * 128], hT[:, kc, :], start=(kc == 0), stop=(kc == 7))
            P.act(qT[:, pair, t0:t0 + 512], pq_, AF.Copy, scale=0.125)
            pk_ = pp[:, 2 + pair, :]
            for kc in range(8):
                P.mm(pk_, wk[:, kc, pair * 128:(pair + 1) * 128], hTr[:, kc, :], start=(kc == 0), stop=(kc == 7))
            P.copy("dve", kT[:, pair, t0:t0 + 512], pk_)
        for ch in range(4):
            pv_ = pz[:, ch % 4, 0:256]
            for kc in range(8):
                P.mm(pv_, hTr[:, kc, ch * 128:(ch + 1) * 128], wv[:, kc, :], start=(kc == 0), stop=(kc == 7))
            P.copy("act" if ch % 2 else "dve", v_all[:, blk * 4 + ch, :], pv_)

    items = []
    for qb in range(NQB):
        nseg = (qb + 1 + 3) // 4
        for head in range(4):
            for sg_ in range(nseg):
                items.append((qb, head, sg_, nseg))
    prevP = {}

    def geom(it):
        qb, head, sg_, nseg = items[it]
        nb = qb + 1
        kb0 = sg_ * 4
        nk = min(4, nb - kb0)
        pair, par = divmod(head, 2)
        return qb, head, sg_, nseg, kb0, nk, nk * 128, pair, slice(par * 64, par * 64 + 64), it % NSET

    def stA(it):
        qb, head, sg_, nseg, kb0, nk, n, pair, prt, b4 = geom(it)
        t0 = qb * 128
        ks = (NQB - 1 - qb) * 128 + kb0 * 128
        zt = pz[:, b4, :]
        P.mm(zt[:, 0:n], qT[prt, pair, t0:t0 + 128], kT[prt, pair, ks:ks + n], start=True, stop=(sg_ != 0))
        if sg_ == 0:
            P.mm(zt[:, 0:128], ident[:], mneg[:], start=False, stop=True)

    def stB(it):
        qb, head, sg_, nseg, kb0, nk, n, pair, prt, b4 = geom(it)
        zt = pz[:, b4, :]
        g_, P_, A_ = gb[b4], Pb[b4], Ab[b4]
        P.act(g_[:, 0:n], zt[:, 0:n], AF.Sigmoid, scale=-1.0)
        if sg_ == 0:
            P.memset("pool", P_[:, 0:1], 1.0)
            P.generic("dve", "tensor_tensor_scan", (P_[:, 1:1 + n], g_[:, 0:n], zeros[:, 0:n], 1.0, ALU.mult, ALU.add),
                      [g_[:, 0:n], zeros[:, 0:n]], [P_[:, 1:1 + n]])
        else:
            pp_, pn = prevP[(qb, head)]
            P.copy("pool", P_[:, 0:1], pp_[:, pn:pn + 1])
            P.generic("dve", "tensor_tensor_scan", (P_[:, 1:1 + n], g_[:, 0:n], zeros[:, 0:n], pp_[:, pn:pn + 1], ALU.mult, ALU.add),
                      [g_[:, 0:n], zeros[:, 0:n], pp_[:, pn:pn + 1]], [P_[:, 1:1 + n]])
        prevP[(qb, head)] = (P_, n)
        P.tt("pool", A_[:, 0:n], P_[:, 0:n], P_[:, 1:1 + n], ALU.subtract)

    def stC(it):
        qb, head, sg_, nseg, kb0, nk, n, pair, prt, b4 = geom(it)
        A_, AT_ = Ab[b4], ATb[b4]
        ptb = pT[:, b4 % 2, (b4 // 2) * 512:(b4 // 2 + 1) * 512]
        for j in range(nk):
            P.transpose(ptb[:, j * 128:(j + 1) * 128], A_[:, j * 128:(j + 1) * 128], ident[:], inc=(j == nk - 1))
        P.copy("act" if (it % 2) else "dve", AT_[:, 0:n], ptb[:, 0:n])

    def stE(it):
        qb, head, sg_, nseg, kb0, nk, n, pair, prt, b4 = geom(it)
        AT_ = ATb[b4]
        pob = po[:, head % 2, :]
        oq = o_qb[qb % 2]
        for j in range(nk):
            rb = (NQB - 1 - qb) + kb0 + j
            first = (sg_ == 0 and j == 0)
            last = (sg_ == nseg - 1 and j == nk - 1)
            P.mm(pob, AT_[:, j * 128:(j + 1) * 128], v_all[:, rb, head * 64:(head + 1) * 64],
                 start=first, stop=last, inc=(last or j == nk - 1))
        if sg_ == nseg - 1:
            P.copy("act", oq[:, head * 64:(head + 1) * 64], pob)
            if head == 3:
                P.dma("sp", out_d[:, qb, :], oq[:], is_output=True)

    D1, D2 = 2, 3
    NI = len(items)
    for st in range(NI + D2):
        if st < NI:
            stA(st)
            stB(st)
        if 0 <= st - D1 < NI:
            stC(st - D1)
        if 0 <= st - D2 < NI:
            stE(st - D2)
    P.emit()
    return nc


def run_Msb(x_full, mod, inp):
    if "Msb" not in _CACHE:
        _CACHE["Msb"] = build_Msb()
    nc = _CACHE["Msb"]
    layer = 3
    w = np.asarray(inp["sb_w_in"])
    ident = np.eye(128, dtype=np.float32).astype(ml_dtypes.bfloat16)
    idx = np.arange(128)
    mneg = np.where(idx[:, None] + idx[None, :] <= 127, -30000.0, 0.0).astype(np.float32).astype(ml_dtypes.bfloat16)
    xTs = [xT_layout(x_full[b]) for b in range(2)]
    xTrs = [np.ascontiguousarray(t[:, :, ::-1]) for t in xTs]
    in_maps = []
    for core in range(NCORES):
        b, hg = divmod(core, 4)
        m = mod[b, layer]
        sh1, sc1 = m[0:1024], m[1024:2048]
        pm1 = np.ascontiguousarray(np.stack([pm(sc1), pm(sh1)], 1)).astype(np.float32)
        in_maps.append({
            "xT": xTs[b], "xTr": xTrs[b], "pm1": pm1,
            "wq": wtile(w[:, hg * 256:(hg + 1) * 256]),
            "wk": wtile(w[:, 1024 + hg * 256:1024 + (hg + 1) * 256]),
            "wv": wtile(w[:, 2048 + hg * 256:2048 + (hg + 1) * 256]),
            "mneg": mneg, "ident": ident,
        })
    res = run_bass_kernel_spmd(nc, in_maps, core_ids=list(range(NCORES)))
    o_full = np.empty((2, S, 1024), ml_dtypes.bfloat16)
    for core in range(NCORES):
        b, hg = divmod(core, 4)
        o = np.asarray(res.results[core]["o"])
        o_full[b, :, hg * 256:(hg + 1) * 256] = o.transpose(1, 0, 2).reshape(S, 256)
    return oT_from_tokenmajor(o_full)


def build_Mgdn(nblk=None):
    nc = bass.Bass("TRN2", target_bir_lowering=False)
    P = Prog(nc)
    NTL = S // 128
    xT_d = P.dram("xT", [128, 8, S], F32, "ExternalInput")
    pm1_d = P.dram("pm1", [128, 2, 8], F32, "ExternalInput")
    w_d = P.dram("w", [128, 8, 1024], F32, "ExternalInput")
    wab_d = P.dram("wab", [128, 8, 4], F32, "ExternalInput")
    cw_d = P.dram("cw", [128, 6, 4], F32, "ExternalInput")
    cst_d = P.dram("cst", [128, 8, 128], F32, "ExternalInput")
    hs_d = P.dram("hs", [128, 4], F32, "ExternalInput")
    nw_d = P.dram("nw", [128, 128], F32, "ExternalInput")
    idb_d = P.dram("identb", [128, 128], BF16, "ExternalInput")
    out_d = P.dram("o", [128, NTL, 256], BF16, "ExternalOutput")

    w = P.sb("w_sb", [128, 8, 1024], BF16)
    wab = P.sb("wab_sb", [128, 8, 4])
    pm1 = P.sb("pm1_sb", [128, 2, 8])
    cw = P.sb("cw_sb", [128, 6, 4])
    cst = P.sb("cst_sb", [128, 8, 128])
    hs = P.sb("hs_sb", [128, 4])
    nw = P.sb("nw_sb", [128, 128])
    identb = P.sb("identb_sb", [128, 128], BF16)
    onesb = P.sb("onesb", [128, 128], BF16)
    negA = P.sb("negA", [128, 2])
    eps6 = P.sb("eps6", [128, 1])
    eps6q = P.sb("eps6q", [128, 1])
    one1 = P.sb("one1", [128, 1])
    ident_f, onesBD, triBD, blk0, blk1, posmask, posmaskT, negstrict = [cst[:, i, :] for i in range(8)]

    xs = [P.sb("xs%d" % i, [128, 512]) for i in range(3)]
    hT = P.sb("hT", [128, 8, 512], BF16)
    hTf = P.sb("hTf", [128, 8, 512])
    xc = [P.sb("xc%d" % i, [128, 515]) for i in range(6)]
    cv = [P.sb("cv%d" % i, [128, 512]) for i in range(2)]
    sq = [P.sb("sq%d" % i, [128, 512], BF16) for i in range(2)]
    rn = [P.sb("rn%d" % i, [128, 512]) for i in range(2)]
    qhT = [P.sb("qhT%d" % i, [128, 512], BF16) for i in range(2)]
    khT = [P.sb("khT%d" % i, [128, 512], BF16) for i in range(2)]
    vcT = [P.sb("vcT%d" % i, [128, 512], BF16) for i in range(2)]
    ktok = [P.sb("ktok%d" % i, [128, 4, 128]) for i in range(2)]
    vtok = [P.sb("vtok%d" % i, [128, 4, 128]) for i in range(2)]
    nz = [P.sb("nz%d" % i, [128, 4, 128]) for i in range(2)]
    gat = {}
    for nm in ("g", "beta", "gc", "glo", "glb0", "glb1", "egc", "edl", "egl0", "egl1", "bg", "e1"):
        gat[nm] = [P.sb("%s%d" % (nm, i), [128, 4]) for i in range(2)]
    Dg = [P.sb("Dg%d" % i, [128, 128]) for i in range(4)]
    dec = [P.sb("dec%d" % i, [128, 128]) for i in range(4)]
    decT = [P.sb("decT%d" % i, [128, 128]) for i in range(4)]
    tmpE = [P.sb("tmpE%d" % i, [128, 128]) for i in range(4)]
    Yb = [[P.sb("Y%d_%d" % (i, k), [128, 128]) for k in range(2)] for i in range(4)]
    Zb = [[P.sb("Z%d_%d" % (i, k), [128, 128]) for k in range(2)] for i in range(4)]
    Ttb = [[P.sb("Tt%d_%d" % (i, k), [128, 128]) for k in range(2)] for i in range(4)]
    Tmb = [[P.sb("Tm%d_%d" % (i, k), [128, 128]) for k in range(2)] for i in range(4)]
    Ttbf = [P.sb("Ttbf%d" % i, [128, 128], BF16) for i in range(4)]
    vbt = [P.sb("vbt%d" % i, [128, 128], BF16) for i in range(4)]
    kbe = [P.sb("kbe%d" % i, [128, 128], BF16) for i in range(4)]
    kd = [P.sb("kd%d" % i, [128, 128], BF16) for i in range(4)]
    u_sb = [P.sb("u%d" % i, [128, 128]) for i in range(4)]
    wTA = [P.sb("wTA%d" % i, [128, 128], BF16) for i in range(4)]
    wTB = [P.sb("wTB%d" % i, [128, 128], BF16) for i in range(4)]
    qkT = [P.sb("qkT%d" % i, [128, 128], BF16) for i in range(4)]
    vn = [P.sb("vn%d" % i, [128, 128], BF16) for i in range(2)]
    o1 = [P.sb("o1_%d" % i, [128, 128]) for i in range(2)]
    osum = [P.sb("osum%d" % i, [128, 128]) for i in range(2)]
    osq = P.sb("osq", [128, 128])
    ssq = P.sb("ssq", [128, 1])
    rinv = P.sb("rinv", [128, 1])
    St = [P.sb("S%d" % i, [128, 128]) for i in range(2)]
    Sbf = [P.sb("Sbf%d" % i, [128, 128], BF16) for i in range(2)]
    o_all = P.sb("o_all", [128, NTL, 256], BF16)

    PB = P.ps("PB", [128, 4, 512])
    PTr = P.ps("PTr", [128, 1024], BF16)
    PG = P.ps("PG", [128, 512])
    PSC = P.ps("PSC", [128, 2, 512])

    def slot(i):
        return PB[:, i // 4, (i % 4) * 128:(i % 4 + 1) * 128]

    for dst, src in ((pm1, pm1_d), (wab, wab_d), (cw, cw_d), (cst, cst_d), (hs, hs_d), (nw, nw_d), (identb, idb_d)):
        P.dma("sp", dst[:], src)
    for kc in range(8):
        P.dma("pool", w[:, kc, :], w_d[:, kc, :])
    P.memset("dve", onesb[:], 1.0)
    P.memset("dve", eps6[:], 1e-6)
    P.memset("dve", eps6q[:], 128e-6)
    P.memset("dve", one1[:], 1.0)
    for i in range(6):
        P.memset("pool", xc[i][:, 0:3], 0.0)
    for i in range(4):
        P.memset("pool", wTA[i][:], 0.0)
        P.memset("pool", wTB[i][:], 0.0)
    for h2 in range(2):
        P.memset("dve", St[h2][:], 0.0)
        P.memset("dve", Sbf[h2][:], 0.0)
    P.ts("dve", pm1[:, 0, :], pm1[:, 0, :], 1.0, None, ALU.add)
    if nblk:
        P.memset("pool", o_all[:], 0.0)
    P.act(negA[:], hs[:, 2:4], AF.Exp)
    P.ts("dve", negA[:], negA[:], -1.0, None, ALU.mult)

    xi_ = 0
    for blk in range(nblk or (S // 512)):
        t0 = blk * 512
        for kc in range(8):
            xb = xs[xi_ % 3]
            xi_ += 1
            P.dma("sp", xb[:], xT_d[:, kc, t0:t0 + 512])
            P.act(hT[:, kc, :], xb[:], AF.Identity, bias=pm1[:, 1, kc:kc + 1], scale=pm1[:, 0, kc:kc + 1])
            P.ts("dve", hTf[:, kc, :], xb[:], pm1[:, 0, kc:kc + 1], pm1[:, 1, kc:kc + 1], ALU.mult, ALU.add)
        for tl in range(4):
            for kc in range(8):
                P.mm(PG[:, tl * 4:tl * 4 + 4], hTf[:, kc, tl * 128:(tl + 1) * 128], wab[:, kc, :], start=(kc == 0), stop=(kc == 7))
        pgv = PG[:, 0:16].rearrange("p (t c) -> p t c", c=4)
        for h2 in range(2):
            G = {k: v[h2] for k, v in gat.items()}
            P.act(G["e1"][:], pgv[:, :, h2], AF.Exp, bias=hs[:, h2:h2 + 1])
            P.act(G["e1"][:], G["e1"][:], AF.Ln, bias=one1[:])
            P.ts("dve", G["g"][:], G["e1"][:], negA[:, h2:h2 + 1], None, ALU.mult)
            P.act(G["beta"][:], pgv[:, :, 2 + h2], AF.Sigmoid)
            for nm, cm, off in (("gc", triBD, 16), ("glo", onesBD, 20), ("glb0", blk0, 24), ("glb1", blk1, 28)):
                o_ = off + h2 * 16
                P.mm(PG[:, 32 + o_ - 16:32 + o_ - 12], cm, G["g"][:], start=True, stop=True)
                P.copy("dve", G[nm][:], PG[:, 32 + o_ - 16:32 + o_ - 12])
            P.act(G["egc"][:], G["gc"][:], AF.Exp)
            P.tt("dve", G["edl"][:], G["glo"][:], G["gc"][:], ALU.subtract)
            P.act(G["edl"][:], G["edl"][:], AF.Exp)
            P.act(G["egl0"][:], G["glb0"][:], AF.Exp)
            P.act(G["egl1"][:], G["glb1"][:], AF.Exp)
            P.tt("dve", G["bg"][:], G["beta"][:], G["egc"][:], ALU.mult)
        for h2 in range(2):
            for i in range(3):
                ci = h2 * 3 + i
                pp_ = PB[:, ci % 4, :]
                c0 = h2 * 384 + i * 128
                for kc in range(8):
                    P.mm(pp_, w[:, kc, c0:c0 + 128], hT[:, kc, :], start=(kc == 0), stop=(kc == 7))
                xcb = xc[ci]
                if blk > 0:
                    P.copy("pool", xcb[:, 0:3], xcb[:, 512:515])
                P.copy("act", xcb[:, 3:515], pp_)
                cvb = cv[ci % 2]
                P.ts("dve", cvb[:], xcb[:, 0:512], cw[:, ci, 0:1], None, ALU.mult)
                for tap in range(1, 4):
                    P.stt("dve", cvb[:], xcb[:, tap:tap + 512], cw[:, ci, tap:tap + 1], cvb[:], ALU.mult, ALU.add)
                if i == 2:
                    P.act(vcT[h2][:], cvb[:], AF.Silu)
                else:
                    P.act(cvb[:], cvb[:], AF.Silu)
                    sqb, rnb = sq[ci % 2], rn[ci % 2]
                    P.act(sqb[:], cvb[:], AF.Square)
                    pn = PB[:, (ci + 2) % 4, :]
                    P.mm(pn, onesb[:], sqb[:], start=True, stop=True)
                    if i == 0:
                        P.act(rnb[:], pn, AF.Sqrt, bias=eps6q[:], scale=128.0)
                    else:
                        P.act(rnb[:], pn, AF.Sqrt, bias=eps6[:])
                    P.generic("dve", "reciprocal", (rnb[:], rnb[:]), [rnb[:]], [rnb[:]])
                    P.tt("dve", (qhT if i == 0 else khT)[h2][:], cvb[:], rnb[:], ALU.mult)
            for tl in range(4):
                tok = slice(tl * 128, (tl + 1) * 128)
                P.transpose(PTr[:, 0:128], khT[h2][:, tok], identb[:], inc=False)
                P.transpose(PTr[:, 128:256], vcT[h2][:, tok], identb[:])
                P.copy("act", ktok[h2][:, tl, :], PTr[:, 0:128])
                P.copy("dve", vtok[h2][:, tl, :], PTr[:, 128:256])
                pz_ = PB[:, tl % 4, 0:128]
                for kc in range(8):
                    P.mm(pz_, hT[:, kc, tok], w[:, kc, 768 + h2 * 128:768 + (h2 + 1) * 128], start=(kc == 0), stop=(kc == 7))
                P.act(nz[h2][:, tl, :], pz_, AF.Silu)
                P.tt("pool", nz[h2][:, tl, :], nz[h2][:, tl, :], nw[:], ALU.mult)

        for h2 in range(2):
            G = {k: v[h2] for k, v in gat.items()}
            for tl in range(4):
                tok = slice(tl * 128, (tl + 1) * 128)
                gcc = G["gc"][:, tl:tl + 1]
                P.ts("dve", Dg[tl][:], ident_f, gcc, None, ALU.mult)
                P.mm(slot(tl), onesBD, Dg[tl][:], start=True, stop=True)
                P.stt("dve", tmpE[tl][:], slot(tl), gcc, posmask, ALU.subtract, ALU.add)
                P.act(dec[tl][:], tmpE[tl][:], AF.Exp, scale=-1.0)
                P.stt("dve", tmpE[tl][:], slot(tl), gcc, posmaskT, ALU.subtract, ALU.subtract)
                P.act(decT[tl][:], tmpE[tl][:], AF.Exp)
                P.mm(slot(4 + tl), khT[h2][:, tok], khT[h2][:, tok], start=True, stop=True)
                P.tt("dve", tmpE[tl][:], slot(4 + tl), dec[tl][:], ALU.mult)
                P.stt("dve", Zb[tl][0][:], tmpE[tl][:], G["beta"][:, tl:tl + 1], negstrict, ALU.mult, ALU.mult)
                P.mm(slot(8 + tl), khT[h2][:, tok], qhT[h2][:, tok], start=True, stop=True)
                P.tt("dve", qkT[tl][:], slot(8 + tl), decT[tl][:], ALU.mult)
                P.ts("dve", vbt[tl][:], vtok[h2][:, tl, :], G["beta"][:, tl:tl + 1], None, ALU.mult)
                P.act(kbe[tl][:], ktok[h2][:, tl, :], AF.Identity, scale=G["bg"][:, tl:tl + 1])
                P.act(kd[tl][:], ktok[h2][:, tl, :], AF.Identity, scale=G["edl"][:, tl:tl + 1])
                P.mm(slot(12 + tl), Zb[tl][0][:], ident_f, start=True, stop=True)
                P.copy("act", Yb[tl][0][:], slot(12 + tl))
                P.tt("dve", Ttb[tl][0][:], slot(12 + tl), ident_f, ALU.add)
                P.tt("pool", Tmb[tl][0][:], Zb[tl][0][:], ident_f, ALU.add)
            for st in range(1, 6):
                a_, b_ = (st - 1) % 2, st % 2
                last = (st == 5)
                for tl in range(4):
                    P.mm(slot(tl), Zb[tl][a_][:], Yb[tl][a_][:], start=True, stop=True)
                    if not last:
                        P.mm(slot(4 + tl), Yb[tl][a_][:], Zb[tl][a_][:], start=True, stop=True)
                for tl in range(4):
                    P.copy("act", Yb[tl][b_][:], slot(tl))
                    if not last:
                        P.copy("act", Zb[tl][b_][:], slot(4 + tl))
                for tl in range(4):
                    P.mm(slot(8 + tl), Tmb[tl][a_][:], Yb[tl][b_][:], start=True, stop=True)
                    if not last:
                        P.mm(slot(12 + tl), Ttb[tl][a_][:], Zb[tl][b_][:], start=True, stop=True)
                for tl in range(4):
                    if last:
                        P.tt("dve", Ttbf[tl][:], slot(8 + tl), Ttb[tl][a_][:], ALU.add)
                    else:
                        P.tt("dve", Ttb[tl][b_][:], slot(8 + tl), Ttb[tl][a_][:], ALU.add)
                        P.tt("dve", Tmb[tl][b_][:], slot(12 + tl), Tmb[tl][a_][:], ALU.add)
            for tl in range(4):
                P.mm(slot(tl), Ttbf[tl][:], vbt[tl][:], start=True, stop=True)
                P.mm(slot(4 + tl), kbe[tl][:], Ttbf[tl][:], start=True, stop=True)
                P.copy("act", u_sb[tl][:], slot(tl))
                P.copy("dve", wTA[tl][:, 0:64], slot(4 + tl)[:, 0:64])
                P.copy("act", wTB[tl][:, 64:128], slot(4 + tl)[:, 64:128])
            S_, Sb_ = St[h2], Sbf[h2]
            for tl in range(4):
                tok = slice(tl * 128, (tl + 1) * 128)
                gt = blk * 4 + tl
                vnb, o1b, osb = vn[tl % 2], o1[tl % 2], osum[tl % 2]
                for j in range(2):
                    pr = slice(j * 64, j * 64 + 64)
                    pws = PSC[:, j, 0:128]
                    po1 = PSC[:, j, 128:256]
                    psu = PSC[:, j, 256:384]
                    P.mm(pws, (wTA if j == 0 else wTB)[tl][:], Sb_[:], start=True, stop=True)
                    P.mm(po1, qhT[h2][:, tok], Sb_[:], start=True, stop=True)
                    P.tt("dve", vnb[pr, :], u_sb[tl][pr, :], pws[pr, :], ALU.subtract)
                    P.act(o1b[pr, :], po1[pr, :], AF.Identity, scale=G["egc"][pr, tl:tl + 1])
                    P.mm(psu, kd[tl][pr, :], vnb[pr, :], start=True, stop=True)
                    P.stt("dve", S_[:], S_[:], G["egl%d" % j][:, tl:tl + 1], psu, ALU.mult, ALU.add)
                    P.copy("act", Sb_[:], S_[:])
                po2 = PSC[:, 0, 384:512]
                P.mm(po2, qkT[tl][:], vnb[:], start=True, stop=True)
                P.tt("dve", osb[:], o1b[:], po2, ALU.add)
                P.tt("dve", osq[:], osb[:], osb[:], ALU.mult)
                P.generic("dve", "reduce_sum", (ssq[:], osq[:]), [osq[:]], [ssq[:]], axis=AX.X)
                P.act(rinv[:], ssq[:], AF.Sqrt, bias=eps6[:], scale=1.0 / 128.0)
                P.generic("dve", "reciprocal", (rinv[:], rinv[:]), [rinv[:]], [rinv[:]])
                P.stt("dve", o_all[:, gt, h2 * 128:(h2 + 1) * 128], osb[:], rinv[:], nz[h2][:, tl, :], ALU.mult, ALU.mult)
    for q4 in range(4):
        P.dma("sp", out_d[:, q4 * 16:(q4 + 1) * 16, :], o_all[:, q4 * 16:(q4 + 1) * 16, :], is_output=True)
    P.emit()
    return nc


def run_Mgdn(x_full, mod, inp):
    if "Mgdn" not in _CACHE:
        _CACHE["Mgdn"] = build_Mgdn()
    nc = _CACHE["Mgdn"]
    layer = 0
    wi = np.asarray(inp["gdn_w_in"])
    cwf = np.asarray(inp["gdn_conv_w"])
    p = np.arange(128)
    same = (p[:, None] // 64) == (p[None, :] // 64)
    ident = np.eye(128, dtype=np.float32)
    onesBD = same.astype(np.float32)
    triBD = (same & (p[:, None] <= p[None, :])).astype(np.float32)
    blk0 = np.broadcast_to((p[:, None] < 64), (128, 128)).astype(np.float32)
    blk1 = np.broadcast_to((p[:, None] >= 64), (128, 128)).astype(np.float32)
    posmask = np.where(same & (p[None, :] <= p[:, None]), 0.0, 1e4).astype(np.float32)
    posmaskT = np.where(same & (p[None, :] >= p[:, None]), 0.0, 1e4).astype(np.float32)
    negstrict = np.where(same & (p[None, :] < p[:, None]), -1.0, 0.0).astype(np.float32)
    cst = np.ascontiguousarray(np.stack([ident, onesBD, triBD, blk0, blk1, posmask, posmaskT, negstrict], 1))
    nwr = np.ascontiguousarray(np.broadcast_to(np.asarray(inp["gdn_norm_w"])[None, :], (128, 128))).astype(np.float32)
    identb = ident.astype(ml_dtypes.bfloat16)
    xTs = [xT_layout(x_full[b]) for b in range(2)]
    in_maps = []
    for core in range(NCORES):
        b, hp = divmod(core, 4)
        m = mod[b, layer]
        sh1, sc1 = m[0:1024], m[1024:2048]
        pm1 = np.ascontiguousarray(np.stack([pm(sc1), pm(sh1)], 1)).astype(np.float32)
        cols = []
        cws = []
        for h2 in range(2):
            hd = hp * 2 + h2
            for i in range(3):
                cols.append(wi[:, i * 1024 + hd * 128:i * 1024 + (hd + 1) * 128])
                cws.append(cwf[:, i * 1024 + hd * 128:i * 1024 + (hd + 1) * 128].T)
        for h2 in range(2):
            hd = hp * 2 + h2
            cols.append(wi[:, 3072 + hd * 128:3072 + (hd + 1) * 128])
        wcat = np.concatenate(cols, axis=1)
        h0, h1 = hp * 2, hp * 2 + 1
        wab = np.stack([wi[:, 4096 + h0], wi[:, 4096 + h1], wi[:, 4104 + h0], wi[:, 4104 + h1]], 1)
        hs = np.array([inp["gdn_dt_bias"][h0], inp["gdn_dt_bias"][h1], inp["gdn_a_log"][h0], inp["gdn_a_log"][h1]], np.float32)
        in_maps.append({
            "xT": xTs[b], "pm1": pm1, "w": wtile(wcat), "wab": wtile(wab),
            "cw": np.ascontiguousarray(np.stack(cws, 1)).astype(np.float32),
            "cst": cst, "hs": np.ascontiguousarray(np.broadcast_to(hs[None], (128, 4))), "nw": nwr, "identb": identb,
        })
    res = run_bass_kernel_spmd(nc, in_maps, core_ids=list(range(NCORES)))
    o_full = np.empty((2, S, 1024), ml_dtypes.bfloat16)
    for core in range(NCORES):
        b, hp = divmod(core, 4)
        o = np.asarray(res.results[core]["o"])
        o_full[b, :, hp * 256:(hp + 1) * 256] = o.transpose(1, 0, 2).reshape(S, 256)
    return oT_from_tokenmajor(o_full)


def kernel(**inputs):
    inp = {k: np.asarray(v) for k, v in inputs.items()}
    mod = run_C(inp)
    x = np.ascontiguousarray(inp["x"], dtype=np.float32)
    mixers = (run_Mgdn, run_Mret, run_Mgmlp, run_Msb)
    wouts = (inp["gdn_w_out"], inp["ret_w_out"], inp["gmlp_w_out"], inp["sb_w_out"])
    for layer in range(DEPTH):
        oT = mixers[layer](x, mod, inp)
        x = run_F(x, oT, layer, mod, inp, wouts[layer])
    return x.astype(np.float32)
```

```python
import contextlib
import math

import numpy as np
import ml_dtypes

import concourse.bass as bass
import concourse.mybir as mybir
from concourse.bass_utils import run_bass_kernel_spmd

F32 = mybir.dt.float32
BF16 = mybir.dt.bfloat16
AF = mybir.ActivationFunctionType
ALU = mybir.AluOpType
AX = mybir.AxisListType

NCORES = 8
D = 1024
B = 2
S = 8192
DEPTH = 4
FH = 2816
ALPHA = (2.0 * DEPTH) ** 0.25
LN_EPS = 1e-5

ENGS = ("pe", "act", "dve", "pool", "sp")


def _prod(xs):
    r = 1
    for v in xs:
        r *= int(v)
    return r


class Prog:
    N_DMA_SLOTS = 12

    def __init__(self, nc):
        self.nc = nc
        self.stack = contextlib.ExitStack()
        self.streams = {e: [] for e in ENGS}
        self.cnt = {e: 0 for e in ENGS}
        self.seen = {e: {} for e in ENGS}
        self.acc = {}
        self.esem = {e: self.stack.enter_context(nc.semaphore("sem_" + e)) for e in ENGS}
        self.semobj = {("e", e): self.esem[e] for e in ENGS}
        self.dma_slots = {}
        self.dma_next = {}
        for q in ("sp", "pool", "act"):
            self.dma_slots[q] = []
            self.dma_next[q] = 0
        self.out_dmas = []
        self.psum_names = set()

    def sb(self, name, shape, dtype=F32):
        return self.stack.enter_context(self.nc.sbuf_tensor(name, list(shape), dtype))

    def ps(self, name, shape, dtype=F32):
        self.psum_names.add(name)
        return self.stack.enter_context(self.nc.psum_tensor(name, list(shape), dtype))

    def dram(self, name, shape, dtype, kind):
        return self.nc.dram_tensor(name, list(shape), dtype, kind=kind).ap()

    @staticmethod
    def region(ap):
        t = ap.tensor
        pairs = [(int(s), int(c)) for s, c in ap.ap]
        off = int(ap.offset)
        kind = type(t).__name__
        if kind.startswith("DRam"):
            ext = sum((c - 1) * abs(s) for s, c in pairs)
            return (t.name, 0, 0, off, off + ext)
        fsz = _prod(t.shape[1:])
        p0, f0 = divmod(off, fsz)
        ps_, pc = pairs[0]
        pstep = ps_ // fsz if ps_ else 0
        ext = sum((c - 1) * abs(s) for s, c in pairs[1:])
        f1 = f0 + ext
        if kind.startswith("PSum"):
            epb = 2048 // (2 if t.dtype == BF16 else 4)
            f0 = (f0 // epb) * epb
            f1 = (f1 // epb) * epb + epb - 1
        return (t.name, p0, p0 + (pc - 1) * pstep, f0, f1)

    def _deps(self, eng, reads, writes, rec_key):
        need = {}

        def add(k, v):
            if need.get(k, 0) < v:
                need[k] = v

        rr = [self.region(a) for a in reads]
        ww = [self.region(a) for a in writes]
        for (name, pl, ph, fl, fh) in rr:
            ps_rar = name in self.psum_names
            for rec in self.acc.get(name, ()):
                if (rec[5] or (ps_rar and rec[4] != eng)) and not (rec[1] < pl or rec[0] > ph or rec[3] < fl or rec[2] > fh):
                    add(rec[6], rec[7])
        for (name, pl, ph, fl, fh) in ww:
            for rec in self.acc.get(name, ()):
                if not (rec[1] < pl or rec[0] > ph or rec[3] < fl or rec[2] > fh):
                    add(rec[6], rec[7])
        for (name, pl, ph, fl, fh) in ww:
            lst = self.acc.setdefault(name, [])
            lst[:] = [r for r in lst if not (r[0] >= pl and r[1] <= ph and r[2] >= fl and r[3] <= fh)]
            lst.append((pl, ph, fl, fh, eng, True, rec_key[0], rec_key[1]))
        for (name, pl, ph, fl, fh) in rr:
            lst = self.acc.setdefault(name, [])
            lst[:] = [r for r in lst if not ((not r[5]) and r[6] == rec_key[0]
                                             and r[0] >= pl and r[1] <= ph and r[2] >= fl and r[3] <= fh)]
            lst.append((pl, ph, fl, fh, eng, False, rec_key[0], rec_key[1]))
        waits = []
        own = ("e", eng)
        for k, v in need.items():
            if k == own:
                if eng == "pe" or v > self.cnt[eng]:
                    continue
            if self.seen[eng].get(k, 0) >= v:
                continue
            self.seen[eng][k] = v
            waits.append((k, v))
        return waits

    def op(self, eng, name, args, kwargs, reads, writes, inc=True):
        own = ("e", eng)
        val = self.cnt[eng] + 1
        waits = self._deps(eng, reads, writes, (own, val))
        if inc:
            self.cnt[eng] = val
        self.streams[eng].append((name, args, kwargs, waits, own if inc else None, 1))

    def dma(self, q, out, in_, is_output=False, **kwargs):
        slots = self.dma_slots[q]
        if len(slots) < self.N_DMA_SLOTS:
            key = ("d", q, len(slots))
            sem = self.stack.enter_context(self.nc.semaphore("dsem_%s_%d" % (q, len(slots))))
            self.semobj[key] = sem
            slots.append([key, 0])
            slot = slots[-1]
        else:
            slot = slots[self.dma_next[q] % self.N_DMA_SLOTS]
        self.dma_next[q] += 1
        key, uses = slot
        val = 16 * (uses + 1)
        waits = self._deps(q, [in_], [out], (key, val))
        if uses > 0 and self.seen[q].get(key, 0) < 16 * uses:
            self.seen[q][key] = 16 * uses
            waits.append((key, 16 * uses))
        slot[1] = uses + 1
        self.streams[q].append(("dma_start", (), dict(out=out, in_=in_, **kwargs), waits, key, 16))
        if is_output:
            self.out_dmas.append((q, key, val))

    def collective(self, kind, out, in_, groups, amt=16, op=None):
        q = "pool"
        key = ("c", len(self.semobj))
        sem = self.stack.enter_context(self.nc.semaphore("csem_%d" % len(self.semobj)))
        self.semobj[key] = sem
        waits = self._deps(q, [in_], [out], (key, amt))
        self.streams[q].append(("collective_compute", (kind, op if op is not None else ALU.bypass),
                                dict(replica_groups=groups, ins=[in_], outs=[out]), waits, key, amt))

    def mm(self, out, lhsT, rhs, start=True, stop=True, inc=None):
        if inc is None:
            inc = stop
        self.op("pe", "matmul", (out, lhsT, rhs), dict(start=start, stop=stop), [lhsT, rhs], [out], inc=inc)

    def transpose(self, out, in_, ident, inc=True):
        self.op("pe", "transpose", (out, in_, ident), {}, [in_, ident], [out], inc=inc)

    def act(self, out, in_, func, bias=None, scale=None, accum_out=None, eng="act"):
        kw = {}
        reads = [in_]
        writes = [out]
        if bias is not None:
            kw["bias"] = bias
            if not isinstance(bias, (int, float)):
                reads.append(bias)
        if scale is not None:
            kw["scale"] = scale
            if not isinstance(scale, (int, float)):
                reads.append(scale)
        if accum_out is not None:
            kw["accum_out"] = accum_out
            writes.append(accum_out)
        self.op(eng, "activation", (out, in_, func), kw, reads, writes)

    def tt(self, eng, out, in0, in1, op):
        self.op(eng, "tensor_tensor", (out, in0, in1, op), {}, [in0, in1], [out])

    def ts(self, eng, out, in0, s1, s2, op0, op1=None, accum_out=None):
        reads = [in0]
        for s in (s1, s2):
            if s is not None and not isinstance(s, (int, float)):
                reads.append(s)
        kw = {}
        writes = [out]
        if accum_out is not None:
            kw["accum_out"] = accum_out
            writes.append(accum_out)
        if op1 is None:
            self.op(eng, "tensor_scalar", (out, in0, s1, None, op0), kw, reads, writes)
        else:
            self.op(eng, "tensor_scalar", (out, in0, s1, s2, op0, op1), kw, reads, writes)

    def stt(self, eng, out, in0, scalar, in1, op0, op1):
        reads = [in0, in1]
        if not isinstance(scalar, (int, float)):
            reads.append(scalar)
        self.op(eng, "scalar_tensor_tensor", (out, in0, scalar, in1, op0, op1), {}, reads, [out])

    def copy(self, eng, out, in_):
        if eng == "act":
            self.op(eng, "copy", (out, in_), {}, [in_], [out])
        else:
            self.op(eng, "tensor_copy", (out, in_), {}, [in_], [out])

    def memset(self, eng, ap, val):
        self.op(eng, "memset", (ap, val), {}, [], [ap])

    def generic(self, eng, name, args, reads, writes, **kwargs):
        self.op(eng, name, tuple(args), kwargs, reads, writes)

    def check(self):
        counts = {}
        ptr = {e: 0 for e in ENGS}
        while True:
            progress = False
            for e in ENGS:
                st = self.streams[e]
                while ptr[e] < len(st):
                    name, args, kwargs, waits, inc, amt = st[ptr[e]]
                    if any(counts.get(k, 0) < v for k, v in waits):
                        break
                    if inc is not None:
                        counts[inc] = counts.get(inc, 0) + amt
                    ptr[e] += 1
                    progress = True
            if all(ptr[e] == len(self.streams[e]) for e in ENGS):
                return
            if not progress:
                msg = []
                for e in ENGS:
                    if ptr[e] < len(self.streams[e]):
                        name, args, kwargs, waits, inc, amt = self.streams[e][ptr[e]]
                        bad = [(k, v, counts.get(k, 0)) for k, v in waits if counts.get(k, 0) < v]
                        msg.append("%s stuck at %d/%d (%s) waiting %s" % (e, ptr[e], len(self.streams[e]), name, bad))
                raise RuntimeError("DEADLOCK in recorded program:\n" + "\n".join(msg))

    def emit(self):
        nc = self.nc
        self.check()
        tail = {}
        for q, key, val in self.out_dmas:
            d = tail.setdefault(q, {})
            d[key] = max(d.get(key, 0), val)

        def replay(e, eng):
            for (name, args, kwargs, waits, inc, amt) in self.streams[eng]:
                for k, v in waits:
                    e.wait_ge(self.semobj[k], v)
                ins = getattr(e, name)(*args, **kwargs)
                if inc is not None:
                    ins.then_inc(self.semobj[inc], amt)
            for k, v in tail.get(eng, {}).items():
                e.wait_ge(self.semobj[k], v)

        with nc.Block() as block:
            @block.tensor
            def _(e):
                replay(e, "pe")

            @block.scalar
            def _(e):
                replay(e, "act")

            @block.vector
            def _(e):
                replay(e, "dve")

            @block.gpsimd
            def _(e):
                replay(e, "pool")

            @block.sync
            def _(e):
                replay(e, "sp")
        self.stack.close()

    def stats(self):
        return {e: len(self.streams[e]) for e in ENGS}


def build_C():
    nc = bass.Bass("TRN2", target_bir_lowering=False)
    P = Prog(nc)
    cT_d = P.dram("cT", [128, 8, 2], F32, "ExternalInput")
    cw_d = P.dram("cond_w", [128, 8, 1024], F32, "ExternalInput")
    cb_d = P.dram("cond_b", [128, 8], F32, "ExternalInput")
    aw_d = P.dram("ada_w", [128, 8, 3072], F32, "ExternalInput")
    ab_d = P.dram("ada_b", [128, 24], F32, "ExternalInput")
    out_d = P.dram("modpm", [128, 24, 2], F32, "ExternalOutput")

    cT = P.sb("cT_sb", [128, 8, 2])
    cw = P.sb("cw_sb", [128, 8, 1024])
    cb = P.sb("cb_sb", [128, 8])
    aw = P.sb("aw_sb", [128, 8, 3072])
    ab = P.sb("ab_sb", [128, 24])
    eT = P.sb("eT_sb", [128, 8, 2])
    mo = P.sb("mo_sb", [128, 24, 2])
    e_ps = P.ps("e_ps", [128, 8, 2])
    m_ps = P.ps("m_ps", [128, 24, 2])

    P.dma("sp", cT[:], cT_d)
    P.dma("sp", cb[:], cb_d)
    P.dma("sp", ab[:], ab_d)
    for kc in range(8):
        P.dma("sp", cw[:, kc, :], cw_d[:, kc, :])
    for kc in range(8):
        P.dma("sp", aw[:, kc, :], aw_d[:, kc, :])
    for jc in range(8):
        for kc in range(8):
            P.mm(e_ps[:, jc, :], cw[:, kc, jc * 128:(jc + 1) * 128], cT[:, kc, :], start=(kc == 0), stop=(kc == 7))
        P.act(eT[:, jc, :], e_ps[:, jc, :], AF.Silu, bias=cb[:, jc:jc + 1])
    for jc in range(24):
        for kc in range(8):
            P.mm(m_ps[:, jc, :], aw[:, kc, jc * 128:(jc + 1) * 128], eT[:, kc, :], start=(kc == 0), stop=(kc == 7))
        P.act(mo[:, jc, :], m_ps[:, jc, :], AF.Identity, bias=ab[:, jc:jc + 1])
    P.dma("sp", out_d, mo[:], is_output=True)
    P.emit()
    return nc


def pm(v, n=128):
    v = np.asarray(v)
    k = v.shape[-1] // n
    return np.ascontiguousarray(np.moveaxis(v.reshape(v.shape[:-1] + (k, n)), -1, 0))


def wtile(w):
    w = np.asarray(w)
    K, N = w.shape
    return np.ascontiguousarray(w.reshape(K // 128, 128, N).transpose(1, 0, 2))


def run_C(inp):
    nc = build_C()
    ada_flat = np.asarray(inp["ada_w"]).transpose(1, 0, 2).reshape(1024, 4 * 6144)
    adab_flat = np.asarray(inp["ada_b"]).reshape(4 * 6144)
    cT = pm(inp["c"])
    cT = np.ascontiguousarray(cT.transpose(0, 2, 1))
    cw = wtile(inp["cond_w"])
    cb = pm(inp["cond_b"])
    in_maps = []
    for core in range(NCORES):
        sl = slice(core * 3072, (core + 1) * 3072)
        in_maps.append({
            "cT": cT, "cond_w": cw, "cond_b": cb,
            "ada_w": wtile(ada_flat[:, sl]),
            "ada_b": pm(adab_flat[sl]),
        })
    res = run_bass_kernel_spmd(nc, in_maps, core_ids=list(range(NCORES)))
    mod = np.zeros((2, 4 * 6144), np.float32)
    for core in range(NCORES):
        o = np.asarray(res.results[core]["modpm"])
        mod[:, core * 3072:(core + 1) * 3072] = o.transpose(2, 1, 0).reshape(2, 3072)
    return mod.reshape(2, 4, 6144)


NT = 16
TBT = 8
GH = 2
NG = 22 // GH


def ln_tile(P, y_ps, x_in, Gt, lng, lnb, x_out, xhat_bf, sc):
    tmp, stats, mv, rstd, xhat = sc["tmp"], sc["stats"], sc["mv"], sc["rstd"], sc["xhat"]
    P.tt("dve", tmp[:], y_ps, Gt, ALU.mult)
    P.stt("dve", tmp[:], x_in, ALPHA, tmp[:], ALU.mult, ALU.add)
    for h in range(2):
        P.generic("dve", "bn_stats", (stats[:, h, :], tmp[:, h * 512:(h + 1) * 512]),
                  [tmp[:, h * 512:(h + 1) * 512]], [stats[:, h, :]])
    P.generic("dve", "bn_aggr", (mv[:], stats[:].rearrange("p a b -> p (a b)")), [stats[:]], [mv[:]])
    P.act(rstd[:], mv[:, 1:2], AF.Sqrt, bias=sc["eps"][:])
    P.generic("dve", "reciprocal", (rstd[:], rstd[:]), [rstd[:]], [rstd[:]])
    P.ts("dve", xhat[:], tmp[:], mv[:, 0:1], rstd[:], ALU.subtract, ALU.mult)
    if xhat_bf is not None:
        P.copy("act", xhat_bf, xhat[:])
    P.tt("pool", x_out, xhat[:], lng, ALU.mult)
    P.tt("pool", x_out, x_out, lnb, ALU.add)


def build_F(W):
    KO = W // 128
    nc = bass.Bass("TRN2", target_bir_lowering=False)
    P = Prog(nc)
    NTT = NT + 1
    x_d = P.dram("xin", [NTT * 128, 1024], F32, "ExternalInput")
    oT_d = P.dram("oT", [128, KO, NTT * 128], BF16, "ExternalInput")
    wo_d = P.dram("w_out", [128, KO * 1024], F32, "ExternalInput")
    wu_d = P.dram("w_up", [NG, 128, 8 * 2 * GH * 128], F32, "ExternalInput")
    wd_d = P.dram("w_dn", [NG, 128, GH * 1024], F32, "ExternalInput")
    cw_d = P.dram("conv_w", [128, 22, 3], F32, "ExternalInput")
    cb_d = P.dram("conv_b", [128, 22], F32, "ExternalInput")
    rows_d = P.dram("rows", [128, 6, 1024], F32, "ExternalInput")
    pmv_d = P.dram("pmv", [128, 4, 8], F32, "ExternalInput")
    flag_d = P.dram("flag", [128, 1], F32, "ExternalInput")
    id_d = P.dram("ident", [128, 128], BF16, "ExternalInput")
    out_d = P.dram("xout", [NT * 128, 1024], F32, "ExternalOutput")

    big = P.sb("big", [128, 22 * 1024], BF16)
    wup = [P.sb("wup%d" % i, [128, 8, 2, GH * 128], BF16) for i in range(3)]
    wdn = [P.sb("wdn%d" % i, [128, GH, 1024], BF16) for i in range(3)]
    x1 = P.sb("x1", [128, TBT, 1024])
    xin = [P.sb("xin%d" % i, [128, 1024]) for i in range(2)]
    oT = [P.sb("oT%d" % i, [128, KO, 128], BF16) for i in range(3)]
    hT = P.sb("hT", [128, 8, TBT * 128], BF16)
    hTs = P.sb("hTs", [128, 8, 128], BF16)
    rows = P.sb("rows_sb", [128, 6, 1024])
    pmv = P.sb("pmv_sb", [128, 4, 8])
    A2 = P.sb("A2", [128, 8])
    B2 = P.sb("B2", [128, 8])
    cw = P.sb("cw_sb", [128, 22, 3])
    cb = P.sb("cb_sb", [128, 22])
    flag = P.sb("flag_sb", [128, 1])
    ident = P.sb("ident_sb", [128, 128], BF16)
    sc = dict(tmp=P.sb("tmp", [128, 1024]), stats=P.sb("stats", [128, 2, 6]), mv=P.sb("mv", [128, 2]),
              rstd=P.sb("rstd", [128, 1]), xhat=P.sb("xhat", [128, 1024]), eps=P.sb("eps", [128, 1]))
    P.memset("dve", sc["eps"][:], LN_EPS)
    xhb = [P.sb("xhb%d" % i, [128, 1024], BF16) for i in range(4)]
    xo = [P.sb("xo%d" % i, [128, 1024]) for i in range(2)]
    gsb = [P.sb("gsb%d" % i, [128, 2 + 512]) for i in range(2)]
    gc = [P.sb("gc%d" % i, [128, 512]) for i in range(2)]
    gs_rot = [0]
    ghalo = P.sb("ghalo", [128, 22, 2])
    PA = P.ps("PA", [128, 6, 512])
    PT = P.ps("PT", [128, 2, 1024], BF16)

    G1, G2 = rows[:, 0, :], rows[:, 1, :]
    lng0, lnb0, lng1, lnb1 = rows[:, 2, :], rows[:, 3, :], rows[:, 4, :], rows[:, 5, :]

    P.dma("sp", rows[:], rows_d)
    P.dma("sp", pmv[:], pmv_d)
    P.dma("sp", cw[:], cw_d)
    P.dma("sp", cb[:], cb_d)
    P.dma("sp", flag[:], flag_d)
    P.dma("sp", ident[:], id_d)
    P.ts("dve", rows[:, 0:2, :], rows[:, 0:2, :], 1.0, None, ALU.add)
    P.ts("dve", pmv[:, 0, :], pmv[:, 0, :], 1.0, None, ALU.add)
    P.tt("dve", A2[:], pmv[:, 2, :], pmv[:, 0, :], ALU.mult)
    P.tt("dve", B2[:], pmv[:, 3, :], pmv[:, 0, :], ALU.mult)
    P.tt("dve", B2[:], B2[:], pmv[:, 1, :], ALU.add)

    ps_rot = [0]

    def next_y():
        i = ps_rot[0] % 3
        ps_rot[0] += 1
        return PA[:, 2 * i:2 * i + 2, :].rearrange("p a b -> p (a b)"), (PA[:, 2 * i, :], PA[:, 2 * i + 1, :])

    wout_v = big[:, 0:KO * 1024].rearrange("p (k n) -> p k n", k=KO)
    uT = big[:].rearrange("p (j t) -> p j t", j=22)

    xin_rot = [0]
    oT_rot = [0]
    xhb_rot = [0]
    pt_rot = [0]

    def phase_A(tiles, dst_x1, dst_hT):
        n = len(tiles)
        xh_list = []
        for i, gt in enumerate(tiles):
            ob = oT[oT_rot[0] % 3]
            oT_rot[0] += 1
            P.dma("sp", ob[:], oT_d[:, :, gt * 128:(gt + 1) * 128])
            xb = xin[xin_rot[0] % 2]
            xin_rot[0] += 1
            P.dma("sp", xb[:], x_d[gt * 128:(gt + 1) * 128, :])
            yfull, (y0, y1) = next_y()
            oc = 0
            for kc in range(KO):
                P.mm(y0, ob[:, kc, oc:oc + 128], wout_v[:, kc, 0:512], start=(kc == 0), stop=(kc == KO - 1), inc=False)
                P.mm(y1, ob[:, kc, oc:oc + 128], wout_v[:, kc, 512:1024], start=(kc == 0), stop=(kc == KO - 1),
                     inc=(kc == KO - 1))
            xh = xhb[xhb_rot[0] % 4]
            xhb_rot[0] += 1
            d = dst_x1(i)
            if d is None:
                d = xo[0][:]
            ln_tile(P, yfull, xb[:], G1, lng0, lnb0, d, xh[:], sc)
            xh_list.append(xh)
            if len(xh_list) == 4 or i == n - 1:
                m = len(xh_list)
                base = (i - m + 1) * 128
                for j in range(8):
                    pt = PT[:, pt_rot[0] % 2, :]
                    pt_rot[0] += 1
                    for q in range(m):
                        P.transpose(pt[:, q * 128:(q + 1) * 128], xh_list[q][:, j * 128:(j + 1) * 128], ident[:],
                                    inc=(q == m - 1))
                    P.act(dst_hT[:, j, base:base + m * 128], pt[:, 0:m * 128], AF.Identity,
                          bias=B2[:, j:j + 1], scale=A2[:, j:j + 1])
                xh_list = []

    wu_i = [0]
    wd_i = [0]

    def load_wup(g):
        b = wup[wu_i[0] % 3]
        wu_i[0] += 1
        P.dma("pool", b[:].rearrange("p a b c -> p (a b c)"), wu_d[g])
        return b

    def load_wdn(g):
        b = wdn[wd_i[0] % 3]
        wd_i[0] += 1
        P.dma("pool", b[:].rearrange("p a b -> p (a b)"), wd_d[g])
        return b

    for tb in range(NT // TBT):
        for k4 in range(0, KO, 4):
            P.dma("pool", big[:, k4 * 1024:(k4 + 4) * 1024], wo_d[:, k4 * 1024:(k4 + 4) * 1024])
        if tb == 0:
            phase_A([0], lambda i: None, hTs)
        phase_A([1 + tb * TBT + i for i in range(TBT)], lambda i: x1[:, i, :], hT)

        wq = [load_wup(0), load_wup(1)]
        pb = 0
        for g in range(NG):
            if g + 2 < NG:
                wq.append(load_wup(g + 2))
            wb = wq[g]
            for jj in range(GH):
                j = g * GH + jj
                gs_prev = None
                for half in range(2):
                    gs = gsb[gs_rot[0] % 2]
                    g_c = gc[gs_rot[0] % 2]
                    gs_rot[0] += 1
                    if half == 1:
                        P.copy("pool", gs[:, 0:2], gs_prev[:, 512:514])
                    elif tb == 0:
                        hp = PA[:, (pb % 3) * 2, 0:2]
                        for kc in range(8):
                            P.mm(hp, wb[:, kc, 0, jj * 128:(jj + 1) * 128], hTs[:, kc, 126:128], start=(kc == 0), stop=(kc == 7))
                        P.ts("dve", gs[:, 0:2], hp, flag[:, 0:1], None, ALU.mult)
                        pb += 1
                    else:
                        P.copy("pool", gs[:, 0:2], ghalo[:, j, :])
                    i3 = pb % 3
                    pb += 1
                    gp, up = PA[:, 2 * i3, :], PA[:, 2 * i3 + 1, :]
                    tok = slice(half * 512, (half + 1) * 512)
                    for kc in range(8):
                        P.mm(gp, wb[:, kc, 0, jj * 128:(jj + 1) * 128], hT[:, kc, tok], start=(kc == 0), stop=(kc == 7))
                    for kc in range(8):
                        P.mm(up, wb[:, kc, 1, jj * 128:(jj + 1) * 128], hT[:, kc, tok], start=(kc == 0), stop=(kc == 7))
                    P.copy("act", gs[:, 2:514], gp)
                    P.ts("dve", g_c[:], gs[:, 0:512], cw[:, j, 0:1], cb[:, j:j + 1], ALU.mult, ALU.add)
                    P.stt("dve", g_c[:], gs[:, 1:513], cw[:, j, 1:2], g_c[:], ALU.mult, ALU.add)
                    P.stt("dve", g_c[:], gs[:, 2:514], cw[:, j, 2:3], g_c[:], ALU.mult, ALU.add)
                    P.act(g_c[:], g_c[:], AF.Silu)
                    P.tt("dve", uT[:, j, tok], g_c[:], up, ALU.mult)
                    gs_prev = gs
                if tb + 1 < NT // TBT:
                    P.copy("pool", ghalo[:, j, :], gs_prev[:, 512:514])

        dq = [load_wdn(0), load_wdn(1)]
        t0 = 0
        while t0 < TBT:
            tl = list(range(t0, min(t0 + 3, TBT)))
            ys = [next_y() for _ in tl]
            for g in range(NG):
                if g + 2 < NG:
                    dq.append(load_wdn(g + 2))
                elif t0 + 3 < TBT:
                    dq.append(load_wdn(g + 2 - NG))
                db = dq.pop(0)
                for jj in range(GH):
                    j = g * GH + jj
                    for ti, t in enumerate(tl):
                        y0, y1 = ys[ti][1]
                        last = (j == 21)
                        P.mm(y0, uT[:, j, t * 128:(t + 1) * 128], db[:, jj, 0:512], start=(j == 0), stop=last, inc=False)
                        P.mm(y1, uT[:, j, t * 128:(t + 1) * 128], db[:, jj, 512:1024], start=(j == 0), stop=last,
                             inc=(last or (jj == GH - 1 and ti == len(tl) - 1)))
            for ti, t in enumerate(tl):
                xb = xo[(tb * TBT + t) % 2]
                ln_tile(P, ys[ti][0], x1[:, t, :], G2, lng1, lnb1, xb[:], None, sc)
                gt = tb * TBT + t
                P.dma("sp", out_d[gt * 128:(gt + 1) * 128, :], xb[:], is_output=True)
            t0 += 3
    P.emit()
    return nc


_CACHE = {}


def bf16(a):
    return np.asarray(a).astype(ml_dtypes.bfloat16)


def run_F(x_full, oT_cores, layer, mod, inp, w_out):
    W = w_out.shape[0]
    KO = W // 128
    key = ("F", W)
    if key not in _CACHE:
        _CACHE[key] = build_F(W)
    nc = _CACHE[key]
    wup = np.asarray(inp["ffn_up"][layer])
    wt = wtile(wup)
    wu_l = np.empty((NG, 128, 8, 2, GH * 128), np.float32)
    for g in range(NG):
        wu_l[g, :, :, 0, :] = wt[:, :, g * GH * 128:(g + 1) * GH * 128]
        wu_l[g, :, :, 1, :] = wt[:, :, FH + g * GH * 128:FH + (g + 1) * GH * 128]
    wu_l = wu_l.reshape(NG, 128, 8 * 2 * GH * 128)
    wd = wtile(inp["ffn_down"][layer])
    wd_l = np.ascontiguousarray(wd.reshape(128, NG, GH * 1024).transpose(1, 0, 2))
    wo_l = wtile(w_out).reshape(128, KO * 1024)
    cw = np.ascontiguousarray(pm(inp["ffn_conv_w"][layer]).transpose(0, 2, 1))
    cb = pm(inp["ffn_conv_b"][layer])
    ident = np.eye(128, dtype=np.float32).astype(ml_dtypes.bfloat16)
    lng = np.asarray(inp["ln_g"][layer])
    lnb = np.asarray(inp["ln_b"][layer])
    in_maps = []
    for core in range(NCORES):
        b, q = divmod(core, 4)
        m = mod[b, layer]
        g1, sh2, sc2, g2 = m[2048:3072], m[3072:4096], m[4096:5120], m[5120:6144]
        rows = np.stack([g1, g2, lng[0], lnb[0], lng[1], lnb[1]], 0)
        rows = np.ascontiguousarray(np.broadcast_to(rows[None], (128, 6, 1024))).astype(np.float32)
        pmv = np.ascontiguousarray(np.stack([pm(sc2), pm(sh2), pm(lng[0]), pm(lnb[0])], 1)).astype(np.float32)
        t0 = q * 2048
        xin = np.zeros((17 * 128, 1024), np.float32)
        if q > 0:
            xin[:] = x_full[b, t0 - 128:t0 + 2048]
        else:
            xin[128:] = x_full[b, 0:2048]
        in_maps.append({
            "xin": xin, "oT": oT_cores[core], "w_out": wo_l, "w_up": wu_l, "w_dn": wd_l,
            "conv_w": cw, "conv_b": cb, "rows": rows, "pmv": pmv,
            "flag": np.full((128, 1), 0.0 if q == 0 else 1.0, np.float32), "ident": ident,
        })
    res = run_bass_kernel_spmd(nc, in_maps, core_ids=list(range(NCORES)))
    out = np.empty((2, 8192, 1024), np.float32)
    for core in range(NCORES):
        b, q = divmod(core, 4)
        out[b, q * 2048:(q + 1) * 2048] = np.asarray(res.results[core]["xout"])
    return out


def oT_from_tokenmajor(o_full):
    W = o_full.shape[-1]
    KO = W // 128
    outs = []
    for core in range(NCORES):
        b, q = divmod(core, 4)
        t0 = q * 2048
        seg = np.zeros((17 * 128, W), ml_dtypes.bfloat16)
        if q > 0:
            seg[:] = o_full[b, t0 - 128:t0 + 2048]
        else:
            seg[128:] = o_full[b, 0:2048]
        outs.append(np.ascontiguousarray(seg.T.reshape(KO, 128, 17 * 128).transpose(1, 0, 2)))
    return outs


def build_Mgmlp():
    nc = bass.Bass("TRN2", target_bir_lowering=False)
    P = Prog(nc)
    TOK = 2048
    xT_d = P.dram("xT", [128, 8, TOK], F32, "ExternalInput")
    pm1_d = P.dram("pm1", [128, 2, 8], F32, "ExternalInput")
    win_d = P.dram("w_in", [128, 8, 4096], F32, "ExternalInput")
    rows_d = P.dram("rows", [128, 2, 2048], F32, "ExternalInput")
    wsT_d = P.dram("wsT", [128, 8, 128], F32, "ExternalInput")
    mask_d = P.dram("mask", [128, 128], F32, "ExternalInput")
    bs_d = P.dram("bs", [1, 1024], F32, "ExternalInput")
    out_d = P.dram("oT", [128, 16, TOK], BF16, "ExternalOutput")

    wu = P.sb("wu", [128, 8, 2048], BF16)
    wv = P.sb("wv", [128, 8, 2048], BF16)
    rows = P.sb("rows_sb", [128, 2, 2048])
    pm1 = P.sb("pm1_sb", [128, 2, 8])
    wsT = P.sb("wsT_sb", [128, 8, 128])
    mask = P.sb("mask_sb", [128, 128])
    wsTm = P.sb("wsTm", [128, 8, 128], BF16)
    bs = P.sb("bs_sb", [1, 1024])
    ones1 = P.sb("ones1", [1, 128])
    eps = P.sb("eps", [128, 1])
    xs = [P.sb("xs%d" % i, [128, 512]) for i in range(3)]
    hT = P.sb("hT", [128, 8, 512], BF16)
    uT = P.sb("uT", [128, 16, 512])
    vsb = [P.sb("vsb%d" % i, [128, 2048]) for i in range(2)]
    vln = [P.sb("vln%d" % i, [128, 2048], BF16) for i in range(2)]
    uvb = [P.sb("uvb%d" % i, [128, 16, 512], BF16) for i in range(2)]
    stats = P.sb("stats", [128, 4, 6])
    mv = P.sb("mv", [128, 2])
    rstd = P.sb("rstd", [128, 1])
    pu = P.ps("pu", [128, 2, 512])
    pv = P.ps("pv", [128, 2, 512])
    psp = P.ps("psp", [128, 16, 128])

    P.dma("sp", pm1[:], pm1_d)
    P.dma("sp", wsT[:], wsT_d)
    P.dma("sp", mask[:], mask_d)
    P.dma("sp", bs[:], bs_d)
    P.dma("sp", rows[:], rows_d)
    P.memset("dve", ones1[:], 1.0)
    P.memset("dve", eps[:], LN_EPS)
    P.ts("dve", pm1[:, 0, :], pm1[:, 0, :], 1.0, None, ALU.add)
    for g in range(8):
        P.tt("dve", wsTm[:, g, :], wsT[:, g, :], mask[:], ALU.mult)
    for kc in range(8):
        P.dma("pool", wu[:, kc, :], win_d[:, kc, 0:2048])
    for kc in range(8):
        P.dma("pool", wv[:, kc, :], win_d[:, kc, 2048:4096])

    xi = 0
    vi = 0
    for blk in range(TOK // 512):
        t0 = blk * 512
        for kc in range(8):
            xb = xs[xi % 3]
            xi += 1
            P.dma("sp", xb[:], xT_d[:, kc, t0:t0 + 512])
            P.act(hT[:, kc, :], xb[:], AF.Identity, bias=pm1[:, 1, kc:kc + 1], scale=pm1[:, 0, kc:kc + 1])
        for uc in range(16):
            pt = pu[:, uc % 2, :]
            for kc in range(8):
                P.mm(pt, wu[:, kc, uc * 128:(uc + 1) * 128], hT[:, kc, :], start=(kc == 0), stop=(kc == 7))
            P.act(uT[:, uc, :], pt, AF.Gelu)
        ob = uvb[blk % 2]
        for ch in range(4):
            tok = slice(ch * 128, (ch + 1) * 128)
            vb = vsb[vi % 2]
            vl = vln[vi % 2]
            vi += 1
            for half in range(2):
                for q in range(2):
                    c0 = (half * 2 + q) * 512
                    for kc in range(8):
                        P.mm(pv[:, q, :], hT[:, kc, tok], wv[:, kc, c0:c0 + 512], start=(kc == 0), stop=(kc == 7))
                P.act(vb[:, half * 1024:(half + 1) * 1024], pv[:].rearrange("p a b -> p (a b)"), AF.Gelu)
            for q in range(4):
                P.generic("dve", "bn_stats", (stats[:, q, :], vb[:, q * 512:(q + 1) * 512]),
                          [vb[:, q * 512:(q + 1) * 512]], [stats[:, q, :]])
            P.generic("dve", "bn_aggr", (mv[:], stats[:].rearrange("p a b -> p (a b)")), [stats[:]], [mv[:]])
            P.act(rstd[:], mv[:, 1:2], AF.Sqrt, bias=eps[:])
            P.generic("dve", "reciprocal", (rstd[:], rstd[:]), [rstd[:]], [rstd[:]])
            P.ts("dve", vb[:], vb[:], mv[:, 0:1], rstd[:], ALU.subtract, ALU.mult)
            P.tt("pool", vb[:], vb[:], rows[:, 0, :], ALU.mult)
            P.tt("pool", vl[:], vb[:], rows[:, 1, :], ALU.add)
            for cc in range(16):
                g = cc // 2
                P.mm(psp[:, cc, :], vl[:, cc * 128:(cc + 1) * 128], wsTm[:, g, :], start=True, stop=False, inc=False)
                P.mm(psp[:, cc, :], ones1[0:1, :], bs[0:1, g * 128:(g + 1) * 128], start=False, stop=True,
                     inc=(cc % 4 == 3))
            P.tt("dve", ob[:, :, tok], psp[:], uT[:, :, tok], ALU.mult)
        P.dma("sp", out_d[:, :, t0:t0 + 512], ob[:], is_output=True)
    P.emit()
    return nc


def xT_layout(xseg):
    T = xseg.shape[0]
    return np.ascontiguousarray(np.asarray(xseg).T.reshape(8, 128, T).transpose(1, 0, 2))


def run_Mgmlp(x_full, mod, inp):
    if "Mgmlp" not in _CACHE:
        _CACHE["Mgmlp"] = build_Mgmlp()
    nc = _CACHE["Mgmlp"]
    layer = 2
    win = wtile(inp["gmlp_w_in"])
    rows = np.stack([inp["gmlp_ln_g"], inp["gmlp_ln_b"]], 0)
    rows = np.ascontiguousarray(np.broadcast_to(rows[None], (128, 2, 2048))).astype(np.float32)
    wsT = np.ascontiguousarray(np.asarray(inp["gmlp_w_s"]).transpose(2, 0, 1))
    idx = np.arange(128)
    mask = (idx[:, None] <= idx[None, :]).astype(np.float32)
    bs = np.asarray(inp["gmlp_b_s"]).reshape(1, 1024).astype(np.float32)
    in_maps = []
    for core in range(NCORES):
        b, q = divmod(core, 4)
        m = mod[b, layer]
        sh1, sc1 = m[0:1024], m[1024:2048]
        pm1 = np.ascontiguousarray(np.stack([pm(sc1), pm(sh1)], 1)).astype(np.float32)
        in_maps.append({"xT": xT_layout(x_full[b, q * 2048:(q + 1) * 2048]), "pm1": pm1, "w_in": win,
                        "rows": rows, "wsT": wsT, "mask": mask, "bs": bs})
    res = run_bass_kernel_spmd(nc, in_maps, core_ids=list(range(NCORES)))
    oTs = [np.asarray(res.results[c]["oT"]) for c in range(NCORES)]
    outs = []
    for core in range(NCORES):
        b, q = divmod(core, 4)
        sh = np.zeros((128, 16, 128), ml_dtypes.bfloat16) if q == 0 else oTs[core - 1][:, :, -128:]
        outs.append(np.ascontiguousarray(np.concatenate([sh, oTs[core]], axis=2)))
    return outs


def build_Mret():
    nc = bass.Bass("TRN2", target_bir_lowering=False)
    P = Prog(nc)
    xT_d = P.dram("xT", [128, 8, S], F32, "ExternalInput")
    pm1_d = P.dram("pm1", [128, 2, 8], F32, "ExternalInput")
    wq_d = P.dram("wq", [128, 8, 256], F32, "ExternalInput")
    wk_d = P.dram("wk", [128, 8, 256], F32, "ExternalInput")
    wv_d = P.dram("wv", [128, 8, 512], F32, "ExternalInput")
    wg_d = P.dram("wg", [128, 8, 512], F32, "ExternalInput")
    cos_d = P.dram("cosT", [128, S], F32, "ExternalInput")
    sin_d = P.dram("sinT", [128, S], F32, "ExternalInput")
    xi_d = P.dram("xi_row", [128, 512], F32, "ExternalInput")
    dm_d = P.dram("dmaskT", [128, 128], F32, "ExternalInput")
    col_d = P.dram("cols", [128, 2], F32, "ExternalInput")
    id_d = P.dram("ident", [128, 128], BF16, "ExternalInput")
    out_d = P.dram("o", [S, 512], BF16, "ExternalOutput")

    wq = P.sb("wq_sb", [128, 8, 256], BF16)
    wk = P.sb("wk_sb", [128, 8, 256], BF16)
    wv = P.sb("wv_sb", [128, 8, 512], BF16)
    wg = P.sb("wg_sb", [128, 8, 512], BF16)
    pm1 = P.sb("pm1_sb", [128, 2, 8])
    xi_row = P.sb("xi_sb", [128, 512])
    dmT = P.sb("dm_sb", [128, 128])
    cols = P.sb("cols_sb", [128, 2])
    ident = P.sb("ident_sb", [128, 128], BF16)
    eps = P.sb("eps", [128, 1])
    xs = [P.sb("xs%d" % i, [128, 512]) for i in range(3)]
    hT = P.sb("hT", [128, 8, 512], BF16)
    cs = [P.sb("cs%d" % i, [128, 2, 512]) for i in range(2)]
    tmp = [P.sb("tmp%d" % i, [128, 512]) for i in range(4)]
    qT = P.sb("qT", [128, 2, 512], BF16)
    qxT = P.sb("qxT", [128, 2, 512], BF16)
    kT = P.sb("kT", [128, 2, 512], BF16)
    vsb = [P.sb("vsb%d" % i, [128, 512], BF16) for i in range(2)]
    sg = [P.sb("sg%d" % i, [128, 512]) for i in range(2)]
    kz = [P.sb("kz%d" % i, [128, 256], BF16) for i in range(2)]
    sT = [P.sb("sT%d" % i, [128, 128], BF16) for i in range(2)]
    on = [P.sb("on%d" % i, [128, 512]) for i in range(2)]
    ob = [P.sb("ob%d" % i, [128, 512], BF16) for i in range(2)]
    state = P.sb("state", [128, 2, 512])
    state_bf = P.sb("state_bf", [128, 2, 512], BF16)
    stats = P.sb("stats", [128, 6])
    mv = P.sb("mv", [128, 2])
    rstd = P.sb("rstd", [128, 1])
    pq = P.ps("pq", [128, 2, 512])
    pvg = P.ps("pvg", [128, 512])
    pS = P.ps("pS", [128, 512])
    pkt = P.ps("pkt", [128, 1024], BF16)
    po = P.ps("po", [128, 512])
    pst = P.ps("pst", [128, 2, 512])

    for dst, src in ((pm1, pm1_d), (xi_row, xi_d), (dmT, dm_d), (cols, col_d), (ident, id_d)):
        P.dma("sp", dst[:], src)
    P.memset("dve", eps[:], 1e-6)
    P.memset("dve", state[:], 0.0)
    P.memset("pool", state_bf[:], 0.0)
    P.ts("dve", pm1[:, 0, :], pm1[:, 0, :], 1.0, None, ALU.add)
    for dst, src in ((wq, wq_d), (wk, wk_d), (wv, wv_d), (wg, wg_d)):
        P.dma("pool", dst[:], src)

    xi_ = 0
    ci = 0
    for blk in range(S // 512):
        t0 = blk * 512
        cb = cs[blk % 2]
        P.dma("sp", cb[:, 0, :], cos_d[:, t0:t0 + 512])
        P.dma("sp", cb[:, 1, :], sin_d[:, t0:t0 + 512])
        for kc in range(8):
            xb = xs[xi_ % 3]
            xi_ += 1
            P.dma("sp", xb[:], xT_d[:, kc, t0:t0 + 512])
            P.act(hT[:, kc, :], xb[:], AF.Identity, bias=pm1[:, 1, kc:kc + 1], scale=pm1[:, 0, kc:kc + 1])
        for which, w_sb in (("q", wq), ("k", wk)):
            for dkc in range(2):
                for kc in range(8):
                    P.mm(pq[:, dkc, :], w_sb[:, kc, dkc * 128:(dkc + 1) * 128], hT[:, kc, :], start=(kc == 0), stop=(kc == 7))
            P.tt("dve", tmp[0][:], pq[:, 0, :], cb[:, 0, :], ALU.mult)
            P.tt("dve", tmp[1][:], pq[:, 1, :], cb[:, 1, :], ALU.mult)
            P.tt("dve", tmp[2][:], pq[:, 0, :], cb[:, 1, :], ALU.mult)
            P.tt("dve", tmp[3][:], pq[:, 1, :], cb[:, 0, :], ALU.mult)
            P.tt("pool", tmp[0][:], tmp[0][:], tmp[1][:], ALU.subtract)
            P.tt("pool", tmp[2][:], tmp[2][:], tmp[3][:], ALU.add)
            dst = qT if which == "q" else kT
            P.copy("act", dst[:, 0, :], tmp[0][:])
            P.copy("act", dst[:, 1, :], tmp[2][:])
            if which == "q":
                P.tt("pool", qxT[:, 0, :], tmp[0][:], xi_row[:], ALU.mult)
                P.tt("pool", qxT[:, 1, :], tmp[2][:], xi_row[:], ALU.mult)
        for ch in range(4):
            tok = slice(ch * 128, (ch + 1) * 128)
            vb, sgb, kzb, sTb, onb, obb = vsb[ci % 2], sg[ci % 2], kz[ci % 2], sT[ci % 2], on[ci % 2], ob[ci % 2]
            ci += 1
            for kc in range(8):
                P.mm(pvg[:], hT[:, kc, tok], wv[:, kc, :], start=(kc == 0), stop=(kc == 7))
            P.copy("act", vb[:], pvg[:])
            for kc in range(8):
                P.mm(pvg[:], hT[:, kc, tok], wg[:, kc, :], start=(kc == 0), stop=(kc == 7))
            P.act(sgb[:], pvg[:], AF.Silu)
            for dkc in range(2):
                P.transpose(pkt[:, dkc * 128:(dkc + 1) * 128], kT[:, dkc, tok], ident[:], inc=(dkc == 1))
            P.ts("dve", kzb[:], pkt[:, 0:256], cols[:, 0:1], None, ALU.mult)
            for dkc in range(2):
                P.mm(pS[:, 0:128], kT[:, dkc, tok], qT[:, dkc, tok], start=(dkc == 0), stop=(dkc == 1))
            P.tt("dve", sTb[:], pS[:, 0:128], dmT[:], ALU.mult)
            P.mm(po[:], sTb[:], vb[:], start=True, stop=False, inc=False)
            P.mm(po[:], qxT[:, 0, tok], state_bf[:, 0, :], start=False, stop=False, inc=False)
            P.mm(po[:], qxT[:, 1, tok], state_bf[:, 1, :], start=False, stop=True)
            for dkc in range(2):
                P.mm(pst[:, dkc, :], kzb[:, dkc * 128:(dkc + 1) * 128], vb[:], start=True, stop=True)
            P.stt("dve", state[:].rearrange("p a b -> p (a b)"), state[:].rearrange("p a b -> p (a b)"), cols[:, 1:2],
                  pst[:].rearrange("p a b -> p (a b)"), ALU.mult, ALU.add)
            P.copy("act", state_bf[:].rearrange("p a b -> p (a b)"), state[:].rearrange("p a b -> p (a b)"))
            P.generic("dve", "bn_stats", (stats[:], po[:]), [po[:]], [stats[:]])
            P.generic("dve", "bn_aggr", (mv[:], stats[:]), [stats[:]], [mv[:]])
            P.act(rstd[:], mv[:, 1:2], AF.Sqrt, bias=eps[:])
            P.generic("dve", "reciprocal", (rstd[:], rstd[:]), [rstd[:]], [rstd[:]])
            P.ts("dve", onb[:], po[:], mv[:, 0:1], rstd[:], ALU.subtract, ALU.mult)
            P.tt("pool", obb[:], onb[:], sgb[:], ALU.mult)
            P.dma("sp", out_d[t0 + ch * 128:t0 + (ch + 1) * 128, :], obb[:], is_output=True)
    P.emit()
    return nc


def run_Mret(x_full, mod, inp):
    if "Mret" not in _CACHE:
        _CACHE["Mret"] = build_Mret()
    nc = _CACHE["Mret"]
    layer = 1
    H, dk, dv, C = 4, 256, 512, 128
    w = np.asarray(inp["ret_w_in"])
    pos = np.arange(S, dtype=np.float32)
    inv_freq = (np.float32(10000.0) ** (-np.linspace(0.0, 1.0, dk // 2, dtype=np.float32))).astype(np.float32)
    ang = (pos[:, None] * inv_freq[None, :]).astype(np.float32)
    cosT = np.ascontiguousarray(np.cos(ang).T.astype(np.float32))
    sinT = np.ascontiguousarray(np.sin(ang).T.astype(np.float32))
    ident = np.eye(128, dtype=np.float32).astype(ml_dtypes.bfloat16)
    idx = np.arange(C, dtype=np.float32)
    in_maps = []
    xTs = [xT_layout(x_full[b]) for b in range(2)]
    for core in range(NCORES):
        b, h = divmod(core, 4)
        m = mod[b, layer]
        sh1, sc1 = m[0:1024], m[1024:2048]
        pm1 = np.ascontiguousarray(np.stack([pm(sc1), pm(sh1)], 1)).astype(np.float32)
        lg = np.log(np.float32(1.0) - np.power(np.float32(2.0), np.float32(-5.0 - h))).astype(np.float32)
        rel = idx[None, :] - idx[:, None]
        dmT = np.where(rel >= 0, np.exp(np.maximum(rel, 0.0) * lg), 0.0).astype(np.float32) * np.float32(dk ** -0.5)
        zeta = np.exp((C - 1.0 - idx) * lg).astype(np.float32) * np.float32(dk ** -0.5)
        xi = np.exp((idx + 1.0) * lg).astype(np.float32)
        gam = np.exp(np.float32(C) * lg).astype(np.float32)
        cols = np.stack([zeta, np.full(128, gam, np.float32)], 1).astype(np.float32)
        xi_row = np.ascontiguousarray(np.broadcast_to(np.tile(xi, 4)[None], (128, 512))).astype(np.float32)
        in_maps.append({
            "xT": xTs[b], "pm1": pm1,
            "wq": wtile(w[:, h * dk:(h + 1) * dk]), "wk": wtile(w[:, H * dk + h * dk:H * dk + (h + 1) * dk]),
            "wv": wtile(w[:, 2 * H * dk + h * dv:2 * H * dk + (h + 1) * dv]),
            "wg": wtile(w[:, 2 * H * dk + H * dv + h * dv:2 * H * dk + H * dv + (h + 1) * dv]),
            "cosT": cosT, "sinT": sinT, "xi_row": xi_row, "dmaskT": np.ascontiguousarray(dmT), "cols": cols, "ident": ident,
        })
    res = run_bass_kernel_spmd(nc, in_maps, core_ids=list(range(NCORES)))
    o_full = np.empty((2, S, H * dv), ml_dtypes.bfloat16)
    for core in range(NCORES):
        b, h = divmod(core, 4)
        o_full[b, :, h * dv:(h + 1) * dv] = np.asarray(res.results[core]["o"])
    return oT_from_tokenmajor(o_full)


def build_Msb():
    nc = bass.Bass("TRN2", target_bir_lowering=False)
    P = Prog(nc)
    NQB = S // 128
    xT_d = P.dram("xT", [128, 8, S], F32, "ExternalInput")
    xTr_d = P.dram("xTr", [128, 8, S], F32, "ExternalInput")
    pm1_d = P.dram("pm1", [128, 2, 8], F32, "ExternalInput")
    wq_d = P.dram("wq", [128, 8, 256], F32, "ExternalInput")
    wk_d = P.dram("wk", [128, 8, 256], F32, "ExternalInput")
    wv_d = P.dram("wv", [128, 8, 256], F32, "ExternalInput")
    mneg_d = P.dram("mneg", [128, 128], BF16, "ExternalInput")
    id_d = P.dram("ident", [128, 128], BF16, "ExternalInput")
    out_d = P.dram("o", [128, NQB, 256], BF16, "ExternalOutput")

    qT = P.sb("qT_all", [128, 2, S], BF16)
    kT = P.sb("kT_all", [128, 2, S], BF16)
    v_all = P.sb("v_all", [128, NQB, 256], BF16)
    o_qb = [P.sb("o_qb%d" % i, [128, 256], BF16) for i in range(2)]
    wq = P.sb("wq_sb", [128, 8, 256], BF16)
    wk = P.sb("wk_sb", [128, 8, 256], BF16)
    wv = P.sb("wv_sb", [128, 8, 256], BF16)
    pm1 = P.sb("pm1_sb", [128, 2, 8])
    mneg = P.sb("mneg_sb", [128, 128], BF16)
    ident = P.sb("ident_sb", [128, 128], BF16)
    zeros = P.sb("zeros", [128, 512])
    xs = [P.sb("xs%d" % i, [128, 512]) for i in range(3)]
    hT = P.sb("hT", [128, 8, 512], BF16)
    hTr = P.sb("hTr", [128, 8, 512], BF16)
    NSET = 5
    gb = [P.sb("gb%d" % i, [128, 512]) for i in range(NSET)]
    Pb = [P.sb("Pb%d" % i, [128, 513]) for i in range(NSET)]
    Ab = [P.sb("Ab%d" % i, [128, 512], BF16) for i in range(NSET)]
    ATb = [P.sb("ATb%d" % i, [128, 512], BF16) for i in range(NSET)]
    pz = P.ps("pz", [128, 5, 512])
    pT = P.ps("pT", [128, 2, 1024], BF16)
    po = P.ps("po", [128, 2, 64])
    pp = pz

    for dst, src in ((pm1, pm1_d), (mneg, mneg_d), (ident, id_d)):
        P.dma("sp", dst[:], src)
    P.memset("dve", zeros[:], 0.0)
    P.ts("dve", pm1[:, 0, :], pm1[:, 0, :], 1.0, None, ALU.add)
    for dst, src in ((wq, wq_d), (wk, wk_d), (wv, wv_d)):
        P.dma("pool", dst[:], src)

    xi_ = 0
    for blk in range(S // 512):
        t0 = blk * 512
        for (src_d, dst) in ((xT_d, hT), (xTr_d, hTr)):
            for kc in range(8):
                xb = xs[xi_ % 3]
                xi_ += 1
                P.dma("sp", xb[:], src_d[:, kc, t0:t0 + 512])
                P.act(dst[:, kc, :], xb[:], AF.Identity, bias=pm1[:, 1, kc:kc + 1], scale=pm1[:, 0, kc:kc + 1])
        for pair in range(2):
            pq_ = pp[:, pair, :]
            for kc in range(8):
                P.mm(pq_, wq[:, kc, pair * 128:(pair + 1) * 128], hT[:, kc, :], start=(kc == 0), stop=(kc == 7))
            P.act(qT[:, pair, t0:t0 + 512], pq_, AF.Copy, scale=0.125)
            pk_ = pp[:, 2 + pair, :]
            for kc in range(8):
                P.mm(pk_, wk[:, kc, pair * 128:(pair + 1) * 128], hTr[:, kc, :], start=(kc == 0), stop=(kc == 7))
            P.copy("dve", kT[:, pair, t0:t0 + 512], pk_)
        for ch in range(4):
            pv_ = pz[:, ch % 4, 0:256]
            for kc in range(8):
                P.mm(pv_, hTr[:, kc, ch * 128:(ch + 1) * 128], wv[:, kc, :], start=(kc == 0), stop=(kc == 7))
            P.copy("act" if ch % 2 else "dve", v_all[:, blk * 4 + ch, :], pv_)

    items = []
    for qb in range(NQB):
        nseg = (qb + 1 + 3) // 4
        for head in range(4):
            for sg_ in range(nseg):
                items.append((qb, head, sg_, nseg))
    prevP = {}

    def geom(it):
        qb, head, sg_, nseg = items[it]
        nb = qb + 1
        kb0 = sg_ * 4
        nk = min(4, nb - kb0)
        pair, par = divmod(head, 2)
        return qb, head, sg_, nseg, kb0, nk, nk * 128, pair, slice(par * 64, par * 64 + 64), it % NSET

    def stA(it):
        qb, head, sg_, nseg, kb0, nk, n, pair, prt, b4 = geom(it)
        t0 = qb * 128
        ks = (NQB - 1 - qb) * 128 + kb0 * 128
        zt = pz[:, b4, :]
        P.mm(zt[:, 0:n], qT[prt, pair, t0:t0 + 128], kT[prt, pair, ks:ks + n], start=True, stop=(sg_ != 0))
        if sg_ == 0:
            P.mm(zt[:, 0:128], ident[:], mneg[:], start=False, stop=True)

    def stB(it):
        qb, head, sg_, nseg, kb0, nk, n, pair, prt, b4 = geom(it)
        zt = pz[:, b4, :]
        g_, P_, A_ = gb[b4], Pb[b4], Ab[b4]
        P.act(g_[:, 0:n], zt[:, 0:n], AF.Sigmoid, scale=-1.0)
        if sg_ == 0:
            P.memset("pool", P_[:, 0:1], 1.0)
            P.generic("dve", "tensor_tensor_scan", (P_[:, 1:1 + n], g_[:, 0:n], zeros[:, 0:n], 1.0, ALU.mult, ALU.add),
                      [g_[:, 0:n], zeros[:, 0:n]], [P_[:, 1:1 + n]])
        else:
            pp_, pn = prevP[(qb, head)]
            P.copy("pool", P_[:, 0:1], pp_[:, pn:pn + 1])
            P.generic("dve", "tensor_tensor_scan", (P_[:, 1:1 + n], g_[:, 0:n], zeros[:, 0:n], pp_[:, pn:pn + 1], ALU.mult, ALU.add),
                      [g_[:, 0:n], zeros[:, 0:n], pp_[:, pn:pn + 1]], [P_[:, 1:1 + n]])
        prevP[(qb, head)] = (P_, n)
        P.tt("pool", A_[:, 0:n], P_[:, 0:n], P_[:, 1:1 + n], ALU.subtract)

    def stC(it):
        qb, head, sg_, nseg, kb0, nk, n, pair, prt, b4 = geom(it)
        A_, AT_ = Ab[b4], ATb[b4]
        p4 = it % 4
        ptb = pT[:, p4 % 2, (p4 // 2) * 512:(p4 // 2 + 1) * 512]
        for j in range(nk):
            P.transpose(ptb[:, j * 128:(j + 1) * 128], A_[:, j * 128:(j + 1) * 128], ident[:], inc=(j == nk - 1))
        P.copy("act" if (it % 2) else "dve", AT_[:, 0:n], ptb[:, 0:n])

    def stE(it):
        qb, head, sg_, nseg, kb0, nk, n, pair, prt, b4 = geom(it)
        AT_ = ATb[b4]
        pob = po[:, head % 2, :]
        oq = o_qb[qb % 2]
        for j in range(nk):
            rb = (NQB - 1 - qb) + kb0 + j
            first = (sg_ == 0 and j == 0)
            last = (sg_ == nseg - 1 and j == nk - 1)
            P.mm(pob, AT_[:, j * 128:(j + 1) * 128], v_all[:, rb, head * 64:(head + 1) * 64],
                 start=first, stop=last, inc=(last or j == nk - 1))
        if sg_ == nseg - 1:
            P.copy("act", oq[:, head * 64:(head + 1) * 64], pob)
            if head == 3:
                P.dma("sp", out_d[:, qb, :], oq[:], is_output=True)

    D1, D2 = 3, 4
    NI = len(items)
    for st in range(NI + D2):
        if st < NI:
            stA(st)
            stB(st)
        if 0 <= st - D1 < NI:
            stC(st - D1)
        if 0 <= st - D2 < NI:
            stE(st - D2)
    P.emit()
    return nc


def run_Msb(x_full, mod, inp):
    if "Msb" not in _CACHE:
        _CACHE["Msb"] = build_Msb()
    nc = _CACHE["Msb"]
    layer = 3
    w = np.asarray(inp["sb_w_in"])
    ident = np.eye(128, dtype=np.float32).astype(ml_dtypes.bfloat16)
    idx = np.arange(128)
    mneg = np.where(idx[:, None] + idx[None, :] <= 127, -30000.0, 0.0).astype(np.float32).astype(ml_dtypes.bfloat16)
    xTs = [xT_layout(x_full[b]) for b in range(2)]
    xTrs = [np.ascontiguousarray(t[:, :, ::-1]) for t in xTs]
    in_maps = []
    for core in range(NCORES):
        b, hg = divmod(core, 4)
        m = mod[b, layer]
        sh1, sc1 = m[0:1024], m[1024:2048]
        pm1 = np.ascontiguousarray(np.stack([pm(sc1), pm(sh1)], 1)).astype(np.float32)
        in_maps.append({
            "xT": xTs[b], "xTr": xTrs[b], "pm1": pm1,
            "wq": wtile(w[:, hg * 256:(hg + 1) * 256]),
            "wk": wtile(w[:, 1024 + hg * 256:1024 + (hg + 1) * 256]),
            "wv": wtile(w[:, 2048 + hg * 256:2048 + (hg + 1) * 256]),
            "mneg": mneg, "ident": ident,
        })
    res = run_bass_kernel_spmd(nc, in_maps, core_ids=list(range(NCORES)))
    o_full = np.empty((2, S, 1024), ml_dtypes.bfloat16)
    for core in range(NCORES):
        b, hg = divmod(core, 4)
        o = np.asarray(res.results[core]["o"])
        o_full[b, :, hg * 256:(hg + 1) * 256] = o.transpose(1, 0, 2).reshape(S, 256)
    return oT_from_tokenmajor(o_full)


def build_Mgdn(nblk=None):
    nc = bass.Bass("TRN2", target_bir_lowering=False)
    P = Prog(nc)
    NTL = S // 128
    xT_d = P.dram("xT", [128, 8, S], F32, "ExternalInput")
    pm1_d = P.dram("pm1", [128, 2, 8], F32, "ExternalInput")
    w_d = P.dram("w", [128, 8, 1024], F32, "ExternalInput")
    wab_d = P.dram("wab", [128, 8, 4], F32, "ExternalInput")
    cw_d = P.dram("cw", [128, 6, 4], F32, "ExternalInput")
    cst_d = P.dram("cst", [128, 8, 128], F32, "ExternalInput")
    hs_d = P.dram("hs", [128, 4], F32, "ExternalInput")
    nw_d = P.dram("nw", [128, 128], F32, "ExternalInput")
    idb_d = P.dram("identb", [128, 128], BF16, "ExternalInput")
    out_d = P.dram("o", [128, NTL, 256], BF16, "ExternalOutput")

    w = P.sb("w_sb", [128, 8, 1024], BF16)
    wab = P.sb("wab_sb", [128, 8, 4])
    pm1 = P.sb("pm1_sb", [128, 2, 8])
    cw = P.sb("cw_sb", [128, 6, 4])
    cst = P.sb("cst_sb", [128, 8, 128])
    hs = P.sb("hs_sb", [128, 4])
    nw = P.sb("nw_sb", [128, 128])
    identb = P.sb("identb_sb", [128, 128], BF16)
    onesb = P.sb("onesb", [128, 128], BF16)
    negA = P.sb("negA", [128, 2])
    eps6 = P.sb("eps6", [128, 1])
    eps6q = P.sb("eps6q", [128, 1])
    one1 = P.sb("one1", [128, 1])
    ident_f, onesBD, triBD, blk0, blk1, posmask, posmaskT, negstrict = [cst[:, i, :] for i in range(8)]

    xs = [P.sb("xs%d" % i, [128, 512]) for i in range(3)]
    hT = P.sb("hT", [128, 8, 512], BF16)
    hTf = P.sb("hTf", [128, 8, 512])
    xc = [P.sb("xc%d" % i, [128, 515]) for i in range(6)]
    cv = [P.sb("cv%d" % i, [128, 512]) for i in range(2)]
    sq = [P.sb("sq%d" % i, [128, 512], BF16) for i in range(2)]
    rn = [P.sb("rn%d" % i, [128, 512]) for i in range(2)]
    qhT = [P.sb("qhT%d" % i, [128, 512], BF16) for i in range(2)]
    khT = [P.sb("khT%d" % i, [128, 512], BF16) for i in range(2)]
    vcT = [P.sb("vcT%d" % i, [128, 512], BF16) for i in range(2)]
    ktok = [P.sb("ktok%d" % i, [128, 4, 128]) for i in range(2)]
    vtok = [P.sb("vtok%d" % i, [128, 4, 128]) for i in range(2)]
    nz = [P.sb("nz%d" % i, [128, 4, 128]) for i in range(2)]
    gat = {}
    for nm in ("g", "beta", "gc", "glo", "glb0", "glb1", "egc", "edl", "egl0", "egl1", "bg", "e1"):
        gat[nm] = [P.sb("%s%d" % (nm, i), [128, 4]) for i in range(2)]
    Dg = [P.sb("Dg%d" % i, [128, 128]) for i in range(4)]
    dec = [P.sb("dec%d" % i, [128, 128]) for i in range(4)]
    decT = [P.sb("decT%d" % i, [128, 128]) for i in range(4)]
    tmpE = [P.sb("tmpE%d" % i, [128, 128]) for i in range(4)]
    Yb_h = [[[P.sb("Y%d_%d_%d" % (h, i, k), [128, 128]) for k in range(2)] for i in range(4)] for h in range(2)]
    Zb_h = [[[P.sb("Z%d_%d_%d" % (h, i, k), [128, 128]) for k in range(2)] for i in range(4)] for h in range(2)]
    Ttb_h = [[[P.sb("Tt%d_%d_%d" % (h, i, k), [128, 128]) for k in range(2)] for i in range(4)] for h in range(2)]
    Tmb_h = [[[P.sb("Tm%d_%d_%d" % (h, i, k), [128, 128]) for k in range(2)] for i in range(4)] for h in range(2)]
    Ttbf_h = [[P.sb("Ttbf%d_%d" % (h, i), [128, 128], BF16) for i in range(4)] for h in range(2)]
    vbt_h = [[P.sb("vbt%d_%d" % (h, i), [128, 128], BF16) for i in range(4)] for h in range(2)]
    kbe_h = [[P.sb("kbe%d_%d" % (h, i), [128, 128], BF16) for i in range(4)] for h in range(2)]
    kd_h = [[P.sb("kd%d_%d" % (h, i), [128, 128], BF16) for i in range(4)] for h in range(2)]
    u_h = [[P.sb("u%d_%d" % (h, i), [128, 128]) for i in range(4)] for h in range(2)]
    wTA_h = [[P.sb("wTA%d_%d" % (h, i), [128, 128], BF16) for i in range(4)] for h in range(2)]
    wTB_h = [[P.sb("wTB%d_%d" % (h, i), [128, 128], BF16) for i in range(4)] for h in range(2)]
    qkT_h = [[P.sb("qkT%d_%d" % (h, i), [128, 128], BF16) for i in range(4)] for h in range(2)]
    vn_h = [[P.sb("vn%d_%d" % (h, i), [128, 128], BF16) for i in range(2)] for h in range(2)]
    o1_h = [[P.sb("o1_%d_%d" % (h, i), [128, 128]) for i in range(2)] for h in range(2)]
    osum_h = [[P.sb("osum%d_%d" % (h, i), [128, 128]) for i in range(2)] for h in range(2)]
    osq_h = [P.sb("osq%d" % h, [128, 128]) for h in range(2)]
    ssq_h = [P.sb("ssq%d" % h, [128, 1]) for h in range(2)]
    rinv_h = [P.sb("rinv%d" % h, [128, 1]) for h in range(2)]
    St = [P.sb("S%d" % i, [128, 128]) for i in range(2)]
    Sbf = [P.sb("Sbf%d" % i, [128, 128], BF16) for i in range(2)]
    o_all = P.sb("o_all", [128, NTL, 256], BF16)

    PB = P.ps("PB", [128, 4, 512])
    PTr = P.ps("PTr", [128, 1024], BF16)
    PG = P.ps("PG", [128, 512])
    PSC = P.ps("PSC", [128, 2, 512])

    def slot(i):
        return PB[:, i // 4, (i % 4) * 128:(i % 4 + 1) * 128]

    for dst, src in ((pm1, pm1_d), (wab, wab_d), (cw, cw_d), (cst, cst_d), (hs, hs_d), (nw, nw_d), (identb, idb_d)):
        P.dma("sp", dst[:], src)
    for kc in range(8):
        P.dma("pool", w[:, kc, :], w_d[:, kc, :])
    P.memset("dve", onesb[:], 1.0)
    P.memset("dve", eps6[:], 1e-6)
    P.memset("dve", eps6q[:], 128e-6)
    P.memset("dve", one1[:], 1.0)
    for i in range(6):
        P.memset("pool", xc[i][:, 0:3], 0.0)
    for h in range(2):
        for i in range(4):
            P.memset("pool", wTA_h[h][i][:], 0.0)
            P.memset("pool", wTB_h[h][i][:], 0.0)
    for h2 in range(2):
        P.memset("dve", St[h2][:], 0.0)
        P.memset("dve", Sbf[h2][:], 0.0)
    P.ts("dve", pm1[:, 0, :], pm1[:, 0, :], 1.0, None, ALU.add)
    if nblk:
        P.memset("pool", o_all[:], 0.0)
    P.act(negA[:], hs[:, 2:4], AF.Exp)
    P.ts("dve", negA[:], negA[:], -1.0, None, ALU.mult)

    xi_ = 0
    for blk in range(nblk or (S // 512)):
        t0 = blk * 512
        for kc in range(8):
            xb = xs[xi_ % 3]
            xi_ += 1
            P.dma("sp", xb[:], xT_d[:, kc, t0:t0 + 512])
            P.act(hT[:, kc, :], xb[:], AF.Identity, bias=pm1[:, 1, kc:kc + 1], scale=pm1[:, 0, kc:kc + 1])
            P.ts("dve", hTf[:, kc, :], xb[:], pm1[:, 0, kc:kc + 1], pm1[:, 1, kc:kc + 1], ALU.mult, ALU.add)
        for tl in range(4):
            for kc in range(8):
                P.mm(PG[:, tl * 4:tl * 4 + 4], hTf[:, kc, tl * 128:(tl + 1) * 128], wab[:, kc, :], start=(kc == 0), stop=(kc == 7))
        pgv = PG[:, 0:16].rearrange("p (t c) -> p t c", c=4)
        for h2 in range(2):
            G = {k: v[h2] for k, v in gat.items()}
            P.act(G["e1"][:], pgv[:, :, h2], AF.Exp, bias=hs[:, h2:h2 + 1])
            P.act(G["e1"][:], G["e1"][:], AF.Ln, bias=one1[:])
            P.ts("dve", G["g"][:], G["e1"][:], negA[:, h2:h2 + 1], None, ALU.mult)
            P.act(G["beta"][:], pgv[:, :, 2 + h2], AF.Sigmoid)
            for nm, cm, off in (("gc", triBD, 16), ("glo", onesBD, 20), ("glb0", blk0, 24), ("glb1", blk1, 28)):
                o_ = off + h2 * 16
                P.mm(PG[:, 32 + o_ - 16:32 + o_ - 12], cm, G["g"][:], start=True, stop=True)
                P.copy("dve", G[nm][:], PG[:, 32 + o_ - 16:32 + o_ - 12])
            P.act(G["egc"][:], G["gc"][:], AF.Exp)
            P.tt("dve", G["edl"][:], G["glo"][:], G["gc"][:], ALU.subtract)
            P.act(G["edl"][:], G["edl"][:], AF.Exp)
            P.act(G["egl0"][:], G["glb0"][:], AF.Exp)
            P.act(G["egl1"][:], G["glb1"][:], AF.Exp)
            P.tt("dve", G["bg"][:], G["beta"][:], G["egc"][:], ALU.mult)
        for h2 in range(2):
            for i in range(3):
                ci = h2 * 3 + i
                pp_ = PB[:, ci % 4, :]
                c0 = h2 * 384 + i * 128
                for kc in range(8):
                    P.mm(pp_, w[:, kc, c0:c0 + 128], hT[:, kc, :], start=(kc == 0), stop=(kc == 7))
                xcb = xc[ci]
                if blk > 0:
                    P.copy("pool", xcb[:, 0:3], xcb[:, 512:515])
                P.copy("act", xcb[:, 3:515], pp_)
                cvb = cv[ci % 2]
                P.ts("dve", cvb[:], xcb[:, 0:512], cw[:, ci, 0:1], None, ALU.mult)
                for tap in range(1, 4):
                    P.stt("dve", cvb[:], xcb[:, tap:tap + 512], cw[:, ci, tap:tap + 1], cvb[:], ALU.mult, ALU.add)
                if i == 2:
                    P.act(vcT[h2][:], cvb[:], AF.Silu)
                else:
                    P.act(cvb[:], cvb[:], AF.Silu)
                    sqb, rnb = sq[ci % 2], rn[ci % 2]
                    P.act(sqb[:], cvb[:], AF.Square)
                    pn = PB[:, (ci + 2) % 4, :]
                    P.mm(pn, onesb[:], sqb[:], start=True, stop=True)
                    if i == 0:
                        P.act(rnb[:], pn, AF.Sqrt, bias=eps6q[:], scale=128.0)
                    else:
                        P.act(rnb[:], pn, AF.Sqrt, bias=eps6[:])
                    P.generic("dve", "reciprocal", (rnb[:], rnb[:]), [rnb[:]], [rnb[:]])
                    P.tt("dve", (qhT if i == 0 else khT)[h2][:], cvb[:], rnb[:], ALU.mult)
            for tl in range(4):
                tok = slice(tl * 128, (tl + 1) * 128)
                P.transpose(PTr[:, 0:128], khT[h2][:, tok], identb[:], inc=False)
                P.transpose(PTr[:, 128:256], vcT[h2][:, tok], identb[:])
                P.copy("act", ktok[h2][:, tl, :], PTr[:, 0:128])
                P.copy("dve", vtok[h2][:, tl, :], PTr[:, 128:256])
                pz_ = PB[:, tl % 4, 0:128]
                for kc in range(8):
                    P.mm(pz_, hT[:, kc, tok], w[:, kc, 768 + h2 * 128:768 + (h2 + 1) * 128], start=(kc == 0), stop=(kc == 7))
                P.act(nz[h2][:, tl, :], pz_, AF.Silu)
                P.tt("pool", nz[h2][:, tl, :], nz[h2][:, tl, :], nw[:], ALU.mult)

        for h2 in range(2):
            G = {k: v[h2] for k, v in gat.items()}
            kd, u_sb, wTA, wTB, qkT = kd_h[h2], u_h[h2], wTA_h[h2], wTB_h[h2], qkT_h[h2]
            Yb, Zb, Ttb, Tmb, Ttbf, vbt, kbe = Yb_h[h2], Zb_h[h2], Ttb_h[h2], Tmb_h[h2], Ttbf_h[h2], vbt_h[h2], kbe_h[h2]
            for tl in range(4):
                tok = slice(tl * 128, (tl + 1) * 128)
                gcc = G["gc"][:, tl:tl + 1]
                P.ts("dve", Dg[tl][:], ident_f, gcc, None, ALU.mult)
                P.mm(slot(tl), onesBD, Dg[tl][:], start=True, stop=True)
                P.stt("dve", tmpE[tl][:], slot(tl), gcc, posmask, ALU.subtract, ALU.add)
                P.act(dec[tl][:], tmpE[tl][:], AF.Exp, scale=-1.0)
                P.stt("dve", tmpE[tl][:], slot(tl), gcc, posmaskT, ALU.subtract, ALU.subtract)
                P.act(decT[tl][:], tmpE[tl][:], AF.Exp)
                P.mm(slot(4 + tl), khT[h2][:, tok], khT[h2][:, tok], start=True, stop=True)
                P.tt("dve", tmpE[tl][:], slot(4 + tl), dec[tl][:], ALU.mult)
                P.stt("dve", Zb[tl][0][:], tmpE[tl][:], G["beta"][:, tl:tl + 1], negstrict, ALU.mult, ALU.mult)
                P.mm(slot(8 + tl), khT[h2][:, tok], qhT[h2][:, tok], start=True, stop=True)
                P.tt("dve", qkT[tl][:], slot(8 + tl), decT[tl][:], ALU.mult)
                P.ts("dve", vbt[tl][:], vtok[h2][:, tl, :], G["beta"][:, tl:tl + 1], None, ALU.mult)
                P.act(kbe[tl][:], ktok[h2][:, tl, :], AF.Identity, scale=G["bg"][:, tl:tl + 1])
                P.act(kd[tl][:], ktok[h2][:, tl, :], AF.Identity, scale=G["edl"][:, tl:tl + 1])
                P.mm(slot(12 + tl), Zb[tl][0][:], ident_f, start=True, stop=True)
                P.copy("act", Yb[tl][0][:], slot(12 + tl))
                P.tt("dve", Ttb[tl][0][:], slot(12 + tl), ident_f, ALU.add)
                P.tt("pool", Tmb[tl][0][:], Zb[tl][0][:], ident_f, ALU.add)
        for st in range(1, 6):
            a_, b_ = (st - 1) % 2, st % 2
            last = (st == 5)
            for h2 in range(2):
                Yb, Zb, base = Yb_h[h2], Zb_h[h2], h2 * 8
                for tl in range(4):
                    P.mm(slot(base + tl), Zb[tl][a_][:], Yb[tl][a_][:], start=True, stop=True)
                    if not last:
                        P.mm(slot(base + 4 + tl), Yb[tl][a_][:], Zb[tl][a_][:], start=True, stop=True)
            for h2 in range(2):
                Yb, Zb, base = Yb_h[h2], Zb_h[h2], h2 * 8
                for tl in range(4):
                    P.copy("act", Yb[tl][b_][:], slot(base + tl))
                    if not last:
                        P.copy("act", Zb[tl][b_][:], slot(base + 4 + tl))
            for h2 in range(2):
                Yb, Zb, Ttb, Tmb, base = Yb_h[h2], Zb_h[h2], Ttb_h[h2], Tmb_h[h2], h2 * 8
                for tl in range(4):
                    P.mm(slot(base + tl), Tmb[tl][a_][:], Yb[tl][b_][:], start=True, stop=True)
                    if not last:
                        P.mm(slot(base + 4 + tl), Ttb[tl][a_][:], Zb[tl][b_][:], start=True, stop=True)
            for h2 in range(2):
                Ttb, Tmb, Ttbf, base = Ttb_h[h2], Tmb_h[h2], Ttbf_h[h2], h2 * 8
                for tl in range(4):
                    if last:
                        P.tt("dve", Ttbf[tl][:], slot(base + tl), Ttb[tl][a_][:], ALU.add)
                    else:
                        P.tt("dve", Ttb[tl][b_][:], slot(base + tl), Ttb[tl][a_][:], ALU.add)
                        P.tt("dve", Tmb[tl][b_][:], slot(base + 4 + tl), Tmb[tl][a_][:], ALU.add)
        for h2 in range(2):
            Ttbf, vbt, kbe, u_sb, wTA, wTB, base = Ttbf_h[h2], vbt_h[h2], kbe_h[h2], u_h[h2], wTA_h[h2], wTB_h[h2], h2 * 8
            for tl in range(4):
                P.mm(slot(base + tl), Ttbf[tl][:], vbt[tl][:], start=True, stop=True)
                P.mm(slot(base + 4 + tl), kbe[tl][:], Ttbf[tl][:], start=True, stop=True)
                P.copy("act", u_sb[tl][:], slot(base + tl))
                P.copy("dve", wTA[tl][:, 0:64], slot(base + 4 + tl)[:, 0:64])
                P.copy("act", wTB[tl][:, 64:128], slot(base + 4 + tl)[:, 64:128])
        for tl in range(4):
            tok = slice(tl * 128, (tl + 1) * 128)
            gt = blk * 4 + tl
            for j in range(2):
                pr = slice(j * 64, j * 64 + 64)
                for h2 in range(2):
                    G = {k: v[h2] for k, v in gat.items()}
                    S_, Sb_ = St[h2], Sbf[h2]
                    vnb, o1b = vn_h[h2][tl % 2], o1_h[h2][tl % 2]
                    pws = PSC[:, h2, 0:128]
                    po1 = PSC[:, h2, 128:256]
                    psu = PSC[:, h2, 256:384]
                    P.mm(pws, (wTA_h if j == 0 else wTB_h)[h2][tl][:], Sb_[:], start=True, stop=True)
                    P.mm(po1, qhT[h2][:, tok], Sb_[:], start=True, stop=True)
                    P.tt("dve", vnb[pr, :], u_h[h2][tl][pr, :], pws[pr, :], ALU.subtract)
                    P.act(o1b[pr, :], po1[pr, :], AF.Identity, scale=G["egc"][pr, tl:tl + 1])
                    P.mm(psu, kd_h[h2][tl][pr, :], vnb[pr, :], start=True, stop=True)
                    P.stt("dve", S_[:], S_[:], G["egl%d" % j][:, tl:tl + 1], psu, ALU.mult, ALU.add)
                    P.copy("act", Sb_[:], S_[:])
            for h2 in range(2):
                vnb, o1b, osb = vn_h[h2][tl % 2], o1_h[h2][tl % 2], osum_h[h2][tl % 2]
                osq, ssq, rinv = osq_h[h2], ssq_h[h2], rinv_h[h2]
                po2 = PSC[:, h2, 384:512]
                P.mm(po2, qkT_h[h2][tl][:], vnb[:], start=True, stop=True)
                P.tt("dve", osb[:], o1b[:], po2, ALU.add)
                P.tt("dve", osq[:], osb[:], osb[:], ALU.mult)
                P.generic("dve", "reduce_sum", (ssq[:], osq[:]), [osq[:]], [ssq[:]], axis=AX.X)
                P.act(rinv[:], ssq[:], AF.Sqrt, bias=eps6[:], scale=1.0 / 128.0)
                P.generic("dve", "reciprocal", (rinv[:], rinv[:]), [rinv[:]], [rinv[:]])
                P.stt("dve", o_all[:, gt, h2 * 128:(h2 + 1) * 128], osb[:], rinv[:], nz[h2][:, tl, :], ALU.mult, ALU.mult)
    for q4 in range(4):
        P.dma("sp", out_d[:, q4 * 16:(q4 + 1) * 16, :], o_all[:, q4 * 16:(q4 + 1) * 16, :], is_output=True)
    P.emit()
    return nc


def run_Mgdn(x_full, mod, inp):
    if "Mgdn" not in _CACHE:
        _CACHE["Mgdn"] = build_Mgdn()
    nc = _CACHE["Mgdn"]
    layer = 0
    wi = np.asarray(inp["gdn_w_in"])
    cwf = np.asarray(inp["gdn_conv_w"])
    p = np.arange(128)
    same = (p[:, None] // 64) == (p[None, :] // 64)
    ident = np.eye(128, dtype=np.float32)
    onesBD = same.astype(np.float32)
    triBD = (same & (p[:, None] <= p[None, :])).astype(np.float32)
    blk0 = np.broadcast_to((p[:, None] < 64), (128, 128)).astype(np.float32)
    blk1 = np.broadcast_to((p[:, None] >= 64), (128, 128)).astype(np.float32)
    posmask = np.where(same & (p[None, :] <= p[:, None]), 0.0, 1e4).astype(np.float32)
    posmaskT = np.where(same & (p[None, :] >= p[:, None]), 0.0, 1e4).astype(np.float32)
    negstrict = np.where(same & (p[None, :] < p[:, None]), -1.0, 0.0).astype(np.float32)
    cst = np.ascontiguousarray(np.stack([ident, onesBD, triBD, blk0, blk1, posmask, posmaskT, negstrict], 1))
    nwr = np.ascontiguousarray(np.broadcast_to(np.asarray(inp["gdn_norm_w"])[None, :], (128, 128))).astype(np.float32)
    identb = ident.astype(ml_dtypes.bfloat16)
    xTs = [xT_layout(x_full[b]) for b in range(2)]
    in_maps = []
    for core in range(NCORES):
        b, hp = divmod(core, 4)
        m = mod[b, layer]
        sh1, sc1 = m[0:1024], m[1024:2048]
        pm1 = np.ascontiguousarray(np.stack([pm(sc1), pm(sh1)], 1)).astype(np.float32)
        cols = []
        cws = []
        for h2 in range(2):
            hd = hp * 2 + h2
            for i in range(3):
                cols.append(wi[:, i * 1024 + hd * 128:i * 1024 + (hd + 1) * 128])
                cws.append(cwf[:, i * 1024 + hd * 128:i * 1024 + (hd + 1) * 128].T)
        for h2 in range(2):
            hd = hp * 2 + h2
            cols.append(wi[:, 3072 + hd * 128:3072 + (hd + 1) * 128])
        wcat = np.concatenate(cols, axis=1)
        h0, h1 = hp * 2, hp * 2 + 1
        wab = np.stack([wi[:, 4096 + h0], wi[:, 4096 + h1], wi[:, 4104 + h0], wi[:, 4104 + h1]], 1)
        hs = np.array([inp["gdn_dt_bias"][h0], inp["gdn_dt_bias"][h1], inp["gdn_a_log"][h0], inp["gdn_a_log"][h1]], np.float32)
        in_maps.append({
            "xT": xTs[b], "pm1": pm1, "w": wtile(wcat), "wab": wtile(wab),
            "cw": np.ascontiguousarray(np.stack(cws, 1)).astype(np.float32),
            "cst": cst, "hs": np.ascontiguousarray(np.broadcast_to(hs[None], (128, 4))), "nw": nwr, "identb": identb,
        })
    res = run_bass_kernel_spmd(nc, in_maps, core_ids=list(range(NCORES)))
    o_full = np.empty((2, S, 1024), ml_dtypes.bfloat16)
    for core in range(NCORES):
        b, hp = divmod(core, 4)
        o = np.asarray(res.results[core]["o"])
        o_full[b, :, hp * 256:(hp + 1) * 256] = o.transpose(1, 0, 2).reshape(S, 256)
    return oT_from_tokenmajor(o_full)


def kernel(**inputs):
    inp = {k: np.asarray(v) for k, v in inputs.items()}
    mod = run_C(inp)
    x = np.ascontiguousarray(inp["x"], dtype=np.float32)
    mixers = (run_Mgdn, run_Mret, run_Mgmlp, run_Msb)
    wouts = (inp["gdn_w_out"], inp["ret_w_out"], inp["gmlp_w_out"], inp["sb_w_out"])
    for layer in range(DEPTH):
        oT = mixers[layer](x, mod, inp)
        x = run_F(x, oT, layer, mod, inp, wouts[layer])
    return x.astype(np.float32)
```

```python
import contextlib
import math

import numpy as np
import ml_dtypes

import concourse.bass as bass
import concourse.mybir as mybir
from concourse.bass_utils import run_bass_kernel_spmd

F32 = mybir.dt.float32
BF16 = mybir.dt.bfloat16
AF = mybir.ActivationFunctionType
ALU = mybir.AluOpType
AX = mybir.AxisListType

NCORES = 8
D = 1024
B = 2
S = 8192
DEPTH = 4
FH = 2816
ALPHA = (2.0 * DEPTH) ** 0.25
LN_EPS = 1e-5

ENGS = ("pe", "act", "dve", "pool", "sp")


def _prod(xs):
    r = 1
    for v in xs:
        r *= int(v)
    return r


class Prog:
    N_DMA_SLOTS = 12

    def __init__(self, nc):
        self.nc = nc
        self.stack = contextlib.ExitStack()
        self.streams = {e: [] for e in ENGS}
        self.cnt = {e: 0 for e in ENGS}
        self.seen = {e: {} for e in ENGS}
        self.acc = {}
        self.esem = {e: self.stack.enter_context(nc.semaphore("sem_" + e)) for e in ENGS}
        self.semobj = {("e", e): self.esem[e] for e in ENGS}
        self.dma_slots = {}
        self.dma_next = {}
        for q in ("sp", "pool", "act"):
            self.dma_slots[q] = []
            self.dma_next[q] = 0
        self.out_dmas = []
        self.psum_names = set()

    def sb(self, name, shape, dtype=F32):
        return self.stack.enter_context(self.nc.sbuf_tensor(name, list(shape), dtype))

    def ps(self, name, shape, dtype=F32):
        self.psum_names.add(name)
        return self.stack.enter_context(self.nc.psum_tensor(name, list(shape), dtype))

    def dram(self, name, shape, dtype, kind):
        return self.nc.dram_tensor(name, list(shape), dtype, kind=kind).ap()

    @staticmethod
    def region(ap):
        t = ap.tensor
        pairs = [(int(s), int(c)) for s, c in ap.ap]
        off = int(ap.offset)
        kind = type(t).__name__
        if kind.startswith("DRam"):
            ext = sum((c - 1) * abs(s) for s, c in pairs)
            return (t.name, 0, 0, off, off + ext)
        fsz = _prod(t.shape[1:])
        p0, f0 = divmod(off, fsz)
        ps_, pc = pairs[0]
        pstep = ps_ // fsz if ps_ else 0
        ext = sum((c - 1) * abs(s) for s, c in pairs[1:])
        f1 = f0 + ext
        if kind.startswith("PSum"):
            epb = 2048 // (2 if t.dtype == BF16 else 4)
            f0 = (f0 // epb) * epb
            f1 = (f1 // epb) * epb + epb - 1
        return (t.name, p0, p0 + (pc - 1) * pstep, f0, f1)

    def _deps(self, eng, reads, writes, rec_key):
        need = {}

        def add(k, v):
            if need.get(k, 0) < v:
                need[k] = v

        rr = [self.region(a) for a in reads]
        ww = [self.region(a) for a in writes]
        for (name, pl, ph, fl, fh) in rr:
            ps_rar = name in self.psum_names
            for rec in self.acc.get(name, ()):
                if (rec[5] or (ps_rar and rec[4] != eng)) and not (rec[1] < pl or rec[0] > ph or rec[3] < fl or rec[2] > fh):
                    add(rec[6], rec[7])
        for (name, pl, ph, fl, fh) in ww:
            for rec in self.acc.get(name, ()):
                if not (rec[1] < pl or rec[0] > ph or rec[3] < fl or rec[2] > fh):
                    add(rec[6], rec[7])
        for (name, pl, ph, fl, fh) in ww:
            lst = self.acc.setdefault(name, [])
            lst[:] = [r for r in lst if not (r[0] >= pl and r[1] <= ph and r[2] >= fl and r[3] <= fh)]
            lst.append((pl, ph, fl, fh, eng, True, rec_key[0], rec_key[1]))
        for (name, pl, ph, fl, fh) in rr:
            lst = self.acc.setdefault(name, [])
            lst[:] = [r for r in lst if not ((not r[5]) and r[6] == rec_key[0]
                                             and r[0] >= pl and r[1] <= ph and r[2] >= fl and r[3] <= fh)]
            lst.append((pl, ph, fl, fh, eng, False, rec_key[0], rec_key[1]))
        waits = []
        own = ("e", eng)
        for k, v in need.items():
            if k == own:
                if eng == "pe" or v > self.cnt[eng]:
                    continue
            if self.seen[eng].get(k, 0) >= v:
                continue
            self.seen[eng][k] = v
            waits.append((k, v))
        return waits

    def op(self, eng, name, args, kwargs, reads, writes, inc=True):
        own = ("e", eng)
        val = self.cnt[eng] + 1
        waits = self._deps(eng, reads, writes, (own, val))
        if inc:
            self.cnt[eng] = val
        self.streams[eng].append((name, args, kwargs, waits, own if inc else None, 1))

    def dma(self, q, out, in_, is_output=False, **kwargs):
        slots = self.dma_slots[q]
        if len(slots) < self.N_DMA_SLOTS:
            key = ("d", q, len(slots))
            sem = self.stack.enter_context(self.nc.semaphore("dsem_%s_%d" % (q, len(slots))))
            self.semobj[key] = sem
            slots.append([key, 0])
            slot = slots[-1]
        else:
            slot = slots[self.dma_next[q] % self.N_DMA_SLOTS]
        self.dma_next[q] += 1
        key, uses = slot
        val = 16 * (uses + 1)
        waits = self._deps(q, [in_], [out], (key, val))
        if uses > 0 and self.seen[q].get(key, 0) < 16 * uses:
            self.seen[q][key] = 16 * uses
            waits.append((key, 16 * uses))
        slot[1] = uses + 1
        self.streams[q].append(("dma_start", (), dict(out=out, in_=in_, **kwargs), waits, key, 16))
        if is_output:
            self.out_dmas.append((q, key, val))

    def collective(self, kind, out, in_, groups, amt=16, op=None):
        q = "pool"
        key = ("c", len(self.semobj))
        sem = self.stack.enter_context(self.nc.semaphore("csem_%d" % len(self.semobj)))
        self.semobj[key] = sem
        waits = self._deps(q, [in_], [out], (key, amt))
        self.streams[q].append(("collective_compute", (kind, op if op is not None else ALU.bypass),
                                dict(replica_groups=groups, ins=[in_], outs=[out]), waits, key, amt))

    def mm(self, out, lhsT, rhs, start=True, stop=True, inc=None):
        if inc is None:
            inc = stop
        self.op("pe", "matmul", (out, lhsT, rhs), dict(start=start, stop=stop), [lhsT, rhs], [out], inc=inc)

    def transpose(self, out, in_, ident, inc=True):
        self.op("pe", "transpose", (out, in_, ident), {}, [in_, ident], [out], inc=inc)

    def act(self, out, in_, func, bias=None, scale=None, accum_out=None, eng="act"):
        kw = {}
        reads = [in_]
        writes = [out]
        if bias is not None:
            kw["bias"] = bias
            if not isinstance(bias, (int, float)):
                reads.append(bias)
        if scale is not None:
            kw["scale"] = scale
            if not isinstance(scale, (int, float)):
                reads.append(scale)
        if accum_out is not None:
            kw["accum_out"] = accum_out
            writes.append(accum_out)
        self.op(eng, "activation", (out, in_, func), kw, reads, writes)

    def tt(self, eng, out, in0, in1, op):
        self.op(eng, "tensor_tensor", (out, in0, in1, op), {}, [in0, in1], [out])

    def ts(self, eng, out, in0, s1, s2, op0, op1=None, accum_out=None):
        reads = [in0]
        for s in (s1, s2):
            if s is not None and not isinstance(s, (int, float)):
                reads.append(s)
        kw = {}
        writes = [out]
        if accum_out is not None:
            kw["accum_out"] = accum_out
            writes.append(accum_out)
        if op1 is None:
            self.op(eng, "tensor_scalar", (out, in0, s1, None, op0), kw, reads, writes)
        else:
            self.op(eng, "tensor_scalar", (out, in0, s1, s2, op0, op1), kw, reads, writes)

    def stt(self, eng, out, in0, scalar, in1, op0, op1):
        reads = [in0, in1]
        if not isinstance(scalar, (int, float)):
            reads.append(scalar)
        self.op(eng, "scalar_tensor_tensor", (out, in0, scalar, in1, op0, op1), {}, reads, [out])

    def copy(self, eng, out, in_):
        if eng == "act":
            self.op(eng, "copy", (out, in_), {}, [in_], [out])
        else:
            self.op(eng, "tensor_copy", (out, in_), {}, [in_], [out])

    def memset(self, eng, ap, val):
        self.op(eng, "memset", (ap, val), {}, [], [ap])

    def generic(self, eng, name, args, reads, writes, **kwargs):
        self.op(eng, name, tuple(args), kwargs, reads, writes)

    def check(self):
        counts = {}
        ptr = {e: 0 for e in ENGS}
        while True:
            progress = False
            for e in ENGS:
                st = self.streams[e]
                while ptr[e] < len(st):
                    name, args, kwargs, waits, inc, amt = st[ptr[e]]
                    if any(counts.get(k, 0) < v for k, v in waits):
                        break
                    if inc is not None:
                        counts[inc] = counts.get(inc, 0) + amt
                    ptr[e] += 1
                    progress = True
            if all(ptr[e] == len(self.streams[e]) for e in ENGS):
                return
            if not progress:
                msg = []
                for e in ENGS:
                    if ptr[e] < len(self.streams[e]):
                        name, args, kwargs, waits, inc, amt = self.streams[e][ptr[e]]
                        bad = [(k, v, counts.get(k, 0)) for k, v in waits if counts.get(k, 0) < v]
                        msg.append("%s stuck at %d/%d (%s) waiting %s" % (e, ptr[e], len(self.streams[e]), name, bad))
                raise RuntimeError("DEADLOCK in recorded program:\n" + "\n".join(msg))

    def emit(self):
        nc = self.nc
        self.check()
        tail = {}
        for q, key, val in self.out_dmas:
            d = tail.setdefault(q, {})
            d[key] = max(d.get(key, 0), val)

        def replay(e, eng):
            for (name, args, kwargs, waits, inc, amt) in self.streams[eng]:
                for k, v in waits:
                    e.wait_ge(self.semobj[k], v)
                ins = getattr(e, name)(*args, **kwargs)
                if inc is not None:
                    ins.then_inc(self.semobj[inc], amt)
            for k, v in tail.get(eng, {}).items():
                e.wait_ge(self.semobj[k], v)

        with nc.Block() as block:
            @block.tensor
            def _(e):
                replay(e, "pe")

            @block.scalar
            def _(e):
                replay(e, "act")

            @block.vector
            def _(e):
                replay(e, "dve")

            @block.gpsimd
            def _(e):
                replay(e, "pool")

            @block.sync
            def _(e):
                replay(e, "sp")
        self.stack.close()

    def stats(self):
        return {e: len(self.streams[e]) for e in ENGS}


def build_C():
    nc = bass.Bass("TRN2", target_bir_lowering=False)
    P = Prog(nc)
    cT_d = P.dram("cT", [128, 8, 2], F32, "ExternalInput")
    cw_d = P.dram("cond_w", [128, 8, 1024], F32, "ExternalInput")
    cb_d = P.dram("cond_b", [128, 8], F32, "ExternalInput")
    aw_d = P.dram("ada_w", [128, 8, 3072], F32, "ExternalInput")
    ab_d = P.dram("ada_b", [128, 24], F32, "ExternalInput")
    out_d = P.dram("modpm", [128, 24, 2], F32, "ExternalOutput")

    cT = P.sb("cT_sb", [128, 8, 2])
    cw = P.sb("cw_sb", [128, 8, 1024])
    cb = P.sb("cb_sb", [128, 8])
    aw = P.sb("aw_sb", [128, 8, 3072])
    ab = P.sb("ab_sb", [128, 24])
    eT = P.sb("eT_sb", [128, 8, 2])
    mo = P.sb("mo_sb", [128, 24, 2])
    e_ps = P.ps("e_ps", [128, 8, 2])
    m_ps = P.ps("m_ps", [128, 24, 2])

    P.dma("sp", cT[:], cT_d)
    P.dma("sp", cb[:], cb_d)
    P.dma("sp", ab[:], ab_d)
    for kc in range(8):
        P.dma("sp", cw[:, kc, :], cw_d[:, kc, :])
    for kc in range(8):
        P.dma("sp", aw[:, kc, :], aw_d[:, kc, :])
    for jc in range(8):
        for kc in range(8):
            P.mm(e_ps[:, jc, :], cw[:, kc, jc * 128:(jc + 1) * 128], cT[:, kc, :], start=(kc == 0), stop=(kc == 7))
        P.act(eT[:, jc, :], e_ps[:, jc, :], AF.Silu, bias=cb[:, jc:jc + 1])
    for jc in range(24):
        for kc in range(8):
            P.mm(m_ps[:, jc, :], aw[:, kc, jc * 128:(jc + 1) * 128], eT[:, kc, :], start=(kc == 0), stop=(kc == 7))
        P.act(mo[:, jc, :], m_ps[:, jc, :], AF.Identity, bias=ab[:, jc:jc + 1])
    P.dma("sp", out_d, mo[:], is_output=True)
    P.emit()
    return nc


def pm(v, n=128):
    v = np.asarray(v)
    k = v.shape[-1] // n
    return np.ascontiguousarray(np.moveaxis(v.reshape(v.shape[:-1] + (k, n)), -1, 0))


def wtile(w):
    w = np.asarray(w)
    K, N = w.shape
    return np.ascontiguousarray(w.reshape(K // 128, 128, N).transpose(1, 0, 2))


def run_C(inp):
    nc = build_C()
    ada_flat = np.asarray(inp["ada_w"]).transpose(1, 0, 2).reshape(1024, 4 * 6144)
    adab_flat = np.asarray(inp["ada_b"]).reshape(4 * 6144)
    cT = pm(inp["c"])
    cT = np.ascontiguousarray(cT.transpose(0, 2, 1))
    cw = wtile(inp["cond_w"])
    cb = pm(inp["cond_b"])
    in_maps = []
    for core in range(NCORES):
        sl = slice(core * 3072, (core + 1) * 3072)
        in_maps.append({
            "cT": cT, "cond_w": cw, "cond_b": cb,
            "ada_w": wtile(ada_flat[:, sl]),
            "ada_b": pm(adab_flat[sl]),
        })
    res = run_bass_kernel_spmd(nc, in_maps, core_ids=list(range(NCORES)))
    mod = np.zeros((2, 4 * 6144), np.float32)
    for core in range(NCORES):
        o = np.asarray(res.results[core]["modpm"])
        mod[:, core * 3072:(core + 1) * 3072] = o.transpose(2, 1, 0).reshape(2, 3072)
    return mod.reshape(2, 4, 6144)


NT = 16
TBT = 8
GH = 2
NG = 22 // GH


def ln_tile(P, y_ps, x_in, Gt, lng, lnb, x_out, xhat_bf, sc):
    tmp, stats, mv, rstd, xhat = sc["tmp"], sc["stats"], sc["mv"], sc["rstd"], sc["xhat"]
    P.tt("dve", tmp[:], y_ps, Gt, ALU.mult)
    P.stt("dve", tmp[:], x_in, ALPHA, tmp[:], ALU.mult, ALU.add)
    for h in range(2):
        P.generic("dve", "bn_stats", (stats[:, h, :], tmp[:, h * 512:(h + 1) * 512]),
                  [tmp[:, h * 512:(h + 1) * 512]], [stats[:, h, :]])
    P.generic("dve", "bn_aggr", (mv[:], stats[:].rearrange("p a b -> p (a b)")), [stats[:]], [mv[:]])
    P.act(rstd[:], mv[:, 1:2], AF.Sqrt, bias=sc["eps"][:])
    P.generic("dve", "reciprocal", (rstd[:], rstd[:]), [rstd[:]], [rstd[:]])
    P.ts("dve", xhat[:], tmp[:], mv[:, 0:1], rstd[:], ALU.subtract, ALU.mult)
    if xhat_bf is not None:
        P.copy("act", xhat_bf, xhat[:])
    P.tt("dve", x_out, xhat[:], lng, ALU.mult)
    P.tt("dve", x_out, x_out, lnb, ALU.add)


def build_F(W):
    KO = W // 128
    nc = bass.Bass("TRN2", target_bir_lowering=False)
    P = Prog(nc)
    NTT = NT + 1
    x_d = P.dram("xin", [NTT * 128, 1024], F32, "ExternalInput")
    oT_d = P.dram("oT", [128, KO, NTT * 128], BF16, "ExternalInput")
    wo_d = P.dram("w_out", [128, KO * 1024], F32, "ExternalInput")
    wu_d = P.dram("w_up", [NG, 128, 8 * 2 * GH * 128], F32, "ExternalInput")
    wd_d = P.dram("w_dn", [NG, 128, GH * 1024], F32, "ExternalInput")
    cw_d = P.dram("conv_w", [128, 22, 3], F32, "ExternalInput")
    cb_d = P.dram("conv_b", [128, 22], F32, "ExternalInput")
    rows_d = P.dram("rows", [128, 6, 1024], F32, "ExternalInput")
    pmv_d = P.dram("pmv", [128, 4, 8], F32, "ExternalInput")
    flag_d = P.dram("flag", [128, 1], F32, "ExternalInput")
    id_d = P.dram("ident", [128, 128], BF16, "ExternalInput")
    out_d = P.dram("xout", [NT * 128, 1024], F32, "ExternalOutput")

    big = P.sb("big", [128, 22 * 1024], BF16)
    wup = [P.sb("wup%d" % i, [128, 8, 2, GH * 128], BF16) for i in range(3)]
    wdn = [P.sb("wdn%d" % i, [128, GH, 1024], BF16) for i in range(3)]
    x1 = P.sb("x1", [128, TBT, 1024])
    xin = [P.sb("xin%d" % i, [128, 1024]) for i in range(2)]
    oT = [P.sb("oT%d" % i, [128, KO, 128], BF16) for i in range(3)]
    hT = P.sb("hT", [128, 8, TBT * 128], BF16)
    hTs = P.sb("hTs", [128, 8, 128], BF16)
    rows = P.sb("rows_sb", [128, 6, 1024])
    pmv = P.sb("pmv_sb", [128, 4, 8])
    A2 = P.sb("A2", [128, 8])
    B2 = P.sb("B2", [128, 8])
    cw = P.sb("cw_sb", [128, 22, 3])
    cb = P.sb("cb_sb", [128, 22])
    flag = P.sb("flag_sb", [128, 1])
    ident = P.sb("ident_sb", [128, 128], BF16)
    sc = dict(tmp=P.sb("tmp", [128, 1024]), stats=P.sb("stats", [128, 2, 6]), mv=P.sb("mv", [128, 2]),
              rstd=P.sb("rstd", [128, 1]), xhat=P.sb("xhat", [128, 1024]), eps=P.sb("eps", [128, 1]))
    P.memset("dve", sc["eps"][:], LN_EPS)
    xhb = [P.sb("xhb%d" % i, [128, 1024], BF16) for i in range(4)]
    xo = [P.sb("xo%d" % i, [128, 1024]) for i in range(2)]
    gsb = [P.sb("gsb%d" % i, [128, 2 + 512]) for i in range(2)]
    gc = [P.sb("gc%d" % i, [128, 512]) for i in range(2)]
    gs_rot = [0]
    ghalo = P.sb("ghalo", [128, 22, 2])
    PA = P.ps("PA", [128, 6, 512])
    PT = P.ps("PT", [128, 2, 1024], BF16)

    G1, G2 = rows[:, 0, :], rows[:, 1, :]
    lng0, lnb0, lng1, lnb1 = rows[:, 2, :], rows[:, 3, :], rows[:, 4, :], rows[:, 5, :]

    P.dma("sp", rows[:], rows_d)
    P.dma("sp", pmv[:], pmv_d)
    P.dma("sp", cw[:], cw_d)
    P.dma("sp", cb[:], cb_d)
    P.dma("sp", flag[:], flag_d)
    P.dma("sp", ident[:], id_d)
    P.ts("dve", rows[:, 0:2, :], rows[:, 0:2, :], 1.0, None, ALU.add)
    P.ts("dve", pmv[:, 0, :], pmv[:, 0, :], 1.0, None, ALU.add)
    P.tt("dve", A2[:], pmv[:, 2, :], pmv[:, 0, :], ALU.mult)
    P.tt("dve", B2[:], pmv[:, 3, :], pmv[:, 0, :], ALU.mult)
    P.tt("dve", B2[:], B2[:], pmv[:, 1, :], ALU.add)

    ps_rot = [0]

    def next_y():
        i = ps_rot[0] % 3
        ps_rot[0] += 1
        return PA[:, 2 * i:2 * i + 2, :].rearrange("p a b -> p (a b)"), (PA[:, 2 * i, :], PA[:, 2 * i + 1, :])

    wout_v = big[:, 0:KO * 1024].rearrange("p (k n) -> p k n", k=KO)
    uT = big[:].rearrange("p (j t) -> p j t", j=22)

    xin_rot = [0]
    oT_rot = [0]
    xhb_rot = [0]
    pt_rot = [0]

    def phase_A(tiles, dst_x1, dst_hT):
        n = len(tiles)
        xh_list = []
        for i, gt in enumerate(tiles):
            ob = oT[oT_rot[0] % 3]
            oT_rot[0] += 1
            P.dma("sp", ob[:], oT_d[:, :, gt * 128:(gt + 1) * 128])
            xb = xin[xin_rot[0] % 2]
            xin_rot[0] += 1
            P.dma("sp", xb[:], x_d[gt * 128:(gt + 1) * 128, :])
            yfull, (y0, y1) = next_y()
            oc = 0
            for kc in range(KO):
                P.mm(y0, ob[:, kc, oc:oc + 128], wout_v[:, kc, 0:512], start=(kc == 0), stop=(kc == KO - 1), inc=False)
                P.mm(y1, ob[:, kc, oc:oc + 128], wout_v[:, kc, 512:1024], start=(kc == 0), stop=(kc == KO - 1),
                     inc=(kc == KO - 1))
            xh = xhb[xhb_rot[0] % 4]
            xhb_rot[0] += 1
            d = dst_x1(i)
            if d is None:
                d = xo[0][:]
            ln_tile(P, yfull, xb[:], G1, lng0, lnb0, d, xh[:], sc)
            xh_list.append(xh)
            if len(xh_list) == 4 or i == n - 1:
                m = len(xh_list)
                base = (i - m + 1) * 128
                for j in range(8):
                    pt = PT[:, pt_rot[0] % 2, :]
                    pt_rot[0] += 1
                    for q in range(m):
                        P.transpose(pt[:, q * 128:(q + 1) * 128], xh_list[q][:, j * 128:(j + 1) * 128], ident[:],
                                    inc=(q == m - 1))
                    P.act(dst_hT[:, j, base:base + m * 128], pt[:, 0:m * 128], AF.Identity,
                          bias=B2[:, j:j + 1], scale=A2[:, j:j + 1])
                xh_list = []

    wu_i = [0]
    wd_i = [0]

    def load_wup(g):
        b = wup[wu_i[0] % 3]
        wu_i[0] += 1
        P.dma("pool", b[:].rearrange("p a b c -> p (a b c)"), wu_d[g])
        return b

    def load_wdn(g):
        b = wdn[wd_i[0] % 3]
        wd_i[0] += 1
        P.dma("pool", b[:].rearrange("p a b -> p (a b)"), wd_d[g])
        return b

    for tb in range(NT // TBT):
        for k4 in range(0, KO, 4):
            P.dma("pool", big[:, k4 * 1024:(k4 + 4) * 1024], wo_d[:, k4 * 1024:(k4 + 4) * 1024])
        if tb == 0:
            phase_A([0], lambda i: None, hTs)
        phase_A([1 + tb * TBT + i for i in range(TBT)], lambda i: x1[:, i, :], hT)

        wq = [load_wup(0), load_wup(1)]
        pb = 0
        for g in range(NG):
            if g + 2 < NG:
                wq.append(load_wup(g + 2))
            wb = wq[g]
            for jj in range(GH):
                j = g * GH + jj
                gs_prev = None
                for half in range(2):
                    gs = gsb[gs_rot[0] % 2]
                    g_c = gc[gs_rot[0] % 2]
                    gs_rot[0] += 1
                    if half == 1:
                        P.copy("pool", gs[:, 0:2], gs_prev[:, 512:514])
                    elif tb == 0:
                        hp = PA[:, (pb % 3) * 2, 0:2]
                        for kc in range(8):
                            P.mm(hp, wb[:, kc, 0, jj * 128:(jj + 1) * 128], hTs[:, kc, 126:128], start=(kc == 0), stop=(kc == 7))
                        P.ts("dve", gs[:, 0:2], hp, flag[:, 0:1], None, ALU.mult)
                        pb += 1
                    else:
                        P.copy("pool", gs[:, 0:2], ghalo[:, j, :])
                    i3 = pb % 3
                    pb += 1
                    gp, up = PA[:, 2 * i3, :], PA[:, 2 * i3 + 1, :]
                    tok = slice(half * 512, (half + 1) * 512)
                    for kc in range(8):
                        P.mm(gp, wb[:, kc, 0, jj * 128:(jj + 1) * 128], hT[:, kc, tok], start=(kc == 0), stop=(kc == 7))
                    for kc in range(8):
                        P.mm(up, wb[:, kc, 1, jj * 128:(jj + 1) * 128], hT[:, kc, tok], start=(kc == 0), stop=(kc == 7))
                    P.copy("act", gs[:, 2:514], gp)
                    P.ts("dve", g_c[:], gs[:, 0:512], cw[:, j, 0:1], cb[:, j:j + 1], ALU.mult, ALU.add)
                    P.stt("dve", g_c[:], gs[:, 1:513], cw[:, j, 1:2], g_c[:], ALU.mult, ALU.add)
                    P.stt("dve", g_c[:], gs[:, 2:514], cw[:, j, 2:3], g_c[:], ALU.mult, ALU.add)
                    P.act(g_c[:], g_c[:], AF.Silu)
                    P.tt("dve", uT[:, j, tok], g_c[:], up, ALU.mult)
                    gs_prev = gs
                if tb + 1 < NT // TBT:
                    P.copy("pool", ghalo[:, j, :], gs_prev[:, 512:514])

        dq = [load_wdn(0), load_wdn(1)]
        t0 = 0
        while t0 < TBT:
            tl = list(range(t0, min(t0 + 3, TBT)))
            ys = [next_y() for _ in tl]
            for g in range(NG):
                if g + 2 < NG:
                    dq.append(load_wdn(g + 2))
                elif t0 + 3 < TBT:
                    dq.append(load_wdn(g + 2 - NG))
                db = dq.pop(0)
                for jj in range(GH):
                    j = g * GH + jj
                    for ti, t in enumerate(tl):
                        y0, y1 = ys[ti][1]
                        last = (j == 21)
                        P.mm(y0, uT[:, j, t * 128:(t + 1) * 128], db[:, jj, 0:512], start=(j == 0), stop=last, inc=False)
                        P.mm(y1, uT[:, j, t * 128:(t + 1) * 128], db[:, jj, 512:1024], start=(j == 0), stop=last,
                             inc=(last or (jj == GH - 1 and ti == len(tl) - 1)))
            for ti, t in enumerate(tl):
                xb = xo[(tb * TBT + t) % 2]
                ln_tile(P, ys[ti][0], x1[:, t, :], G2, lng1, lnb1, xb[:], None, sc)
                gt = tb * TBT + t
                P.dma("sp", out_d[gt * 128:(gt + 1) * 128, :], xb[:], is_output=True)
            t0 += 3
    P.emit()
    return nc


_CACHE = {}


def bf16(a):
    return np.asarray(a).astype(ml_dtypes.bfloat16)


def run_F(x_full, oT_cores, layer, mod, inp, w_out):
    W = w_out.shape[0]
    KO = W // 128
    key = ("F", W)
    if key not in _CACHE:
        _CACHE[key] = build_F(W)
    nc = _CACHE[key]
    wup = np.asarray(inp["ffn_up"][layer])
    wt = wtile(wup)
    wu_l = np.empty((NG, 128, 8, 2, GH * 128), np.float32)
    for g in range(NG):
        wu_l[g, :, :, 0, :] = wt[:, :, g * GH * 128:(g + 1) * GH * 128]
        wu_l[g, :, :, 1, :] = wt[:, :, FH + g * GH * 128:FH + (g + 1) * GH * 128]
    wu_l = wu_l.reshape(NG, 128, 8 * 2 * GH * 128)
    wd = wtile(inp["ffn_down"][layer])
    wd_l = np.ascontiguousarray(wd.reshape(128, NG, GH * 1024).transpose(1, 0, 2))
    wo_l = wtile(w_out).reshape(128, KO * 1024)
    cw = np.ascontiguousarray(pm(inp["ffn_conv_w"][layer]).transpose(0, 2, 1))
    cb = pm(inp["ffn_conv_b"][layer])
    ident = np.eye(128, dtype=np.float32).astype(ml_dtypes.bfloat16)
    lng = np.asarray(inp["ln_g"][layer])
    lnb = np.asarray(inp["ln_b"][layer])
    in_maps = []
    for core in range(NCORES):
        b, q = divmod(core, 4)
        m = mod[b, layer]
        g1, sh2, sc2, g2 = m[2048:3072], m[3072:4096], m[4096:5120], m[5120:6144]
        rows = np.stack([g1, g2, lng[0], lnb[0], lng[1], lnb[1]], 0)
        rows = np.ascontiguousarray(np.broadcast_to(rows[None], (128, 6, 1024))).astype(np.float32)
        pmv = np.ascontiguousarray(np.stack([pm(sc2), pm(sh2), pm(lng[0]), pm(lnb[0])], 1)).astype(np.float32)
        t0 = q * 2048
        xin = np.zeros((17 * 128, 1024), np.float32)
        if q > 0:
            xin[:] = x_full[b, t0 - 128:t0 + 2048]
        else:
            xin[128:] = x_full[b, 0:2048]
        in_maps.append({
            "xin": xin, "oT": oT_cores[core], "w_out": wo_l, "w_up": wu_l, "w_dn": wd_l,
            "conv_w": cw, "conv_b": cb, "rows": rows, "pmv": pmv,
            "flag": np.full((128, 1), 0.0 if q == 0 else 1.0, np.float32), "ident": ident,
        })
    res = run_bass_kernel_spmd(nc, in_maps, core_ids=list(range(NCORES)))
    out = np.empty((2, 8192, 1024), np.float32)
    for core in range(NCORES):
        b, q = divmod(core, 4)
        out[b, q * 2048:(q + 1) * 2048] = np.asarray(res.results[core]["xout"])
    return out


def oT_from_tokenmajor(o_full):
    W = o_full.shape[-1]
    KO = W // 128
    outs = []
    for core in range(NCORES):
        b, q = divmod(core, 4)
        t0 = q * 2048
        seg = np.zeros((17 * 128, W), ml_dtypes.bfloat16)
        if q > 0:
            seg[:] = o_full[b, t0 - 128:t0 + 2048]
        else:
            seg[128:] = o_full[b, 0:2048]
        outs.append(np.ascontiguousarray(seg.T.reshape(KO, 128, 17 * 128).transpose(1, 0, 2)))
    return outs


def build_Mgmlp():
    nc = bass.Bass("TRN2", target_bir_lowering=False)
    P = Prog(nc)
    TOK = 2048
    xT_d = P.dram("xT", [128, 8, TOK], F32, "ExternalInput")
    pm1_d = P.dram("pm1", [128, 2, 8], F32, "ExternalInput")
    win_d = P.dram("w_in", [128, 8, 4096], F32, "ExternalInput")
    rows_d = P.dram("rows", [128, 2, 2048], F32, "ExternalInput")
    wsT_d = P.dram("wsT", [128, 8, 128], F32, "ExternalInput")
    mask_d = P.dram("mask", [128, 128], F32, "ExternalInput")
    bs_d = P.dram("bs", [1, 1024], F32, "ExternalInput")
    out_d = P.dram("oT", [128, 16, TOK], BF16, "ExternalOutput")

    wu = P.sb("wu", [128, 8, 2048], BF16)
    wv = P.sb("wv", [128, 8, 2048], BF16)
    rows = P.sb("rows_sb", [128, 2, 2048])
    pm1 = P.sb("pm1_sb", [128, 2, 8])
    wsT = P.sb("wsT_sb", [128, 8, 128])
    mask = P.sb("mask_sb", [128, 128])
    wsTm = P.sb("wsTm", [128, 8, 128], BF16)
    bs = P.sb("bs_sb", [1, 1024])
    ones1 = P.sb("ones1", [1, 128])
    eps = P.sb("eps", [128, 1])
    xs = [P.sb("xs%d" % i, [128, 512]) for i in range(3)]
    hT = P.sb("hT", [128, 8, 512], BF16)
    uT = P.sb("uT", [128, 16, 512])
    vsb = [P.sb("vsb%d" % i, [128, 2048]) for i in range(2)]
    vln = [P.sb("vln%d" % i, [128, 2048], BF16) for i in range(2)]
    uvb = [P.sb("uvb%d" % i, [128, 16, 512], BF16) for i in range(2)]
    stats = P.sb("stats", [128, 4, 6])
    mv = P.sb("mv", [128, 2])
    rstd = P.sb("rstd", [128, 1])
    pu = P.ps("pu", [128, 2, 512])
    pv = P.ps("pv", [128, 2, 512])
    psp = P.ps("psp", [128, 16, 128])

    P.dma("sp", pm1[:], pm1_d)
    P.dma("sp", wsT[:], wsT_d)
    P.dma("sp", mask[:], mask_d)
    P.dma("sp", bs[:], bs_d)
    P.dma("sp", rows[:], rows_d)
    P.memset("dve", ones1[:], 1.0)
    P.memset("dve", eps[:], LN_EPS)
    P.ts("dve", pm1[:, 0, :], pm1[:, 0, :], 1.0, None, ALU.add)
    for g in range(8):
        P.tt("dve", wsTm[:, g, :], wsT[:, g, :], mask[:], ALU.mult)
    for kc in range(8):
        P.dma("pool", wu[:, kc, :], win_d[:, kc, 0:2048])
    for kc in range(8):
        P.dma("pool", wv[:, kc, :], win_d[:, kc, 2048:4096])

    xi = 0
    vi = 0
    for blk in range(TOK // 512):
        t0 = blk * 512
        for kc in range(8):
            xb = xs[xi % 3]
            xi += 1
            P.dma("sp", xb[:], xT_d[:, kc, t0:t0 + 512])
            P.act(hT[:, kc, :], xb[:], AF.Identity, bias=pm1[:, 1, kc:kc + 1], scale=pm1[:, 0, kc:kc + 1])
        for uc in range(16):
            pt = pu[:, uc % 2, :]
            for kc in range(8):
                P.mm(pt, wu[:, kc, uc * 128:(uc + 1) * 128], hT[:, kc, :], start=(kc == 0), stop=(kc == 7))
            P.act(uT[:, uc, :], pt, AF.Gelu)
        ob = uvb[blk % 2]
        for ch in range(4):
            tok = slice(ch * 128, (ch + 1) * 128)
            vb = vsb[vi % 2]
            vl = vln[vi % 2]
            vi += 1
            for half in range(2):
                for q in range(2):
                    c0 = (half * 2 + q) * 512
                    for kc in range(8):
                        P.mm(pv[:, q, :], hT[:, kc, tok], wv[:, kc, c0:c0 + 512], start=(kc == 0), stop=(kc == 7))
                P.act(vb[:, half * 1024:(half + 1) * 1024], pv[:].rearrange("p a b -> p (a b)"), AF.Gelu)
            for q in range(4):
                P.generic("dve", "bn_stats", (stats[:, q, :], vb[:, q * 512:(q + 1) * 512]),
                          [vb[:, q * 512:(q + 1) * 512]], [stats[:, q, :]])
            P.generic("dve", "bn_aggr", (mv[:], stats[:].rearrange("p a b -> p (a b)")), [stats[:]], [mv[:]])
            P.act(rstd[:], mv[:, 1:2], AF.Sqrt, bias=eps[:])
            P.generic("dve", "reciprocal", (rstd[:], rstd[:]), [rstd[:]], [rstd[:]])
            P.ts("dve", vb[:], vb[:], mv[:, 0:1], rstd[:], ALU.subtract, ALU.mult)
            P.tt("pool", vb[:], vb[:], rows[:, 0, :], ALU.mult)
            P.tt("pool", vl[:], vb[:], rows[:, 1, :], ALU.add)
            for cc in range(16):
                g = cc // 2
                P.mm(psp[:, cc, :], vl[:, cc * 128:(cc + 1) * 128], wsTm[:, g, :], start=True, stop=False, inc=False)
                P.mm(psp[:, cc, :], ones1[0:1, :], bs[0:1, g * 128:(g + 1) * 128], start=False, stop=True,
                     inc=(cc % 4 == 3))
            P.tt("dve", ob[:, :, tok], psp[:], uT[:, :, tok], ALU.mult)
        P.dma("sp", out_d[:, :, t0:t0 + 512], ob[:], is_output=True)
    P.emit()
    return nc


def xT_layout(xseg):
    T = xseg.shape[0]
    return np.ascontiguousarray(np.asarray(xseg).T.reshape(8, 128, T).transpose(1, 0, 2))


def run_Mgmlp(x_full, mod, inp):
    if "Mgmlp" not in _CACHE:
        _CACHE["Mgmlp"] = build_Mgmlp()
    nc = _CACHE["Mgmlp"]
    layer = 2
    win = wtile(inp["gmlp_w_in"])
    rows = np.stack([inp["gmlp_ln_g"], inp["gmlp_ln_b"]], 0)
    rows = np.ascontiguousarray(np.broadcast_to(rows[None], (128, 2, 2048))).astype(np.float32)
    wsT = np.ascontiguousarray(np.asarray(inp["gmlp_w_s"]).transpose(2, 0, 1))
    idx = np.arange(128)
    mask = (idx[:, None] <= idx[None, :]).astype(np.float32)
    bs = np.asarray(inp["gmlp_b_s"]).reshape(1, 1024).astype(np.float32)
    in_maps = []
    for core in range(NCORES):
        b, q = divmod(core, 4)
        m = mod[b, layer]
        sh1, sc1 = m[0:1024], m[1024:2048]
        pm1 = np.ascontiguousarray(np.stack([pm(sc1), pm(sh1)], 1)).astype(np.float32)
        in_maps.append({"xT": xT_layout(x_full[b, q * 2048:(q + 1) * 2048]), "pm1": pm1, "w_in": win,
                        "rows": rows, "wsT": wsT, "mask": mask, "bs": bs})
    res = run_bass_kernel_spmd(nc, in_maps, core_ids=list(range(NCORES)))
    oTs = [np.asarray(res.results[c]["oT"]) for c in range(NCORES)]
    outs = []
    for core in range(NCORES):
        b, q = divmod(core, 4)
        sh = np.zeros((128, 16, 128), ml_dtypes.bfloat16) if q == 0 else oTs[core - 1][:, :, -128:]
        outs.append(np.ascontiguousarray(np.concatenate([sh, oTs[core]], axis=2)))
    return outs


def build_Mret():
    nc = bass.Bass("TRN2", target_bir_lowering=False)
    P = Prog(nc)
    xT_d = P.dram("xT", [128, 8, S], F32, "ExternalInput")
    pm1_d = P.dram("pm1", [128, 2, 8], F32, "ExternalInput")
    wq_d = P.dram("wq", [128, 8, 256], F32, "ExternalInput")
    wk_d = P.dram("wk", [128, 8, 256], F32, "ExternalInput")
    wv_d = P.dram("wv", [128, 8, 512], F32, "ExternalInput")
    wg_d = P.dram("wg", [128, 8, 512], F32, "ExternalInput")
    cos_d = P.dram("cosT", [128, S], F32, "ExternalInput")
    sin_d = P.dram("sinT", [128, S], F32, "ExternalInput")
    xi_d = P.dram("xi_row", [128, 512], F32, "ExternalInput")
    dm_d = P.dram("dmaskT", [128, 128], F32, "ExternalInput")
    col_d = P.dram("cols", [128, 2], F32, "ExternalInput")
    id_d = P.dram("ident", [128, 128], BF16, "ExternalInput")
    out_d = P.dram("o", [S, 512], BF16, "ExternalOutput")

    wq = P.sb("wq_sb", [128, 8, 256], BF16)
    wk = P.sb("wk_sb", [128, 8, 256], BF16)
    wv = P.sb("wv_sb", [128, 8, 512], BF16)
    wg = P.sb("wg_sb", [128, 8, 512], BF16)
    pm1 = P.sb("pm1_sb", [128, 2, 8])
    xi_row = P.sb("xi_sb", [128, 512])
    dmT = P.sb("dm_sb", [128, 128])
    cols = P.sb("cols_sb", [128, 2])
    ident = P.sb("ident_sb", [128, 128], BF16)
    eps = P.sb("eps", [128, 1])
    xs = [P.sb("xs%d" % i, [128, 512]) for i in range(3)]
    hT = P.sb("hT", [128, 8, 512], BF16)
    cs = [P.sb("cs%d" % i, [128, 2, 512]) for i in range(2)]
    tmp = [P.sb("tmp%d" % i, [128, 512]) for i in range(4)]
    qT = P.sb("qT", [128, 2, 512], BF16)
    qxT = P.sb("qxT", [128, 2, 512], BF16)
    kT = P.sb("kT", [128, 2, 512], BF16)
    vsb = [P.sb("vsb%d" % i, [128, 512], BF16) for i in range(2)]
    sg = [P.sb("sg%d" % i, [128, 512]) for i in range(2)]
    kz = [P.sb("kz%d" % i, [128, 256], BF16) for i in range(2)]
    sT = [P.sb("sT%d" % i, [128, 128], BF16) for i in range(2)]
    on = [P.sb("on%d" % i, [128, 512]) for i in range(2)]
    ob = [P.sb("ob%d" % i, [128, 512], BF16) for i in range(2)]
    state = P.sb("state", [128, 2, 512])
    state_bf = P.sb("state_bf", [128, 2, 512], BF16)
    stats = P.sb("stats", [128, 6])
    mv = P.sb("mv", [128, 2])
    rstd = P.sb("rstd", [128, 1])
    pq = P.ps("pq", [128, 2, 512])
    pvg = P.ps("pvg", [128, 512])
    pS = P.ps("pS", [128, 512])
    pkt = P.ps("pkt", [128, 1024], BF16)
    po = P.ps("po", [128, 512])
    pst = P.ps("pst", [128, 2, 512])

    for dst, src in ((pm1, pm1_d), (xi_row, xi_d), (dmT, dm_d), (cols, col_d), (ident, id_d)):
        P.dma("sp", dst[:], src)
    P.memset("dve", eps[:], 1e-6)
    P.memset("dve", state[:], 0.0)
    P.memset("pool", state_bf[:], 0.0)
    P.ts("dve", pm1[:, 0, :], pm1[:, 0, :], 1.0, None, ALU.add)
    for dst, src in ((wq, wq_d), (wk, wk_d), (wv, wv_d), (wg, wg_d)):
        P.dma("pool", dst[:], src)

    xi_ = 0
    ci = 0
    for blk in range(S // 512):
        t0 = blk * 512
        cb = cs[blk % 2]
        P.dma("sp", cb[:, 0, :], cos_d[:, t0:t0 + 512])
        P.dma("sp", cb[:, 1, :], sin_d[:, t0:t0 + 512])
        for kc in range(8):
            xb = xs[xi_ % 3]
            xi_ += 1
            P.dma("sp", xb[:], xT_d[:, kc, t0:t0 + 512])
            P.act(hT[:, kc, :], xb[:], AF.Identity, bias=pm1[:, 1, kc:kc + 1], scale=pm1[:, 0, kc:kc + 1])
        for which, w_sb in (("q", wq), ("k", wk)):
            for dkc in range(2):
                for kc in range(8):
                    P.mm(pq[:, dkc, :], w_sb[:, kc, dkc * 128:(dkc + 1) * 128], hT[:, kc, :], start=(kc == 0), stop=(kc == 7))
            P.tt("dve", tmp[0][:], pq[:, 0, :], cb[:, 0, :], ALU.mult)
            P.tt("dve", tmp[1][:], pq[:, 1, :], cb[:, 1, :], ALU.mult)
            P.tt("dve", tmp[2][:], pq[:, 0, :], cb[:, 1, :], ALU.mult)
            P.tt("dve", tmp[3][:], pq[:, 1, :], cb[:, 0, :], ALU.mult)
            P.tt("pool", tmp[0][:], tmp[0][:], tmp[1][:], ALU.subtract)
            P.tt("pool", tmp[2][:], tmp[2][:], tmp[3][:], ALU.add)
            dst = qT if which == "q" else kT
            P.copy("act", dst[:, 0, :], tmp[0][:])
            P.copy("act", dst[:, 1, :], tmp[2][:])
            if which == "q":
                P.tt("pool", qxT[:, 0, :], tmp[0][:], xi_row[:], ALU.mult)
                P.tt("pool", qxT[:, 1, :], tmp[2][:], xi_row[:], ALU.mult)
        for ch in range(4):
            tok = slice(ch * 128, (ch + 1) * 128)
            vb, sgb, kzb, sTb, onb, obb = vsb[ci % 2], sg[ci % 2], kz[ci % 2], sT[ci % 2], on[ci % 2], ob[ci % 2]
            ci += 1
            for kc in range(8):
                P.mm(pvg[:], hT[:, kc, tok], wv[:, kc, :], start=(kc == 0), stop=(kc == 7))
            P.copy("act", vb[:], pvg[:])
            for kc in range(8):
                P.mm(pvg[:], hT[:, kc, tok], wg[:, kc, :], start=(kc == 0), stop=(kc == 7))
            P.act(sgb[:], pvg[:], AF.Silu)
            for dkc in range(2):
                P.transpose(pkt[:, dkc * 128:(dkc + 1) * 128], kT[:, dkc, tok], ident[:], inc=(dkc == 1))
            P.ts("dve", kzb[:], pkt[:, 0:256], cols[:, 0:1], None, ALU.mult)
            for dkc in range(2):
                P.mm(pS[:, 0:128], kT[:, dkc, tok], qT[:, dkc, tok], start=(dkc == 0), stop=(dkc == 1))
            P.tt("dve", sTb[:], pS[:, 0:128], dmT[:], ALU.mult)
            P.mm(po[:], sTb[:], vb[:], start=True, stop=False, inc=False)
            P.mm(po[:], qxT[:, 0, tok], state_bf[:, 0, :], start=False, stop=False, inc=False)
            P.mm(po[:], qxT[:, 1, tok], state_bf[:, 1, :], start=False, stop=True)
            for dkc in range(2):
                P.mm(pst[:, dkc, :], kzb[:, dkc * 128:(dkc + 1) * 128], vb[:], start=True, stop=True)
            P.stt("dve", state[:].rearrange("p a b -> p (a b)"), state[:].rearrange("p a b -> p (a b)"), cols[:, 1:2],
                  pst[:].rearrange("p a b -> p (a b)"), ALU.mult, ALU.add)
            P.copy("act", state_bf[:].rearrange("p a b -> p (a b)"), state[:].rearrange("p a b -> p (a b)"))
            P.generic("dve", "bn_stats", (stats[:], po[:]), [po[:]], [stats[:]])
            P.generic("dve", "bn_aggr", (mv[:], stats[:]), [stats[:]], [mv[:]])
            P.act(rstd[:], mv[:, 1:2], AF.Sqrt, bias=eps[:])
            P.generic("dve", "reciprocal", (rstd[:], rstd[:]), [rstd[:]], [rstd[:]])
            P.ts("dve", onb[:], po[:], mv[:, 0:1], rstd[:], ALU.subtract, ALU.mult)
            P.tt("pool", obb[:], onb[:], sgb[:], ALU.mult)
            P.dma("sp", out_d[t0 + ch * 128:t0 + (ch + 1) * 128, :], obb[:], is_output=True)
    P.emit()
    return nc


def run_Mret(x_full, mod, inp):
    if "Mret" not in _CACHE:
        _CACHE["Mret"] = build_Mret()
    nc = _CACHE["Mret"]
    layer = 1
    H, dk, dv, C = 4, 256, 512, 128
    w = np.asarray(inp["ret_w_in"])
    pos = np.arange(S, dtype=np.float32)
    inv_freq = (np.float32(10000.0) ** (-np.linspace(0.0, 1.0, dk // 2, dtype=np.float32))).astype(np.float32)
    ang = (pos[:, None] * inv_freq[None, :]).astype(np.float32)
    cosT = np.ascontiguousarray(np.cos(ang).T.astype(np.float32))
    sinT = np.ascontiguousarray(np.sin(ang).T.astype(np.float32))
    ident = np.eye(128, dtype=np.float32).astype(ml_dtypes.bfloat16)
    idx = np.arange(C, dtype=np.float32)
    in_maps = []
    xTs = [xT_layout(x_full[b]) for b in range(2)]
    for core in range(NCORES):
        b, h = divmod(core, 4)
        m = mod[b, layer]
        sh1, sc1 = m[0:1024], m[1024:2048]
        pm1 = np.ascontiguousarray(np.stack([pm(sc1), pm(sh1)], 1)).astype(np.float32)
        lg = np.log(np.float32(1.0) - np.power(np.float32(2.0), np.float32(-5.0 - h))).astype(np.float32)
        rel = idx[None, :] - idx[:, None]
        dmT = np.where(rel >= 0, np.exp(np.maximum(rel, 0.0) * lg), 0.0).astype(np.float32) * np.float32(dk ** -0.5)
        zeta = np.exp((C - 1.0 - idx) * lg).astype(np.float32) * np.float32(dk ** -0.5)
        xi = np.exp((idx + 1.0) * lg).astype(np.float32)
        gam = np.exp(np.float32(C) * lg).astype(np.float32)
        cols = np.stack([zeta, np.full(128, gam, np.float32)], 1).astype(np.float32)
        xi_row = np.ascontiguousarray(np.broadcast_to(np.tile(xi, 4)[None], (128, 512))).astype(np.float32)
        in_maps.append({
            "xT": xTs[b], "pm1": pm1,
            "wq": wtile(w[:, h * dk:(h + 1) * dk]), "wk": wtile(w[:, H * dk + h * dk:H * dk + (h + 1) * dk]),
            "wv": wtile(w[:, 2 * H * dk + h * dv:2 * H * dk + (h + 1) * dv]),
            "wg": wtile(w[:, 2 * H * dk + H * dv + h * dv:2 * H * dk + H * dv + (h + 1) * dv]),
            "cosT": cosT, "sinT": sinT, "xi_row": xi_row, "dmaskT": np.ascontiguousarray(dmT), "cols": cols, "ident": ident,
        })
    res = run_bass_kernel_spmd(nc, in_maps, core_ids=list(range(NCORES)))
    o_full = np.empty((2, S, H * dv), ml_dtypes.bfloat16)
    for core in range(NCORES):
        b, h = divmod(core, 4)
        o_full[b, :, h * dv:(h + 1) * dv] = np.asarray(res.results[core]["o"])
    return oT_from_tokenmajor(o_full)


def build_Msb():
    nc = bass.Bass("TRN2", target_bir_lowering=False)
    P = Prog(nc)
    NQB = S // 128
    xT_d = P.dram("xT", [128, 8, S], F32, "ExternalInput")
    xTr_d = P.dram("xTr", [128, 8, S], F32, "ExternalInput")
    pm1_d = P.dram("pm1", [128, 2, 8], F32, "ExternalInput")
    wq_d = P.dram("wq", [128, 8, 256], F32, "ExternalInput")
    wk_d = P.dram("wk", [128, 8, 256], F32, "ExternalInput")
    wv_d = P.dram("wv", [128, 8, 256], F32, "ExternalInput")
    mneg_d = P.dram("mneg", [128, 128], BF16, "ExternalInput")
    id_d = P.dram("ident", [128, 128], BF16, "ExternalInput")
    out_d = P.dram("o", [128, NQB, 256], BF16, "ExternalOutput")

    qT = P.sb("qT_all", [128, 2, S], BF16)
    kT = P.sb("kT_all", [128, 2, S], BF16)
    v_all = P.sb("v_all", [128, NQB, 256], BF16)
    o_qb = [P.sb("o_qb%d" % i, [128, 256], BF16) for i in range(2)]
    wq = P.sb("wq_sb", [128, 8, 256], BF16)
    wk = P.sb("wk_sb", [128, 8, 256], BF16)
    wv = P.sb("wv_sb", [128, 8, 256], BF16)
    pm1 = P.sb("pm1_sb", [128, 2, 8])
    mneg = P.sb("mneg_sb", [128, 128], BF16)
    ident = P.sb("ident_sb", [128, 128], BF16)
    zeros = P.sb("zeros", [128, 512])
    xs = [P.sb("xs%d" % i, [128, 512]) for i in range(3)]
    hT = P.sb("hT", [128, 8, 512], BF16)
    hTr = P.sb("hTr", [128, 8, 512], BF16)
    NSET = 5
    gb = [P.sb("gb%d" % i, [128, 512]) for i in range(NSET)]
    Pb = [P.sb("Pb%d" % i, [128, 513]) for i in range(NSET)]
    Ab = [P.sb("Ab%d" % i, [128, 512], BF16) for i in range(NSET)]
    ATb = [P.sb("ATb%d" % i, [128, 512], BF16) for i in range(NSET)]
    pz = P.ps("pz", [128, 5, 512])
    pT = P.ps("pT", [128, 2, 1024], BF16)
    po = P.ps("po", [128, 2, 64])
    pp = pz

    for dst, src in ((pm1, pm1_d), (mneg, mneg_d), (ident, id_d)):
        P.dma("sp", dst[:], src)
    P.memset("dve", zeros[:], 0.0)
    P.ts("dve", pm1[:, 0, :], pm1[:, 0, :], 1.0, None, ALU.add)
    for dst, src in ((wq, wq_d), (wk, wk_d), (wv, wv_d)):
        P.dma("pool", dst[:], src)

    xi_ = 0
    for blk in range(S // 512):
        t0 = blk * 512
        for (src_d, dst) in ((xT_d, hT), (xTr_d, hTr)):
            for kc in range(8):
                xb = xs[xi_ % 3]
                xi_ += 1
                P.dma("sp", xb[:], src_d[:, kc, t0:t0 + 512])
                P.act(dst[:, kc, :], xb[:], AF.Identity, bias=pm1[:, 1, kc:kc + 1], scale=pm1[:, 0, kc:kc + 1])
        for pair in range(2):
            pq_ = pp[:, pair, :]
            for kc in range(8):
                P.mm(pq_, wq[:, kc, pair * 128:(pair + 1) * 128], hT[:, kc, :], start=(kc == 0), stop=(kc == 7))
            P.act(qT[:, pair, t0:t0 + 512], pq_, AF.Copy, scale=0.125)
            pk_ = pp[:, 2 + pair, :]
            for kc in range(8):
                P.mm(pk_, wk[:, kc, pair * 128:(pair + 1) * 128], hTr[:, kc, :], start=(kc == 0), stop=(kc == 7))
            P.copy("dve", kT[:, pair, t0:t0 + 512], pk_)
        for ch in range(4):
            pv_ = pz[:, ch % 4, 0:256]
            for kc in range(8):
                P.mm(pv_, hTr[:, kc, ch * 128:(ch + 1) * 128], wv[:, kc, :], start=(kc == 0), stop=(kc == 7))
            P.copy("act" if ch % 2 else "dve", v_all[:, blk * 4 + ch, :], pv_)

    items = []
    for qb in range(NQB):
        nseg = (qb + 1 + 3) // 4
        for head in range(4):
            for sg_ in range(nseg):
                items.append((qb, head, sg_, nseg))
    prevP = {}

    def geom(it):
        qb, head, sg_, nseg = items[it]
        nb = qb + 1
        kb0 = sg_ * 4
        nk = min(4, nb - kb0)
        pair, par = divmod(head, 2)
        return qb, head, sg_, nseg, kb0, nk, nk * 128, pair, slice(par * 64, par * 64 + 64), it % NSET

    def stA(it):
        qb, head, sg_, nseg, kb0, nk, n, pair, prt, b4 = geom(it)
        t0 = qb * 128
        ks = (NQB - 1 - qb) * 128 + kb0 * 128
        zt = pz[:, b4, :]
        P.mm(zt[:, 0:n], qT[prt, pair, t0:t0 + 128], kT[prt, pair, ks:ks + n], start=True, stop=(sg_ != 0))
        if sg_ == 0:
            P.mm(zt[:, 0:128], ident[:], mneg[:], start=False, stop=True)

    def stB(it):
        qb, head, sg_, nseg, kb0, nk, n, pair, prt, b4 = geom(it)
        zt = pz[:, b4, :]
        g_, P_, A_ = gb[b4], Pb[b4], Ab[b4]
        P.act(g_[:, 0:n], zt[:, 0:n], AF.Sigmoid, scale=-1.0)
        if sg_ == 0:
            P.memset("pool", P_[:, 0:1], 1.0)
            P.generic("dve", "tensor_tensor_scan", (P_[:, 1:1 + n], g_[:, 0:n], zeros[:, 0:n], 1.0, ALU.mult, ALU.add),
                      [g_[:, 0:n], zeros[:, 0:n]], [P_[:, 1:1 + n]])
        else:
            pp_, pn = prevP[(qb, head)]
            P.copy("pool", P_[:, 0:1], pp_[:, pn:pn + 1])
            P.generic("dve", "tensor_tensor_scan", (P_[:, 1:1 + n], g_[:, 0:n], zeros[:, 0:n], pp_[:, pn:pn + 1], ALU.mult, ALU.add),
                      [g_[:, 0:n], zeros[:, 0:n], pp_[:, pn:pn + 1]], [P_[:, 1:1 + n]])
        prevP[(qb, head)] = (P_, n)
        P.tt("pool", A_[:, 0:n], P_[:, 0:n], P_[:, 1:1 + n], ALU.subtract)

    def stC(it):
        qb, head, sg_, nseg, kb0, nk, n, pair, prt, b4 = geom(it)
        A_, AT_ = Ab[b4], ATb[b4]
        p4 = it % 4
        ptb = pT[:, p4 % 2, (p4 // 2) * 512:(p4 // 2 + 1) * 512]
        for j in range(nk):
            P.transpose(ptb[:, j * 128:(j + 1) * 128], A_[:, j * 128:(j + 1) * 128], ident[:], inc=(j == nk - 1))
        P.copy("act" if (it % 2) else "dve", AT_[:, 0:n], ptb[:, 0:n])

    def stE(it):
        qb, head, sg_, nseg, kb0, nk, n, pair, prt, b4 = geom(it)
        AT_ = ATb[b4]
        pob = po[:, head % 2, :]
        oq = o_qb[qb % 2]
        for j in range(nk):
            rb = (NQB - 1 - qb) + kb0 + j
            first = (sg_ == 0 and j == 0)
            last = (sg_ == nseg - 1 and j == nk - 1)
            P.mm(pob, AT_[:, j * 128:(j + 1) * 128], v_all[:, rb, head * 64:(head + 1) * 64],
                 start=first, stop=last, inc=(last or j == nk - 1))
        if sg_ == nseg - 1:
            P.copy("act", oq[:, head * 64:(head + 1) * 64], pob)
            if head == 3:
                P.dma("sp", out_d[:, qb, :], oq[:], is_output=True)

    D1, D2 = 3, 4
    NI = len(items)
    for st in range(NI + D2):
        if st < NI:
            stA(st)
            stB(st)
        if 0 <= st - D1 < NI:
            stC(st - D1)
        if 0 <= st - D2 < NI:
            stE(st - D2)
    P.emit()
    return nc


def run_Msb(x_full, mod, inp):
    if "Msb" not in _CACHE:
        _CACHE["Msb"] = build_Msb()
    nc = _CACHE["Msb"]
    layer = 3
    w = np.asarray(inp["sb_w_in"])
    ident = np.eye(128, dtype=np.float32).astype(ml_dtypes.bfloat16)
    idx = np.arange(128)
    mneg = np.where(idx[:, None] + idx[None, :] <= 127, -30000.0, 0.0).astype(np.float32).astype(ml_dtypes.bfloat16)
    xTs = [xT_layout(x_full[b]) for b in range(2)]
    xTrs = [np.ascontiguousarray(t[:, :, ::-1]) for t in xTs]
    in_maps = []
    for core in range(NCORES):
        b, hg = divmod(core, 4)
        m = mod[b, layer]
        sh1, sc1 = m[0:1024], m[1024:2048]
        pm1 = np.ascontiguousarray(np.stack([pm(sc1), pm(sh1)], 1)).astype(np.float32)
        in_maps.append({
            "xT": xTs[b], "xTr": xTrs[b], "pm1": pm1,
            "wq": wtile(w[:, hg * 256:(hg + 1) * 256]),
            "wk": wtile(w[:, 1024 + hg * 256:1024 + (hg + 1) * 256]),
            "wv": wtile(w[:, 2048 + hg * 256:2048 + (hg + 1) * 256]),
            "mneg": mneg, "ident": ident,
        })
    res = run_bass_kernel_spmd(nc, in_maps, core_ids=list(range(NCORES)))
    o_full = np.empty((2, S, 1024), ml_dtypes.bfloat16)
    for core in range(NCORES):
        b, hg = divmod(core, 4)
        o = np.asarray(res.results[core]["o"])
        o_full[b, :, hg * 256:(hg + 1) * 256] = o.transpose(1, 0, 2).reshape(S, 256)
    return oT_from_tokenmajor(o_full)


def build_Mgdn(nblk=None):
    nc = bass.Bass("TRN2", target_bir_lowering=False)
    P = Prog(nc)
    NTL = S // 128
    xT_d = P.dram("xT", [128, 8, S], F32, "ExternalInput")
    pm1_d = P.dram("pm1", [128, 2, 8], F32, "ExternalInput")
    w_d = P.dram("w", [128, 8, 1024], F32, "ExternalInput")
    wab_d = P.dram("wab", [128, 8, 4], F32, "ExternalInput")
    cw_d = P.dram("cw", [128, 6, 4], F32, "ExternalInput")
    cst_d = P.dram("cst", [128, 8, 128], F32, "ExternalInput")
    hs_d = P.dram("hs", [128, 4], F32, "ExternalInput")
    nw_d = P.dram("nw", [128, 128], F32, "ExternalInput")
    idb_d = P.dram("identb", [128, 128], BF16, "ExternalInput")
    out_d = P.dram("o", [128, NTL, 256], BF16, "ExternalOutput")

    w = P.sb("w_sb", [128, 8, 1024], BF16)
    wab = P.sb("wab_sb", [128, 8, 4])
    pm1 = P.sb("pm1_sb", [128, 2, 8])
    cw = P.sb("cw_sb", [128, 6, 4])
    cst = P.sb("cst_sb", [128, 8, 128])
    hs = P.sb("hs_sb", [128, 4])
    nw = P.sb("nw_sb", [128, 128])
    identb = P.sb("identb_sb", [128, 128], BF16)
    onesb = P.sb("onesb", [128, 128], BF16)
    negA = P.sb("negA", [128, 2])
    eps6 = P.sb("eps6", [128, 1])
    eps6q = P.sb("eps6q", [128, 1])
    one1 = P.sb("one1", [128, 1])
    ident_f, onesBD, triBD, blk0, blk1, posmask, posmaskT, negstrict = [cst[:, i, :] for i in range(8)]

    xs = [P.sb("xs%d" % i, [128, 512]) for i in range(3)]
    hT = P.sb("hT", [128, 8, 512], BF16)
    hTf = P.sb("hTf", [128, 8, 512])
    xc = [P.sb("xc%d" % i, [128, 515]) for i in range(6)]
    cv = [P.sb("cv%d" % i, [128, 512]) for i in range(2)]
    sq = [P.sb("sq%d" % i, [128, 512], BF16) for i in range(2)]
    rn = [P.sb("rn%d" % i, [128, 512]) for i in range(2)]
    qhT = [P.sb("qhT%d" % i, [128, 512], BF16) for i in range(2)]
    khT = [P.sb("khT%d" % i, [128, 512], BF16) for i in range(2)]
    vcT = [P.sb("vcT%d" % i, [128, 512], BF16) for i in range(2)]
    ktok = [P.sb("ktok%d" % i, [128, 4, 128]) for i in range(2)]
    vtok = [P.sb("vtok%d" % i, [128, 4, 128]) for i in range(2)]
    nz = [P.sb("nz%d" % i, [128, 4, 128]) for i in range(2)]
    gat = {}
    for nm in ("g", "beta", "gc", "glo", "glb0", "glb1", "egc", "edl", "egl0", "egl1", "bg", "e1"):
        gat[nm] = [P.sb("%s%d" % (nm, i), [128, 4]) for i in range(2)]
    Dg = [P.sb("Dg%d" % i, [128, 128]) for i in range(4)]
    dec = [P.sb("dec%d" % i, [128, 128]) for i in range(4)]
    decT = [P.sb("decT%d" % i, [128, 128]) for i in range(4)]
    tmpE = [P.sb("tmpE%d" % i, [128, 128]) for i in range(4)]
    Yb_h = [[[P.sb("Y%d_%d_%d" % (h, i, k), [128, 128]) for k in range(2)] for i in range(4)] for h in range(2)]
    Zb_h = [[[P.sb("Z%d_%d_%d" % (h, i, k), [128, 128]) for k in range(2)] for i in range(4)] for h in range(2)]
    Ttb_h = [[[P.sb("Tt%d_%d_%d" % (h, i, k), [128, 128]) for k in range(2)] for i in range(4)] for h in range(2)]
    Tmb_h = [[[P.sb("Tm%d_%d_%d" % (h, i, k), [128, 128]) for k in range(2)] for i in range(4)] for h in range(2)]
    Ttbf_h = [[P.sb("Ttbf%d_%d" % (h, i), [128, 128], BF16) for i in range(4)] for h in range(2)]
    vbt_h = [[P.sb("vbt%d_%d" % (h, i), [128, 128], BF16) for i in range(4)] for h in range(2)]
    kbe_h = [[P.sb("kbe%d_%d" % (h, i), [128, 128], BF16) for i in range(4)] for h in range(2)]
    kd_h = [[P.sb("kd%d_%d" % (h, i), [128, 128], BF16) for i in range(4)] for h in range(2)]
    u_h = [[P.sb("u%d_%d" % (h, i), [128, 128]) for i in range(4)] for h in range(2)]
    wTA_h = [[P.sb("wTA%d_%d" % (h, i), [128, 128], BF16) for i in range(4)] for h in range(2)]
    wTB_h = [[P.sb("wTB%d_%d" % (h, i), [128, 128], BF16) for i in range(4)] for h in range(2)]
    qkT_h = [[P.sb("qkT%d_%d" % (h, i), [128, 128], BF16) for i in range(4)] for h in range(2)]
    vn_h = [[P.sb("vn%d_%d" % (h, i), [128, 128], BF16) for i in range(2)] for h in range(2)]
    o1_h = [[P.sb("o1_%d_%d" % (h, i), [128, 128]) for i in range(2)] for h in range(2)]
    osum_h = [[P.sb("osum%d_%d" % (h, i), [128, 128]) for i in range(2)] for h in range(2)]
    osq_h = [P.sb("osq%d" % h, [128, 128]) for h in range(2)]
    ssq_h = [P.sb("ssq%d" % h, [128, 1]) for h in range(2)]
    rinv_h = [P.sb("rinv%d" % h, [128, 1]) for h in range(2)]
    St = [P.sb("S%d" % i, [128, 128]) for i in range(2)]
    Sbf = [P.sb("Sbf%d" % i, [128, 128], BF16) for i in range(2)]
    o_all = P.sb("o_all", [128, NTL, 256], BF16)

    PB = P.ps("PB", [128, 4, 512])
    PTr = P.ps("PTr", [128, 1024], BF16)
    PG = P.ps("PG", [128, 512])
    PSC = P.ps("PSC", [128, 2, 512])

    def slot(i):
        return PB[:, i // 4, (i % 4) * 128:(i % 4 + 1) * 128]

    for dst, src in ((pm1, pm1_d), (wab, wab_d), (cw, cw_d), (cst, cst_d), (hs, hs_d), (nw, nw_d), (identb, idb_d)):
        P.dma("sp", dst[:], src)
    for kc in range(8):
        P.dma("pool", w[:, kc, :], w_d[:, kc, :])
    P.memset("dve", onesb[:], 1.0)
    P.memset("dve", eps6[:], 1e-6)
    P.memset("dve", eps6q[:], 128e-6)
    P.memset("dve", one1[:], 1.0)
    for i in range(6):
        P.memset("pool", xc[i][:, 0:3], 0.0)
    for h in range(2):
        for i in range(4):
            P.memset("pool", wTA_h[h][i][:], 0.0)
            P.memset("pool", wTB_h[h][i][:], 0.0)
    for h2 in range(2):
        P.memset("dve", St[h2][:], 0.0)
        P.memset("dve", Sbf[h2][:], 0.0)
    P.ts("dve", pm1[:, 0, :], pm1[:, 0, :], 1.0, None, ALU.add)
    if nblk:
        P.memset("pool", o_all[:], 0.0)
    P.act(negA[:], hs[:, 2:4], AF.Exp)
    P.ts("dve", negA[:], negA[:], -1.0, None, ALU.mult)

    xi_ = 0
    for blk in range(nblk or (S // 512)):
        t0 = blk * 512
        for kc in range(8):
            xb = xs[xi_ % 3]
            xi_ += 1
            P.dma("sp", xb[:], xT_d[:, kc, t0:t0 + 512])
            P.act(hT[:, kc, :], xb[:], AF.Identity, bias=pm1[:, 1, kc:kc + 1], scale=pm1[:, 0, kc:kc + 1])
            P.ts("dve", hTf[:, kc, :], xb[:], pm1[:, 0, kc:kc + 1], pm1[:, 1, kc:kc + 1], ALU.mult, ALU.add)
        for tl in range(4):
            for kc in range(8):
                P.mm(PG[:, tl * 4:tl * 4 + 4], hTf[:, kc, tl * 128:(tl + 1) * 128], wab[:, kc, :], start=(kc == 0), stop=(kc == 7))
        pgv = PG[:, 0:16].rearrange("p (t c) -> p t c", c=4)
        for h2 in range(2):
            G = {k: v[h2] for k, v in gat.items()}
            P.act(G["e1"][:], pgv[:, :, h2], AF.Exp, bias=hs[:, h2:h2 + 1])
            P.act(G["e1"][:], G["e1"][:], AF.Ln, bias=one1[:])
            P.ts("dve", G["g"][:], G["e1"][:], negA[:, h2:h2 + 1], None, ALU.mult)
            P.act(G["beta"][:], pgv[:, :, 2 + h2], AF.Sigmoid)
            for nm, cm, off in (("gc", triBD, 16), ("glo", onesBD, 20), ("glb0", blk0, 24), ("glb1", blk1, 28)):
                o_ = off + h2 * 16
                P.mm(PG[:, 32 + o_ - 16:32 + o_ - 12], cm, G["g"][:], start=True, stop=True)
                P.copy("dve", G[nm][:], PG[:, 32 + o_ - 16:32 + o_ - 12])
            P.act(G["egc"][:], G["gc"][:], AF.Exp)
            P.tt("dve", G["edl"][:], G["glo"][:], G["gc"][:], ALU.subtract)
            P.act(G["edl"][:], G["edl"][:], AF.Exp)
            P.act(G["egl0"][:], G["glb0"][:], AF.Exp)
            P.act(G["egl1"][:], G["glb1"][:], AF.Exp)
            P.tt("dve", G["bg"][:], G["beta"][:], G["egc"][:], ALU.mult)
        for h2 in range(2):
            for i in range(3):
                ci = h2 * 3 + i
                pp_ = PB[:, ci % 4, :]
                c0 = h2 * 384 + i * 128
                for kc in range(8):
                    P.mm(pp_, w[:, kc, c0:c0 + 128], hT[:, kc, :], start=(kc == 0), stop=(kc == 7))
                xcb = xc[ci]
                if blk > 0:
                    P.copy("pool", xcb[:, 0:3], xcb[:, 512:515])
                P.copy("act", xcb[:, 3:515], pp_)
                cvb = cv[ci % 2]
                P.ts("dve", cvb[:], xcb[:, 0:512], cw[:, ci, 0:1], None, ALU.mult)
                for tap in range(1, 4):
                    P.stt("dve", cvb[:], xcb[:, tap:tap + 512], cw[:, ci, tap:tap + 1], cvb[:], ALU.mult, ALU.add)
                if i == 2:
                    P.act(vcT[h2][:], cvb[:], AF.Silu)
                else:
                    P.act(cvb[:], cvb[:], AF.Silu)
                    sqb, rnb = sq[ci % 2], rn[ci % 2]
                    P.act(sqb[:], cvb[:], AF.Square)
                    pn = PB[:, (ci + 2) % 4, :]
                    P.mm(pn, onesb[:], sqb[:], start=True, stop=True)
                    if i == 0:
                        P.act(rnb[:], pn, AF.Sqrt, bias=eps6q[:], scale=128.0)
                    else:
                        P.act(rnb[:], pn, AF.Sqrt, bias=eps6[:])
                    P.generic("dve", "reciprocal", (rnb[:], rnb[:]), [rnb[:]], [rnb[:]])
                    P.tt("dve", (qhT if i == 0 else khT)[h2][:], cvb[:], rnb[:], ALU.mult)
            for tl in range(4):
                tok = slice(tl * 128, (tl + 1) * 128)
                P.transpose(PTr[:, 0:128], khT[h2][:, tok], identb[:], inc=False)
                P.transpose(PTr[:, 128:256], vcT[h2][:, tok], identb[:])
                P.copy("act", ktok[h2][:, tl, :], PTr[:, 0:128])
                P.copy("dve", vtok[h2][:, tl, :], PTr[:, 128:256])
                pz_ = PB[:, tl % 4, 0:128]
                for kc in range(8):
                    P.mm(pz_, hT[:, kc, tok], w[:, kc, 768 + h2 * 128:768 + (h2 + 1) * 128], start=(kc == 0), stop=(kc == 7))
                P.act(nz[h2][:, tl, :], pz_, AF.Silu)
                P.tt("pool", nz[h2][:, tl, :], nz[h2][:, tl, :], nw[:], ALU.mult)

        for h2 in range(2):
            G = {k: v[h2] for k, v in gat.items()}
            kd, u_sb, wTA, wTB, qkT = kd_h[h2], u_h[h2], wTA_h[h2], wTB_h[h2], qkT_h[h2]
            Yb, Zb, Ttb, Tmb, Ttbf, vbt, kbe = Yb_h[h2], Zb_h[h2], Ttb_h[h2], Tmb_h[h2], Ttbf_h[h2], vbt_h[h2], kbe_h[h2]
            for tl in range(4):
                tok = slice(tl * 128, (tl + 1) * 128)
                gcc = G["gc"][:, tl:tl + 1]
                P.ts("dve", Dg[tl][:], ident_f, gcc, None, ALU.mult)
                P.mm(slot(tl), onesBD, Dg[tl][:], start=True, stop=True)
                P.stt("dve", tmpE[tl][:], slot(tl), gcc, posmask, ALU.subtract, ALU.add)
                P.act(dec[tl][:], tmpE[tl][:], AF.Exp, scale=-1.0)
                P.stt("dve", tmpE[tl][:], slot(tl), gcc, posmaskT, ALU.subtract, ALU.subtract)
                P.act(decT[tl][:], tmpE[tl][:], AF.Exp)
                P.mm(slot(4 + tl), khT[h2][:, tok], khT[h2][:, tok], start=True, stop=True)
                P.tt("dve", tmpE[tl][:], slot(4 + tl), dec[tl][:], ALU.mult)
                P.stt("dve", Zb[tl][0][:], tmpE[tl][:], G["beta"][:, tl:tl + 1], negstrict, ALU.mult, ALU.mult)
                P.mm(slot(8 + tl), khT[h2][:, tok], qhT[h2][:, tok], start=True, stop=True)
                P.tt("dve", qkT[tl][:], slot(8 + tl), decT[tl][:], ALU.mult)
                P.ts("dve", vbt[tl][:], vtok[h2][:, tl, :], G["beta"][:, tl:tl + 1], None, ALU.mult)
                P.act(kbe[tl][:], ktok[h2][:, tl, :], AF.Identity, scale=G["bg"][:, tl:tl + 1])
                P.act(kd[tl][:], ktok[h2][:, tl, :], AF.Identity, scale=G["edl"][:, tl:tl + 1])
                P.mm(slot(12 + tl), Zb[tl][0][:], ident_f, start=True, stop=True)
                P.copy("act", Yb[tl][0][:], slot(12 + tl))
                P.tt("dve", Ttb[tl][0][:], slot(12 + tl), ident_f, ALU.add)
                P.tt("pool", Tmb[tl][0][:], Zb[tl][0][:], ident_f, ALU.add)
        for st in range(1, 6):
            a_, b_ = (st - 1) % 2, st % 2
            last = (st == 5)
            for h2 in range(2):
                Yb, Zb, base = Yb_h[h2], Zb_h[h2], h2 * 8
                for tl in range(4):
                    P.mm(slot(base + tl), Zb[tl][a_][:], Yb[tl][a_][:], start=True, stop=True)
                    if not last:
                        P.mm(slot(base + 4 + tl), Yb[tl][a_][:], Zb[tl][a_][:], start=True, stop=True)
            for h2 in range(2):
                Yb, Zb, base = Yb_h[h2], Zb_h[h2], h2 * 8
                for tl in range(4):
                    P.copy("act", Yb[tl][b_][:], slot(base + tl))
                    if not last:
                        P.copy("act", Zb[tl][b_][:], slot(base + 4 + tl))
            for h2 in range(2):
                Yb, Zb, Ttb, Tmb, base = Yb_h[h2], Zb_h[h2], Ttb_h[h2], Tmb_h[h2], h2 * 8
                for tl in range(4):
                    P.mm(slot(base + tl), Tmb[tl][a_][:], Yb[tl][b_][:], start=True, stop=True)
                    if not last:
                        P.mm(slot(base + 4 + tl), Ttb[tl][a_][:], Zb[tl][b_][:], start=True, stop=True)
            for h2 in range(2):
                Ttb, Tmb, Ttbf, base = Ttb_h[h2], Tmb_h[h2], Ttbf_h[h2], h2 * 8
                for tl in range(4):
                    if last:
                        P.tt("dve", Ttbf[tl][:], slot(base + tl), Ttb[tl][a_][:], ALU.add)
                    else:
                        P.tt("dve", Ttb[tl][b_][:], slot(base + tl), Ttb[tl][a_][:], ALU.add)
                        P.tt("dve", Tmb[tl][b_][:], slot(base + 4 + tl), Tmb[tl][a_][:], ALU.add)
        for h2 in range(2):
            Ttbf, vbt, kbe, u_sb, wTA, wTB, base = Ttbf_h[h2], vbt_h[h2], kbe_h[h2], u_h[h2], wTA_h[h2], wTB_h[h2], h2 * 8
            for tl in range(4):
                P.mm(slot(base + tl), Ttbf[tl][:], vbt[tl][:], start=True, stop=True)
                P.mm(slot(base + 4 + tl), kbe[tl][:], Ttbf[tl][:], start=True, stop=True)
                P.copy("act", u_sb[tl][:], slot(base + tl))
                P.copy("dve", wTA[tl][:, 0:64], slot(base + 4 + tl)[:, 0:64])
                P.copy("act", wTB[tl][:, 64:128], slot(base + 4 + tl)[:, 64:128])
        for tl in range(4):
            tok = slice(tl * 128, (tl + 1) * 128)
            gt = blk * 4 + tl
            for j in range(2):
                pr = slice(j * 64, j * 64 + 64)
                for h2 in range(2):
                    G = {k: v[h2] for k, v in gat.items()}
                    S_, Sb_ = St[h2], Sbf[h2]
                    vnb, o1b = vn_h[h2][tl % 2], o1_h[h2][tl % 2]
                    pws = PSC[:, h2, 0:128]
                    po1 = PSC[:, h2, 128:256]
                    psu = PSC[:, h2, 256:384]
                    P.mm(pws, (wTA_h if j == 0 else wTB_h)[h2][tl][:], Sb_[:], start=True, stop=True)
                    P.mm(po1, qhT[h2][:, tok], Sb_[:], start=True, stop=True)
                    P.tt("dve", vnb[pr, :], u_h[h2][tl][pr, :], pws[pr, :], ALU.subtract)
                    P.act(o1b[pr, :], po1[pr, :], AF.Identity, scale=G["egc"][pr, tl:tl + 1])
                    P.mm(psu, kd_h[h2][tl][pr, :], vnb[pr, :], start=True, stop=True)
                    P.stt("dve", S_[:], S_[:], G["egl%d" % j][:, tl:tl + 1], psu, ALU.mult, ALU.add)
                    P.copy("act", Sb_[:], S_[:])
            for h2 in range(2):
                vnb, o1b, osb = vn_h[h2][tl % 2], o1_h[h2][tl % 2], osum_h[h2][tl % 2]
                osq, ssq, rinv = osq_h[h2], ssq_h[h2], rinv_h[h2]
                po2 = PSC[:, h2, 384:512]
                P.mm(po2, qkT_h[h2][tl][:], vnb[:], start=True, stop=True)
                P.tt("dve", osb[:], o1b[:], po2, ALU.add)
                P.tt("dve", osq[:], osb[:], osb[:], ALU.mult)
                P.generic("dve", "reduce_sum", (ssq[:], osq[:]), [osq[:]], [ssq[:]], axis=AX.X)
                P.act(rinv[:], ssq[:], AF.Sqrt, bias=eps6[:], scale=1.0 / 128.0)
                P.generic("dve", "reciprocal", (rinv[:], rinv[:]), [rinv[:]], [rinv[:]])
                P.stt("dve", o_all[:, gt, h2 * 128:(h2 + 1) * 128], osb[:], rinv[:], nz[h2][:, tl, :], ALU.mult, ALU.mult)
    for q4 in range(4):
        P.dma("sp", out_d[:, q4 * 16:(q4 + 1) * 16, :], o_all[:, q4 * 16:(q4 + 1) * 16, :], is_output=True)
    P.emit()
    return nc


def run_Mgdn(x_full, mod, inp):
    if "Mgdn" not in _CACHE:
        _CACHE["Mgdn"] = build_Mgdn()
    nc = _CACHE["Mgdn"]
    layer = 0
    wi = np.asarray(inp["gdn_w_in"])
    cwf = np.asarray(inp["gdn_conv_w"])
    p = np.arange(128)
    same = (p[:, None] // 64) == (p[None, :] // 64)
    ident = np.eye(128, dtype=np.float32)
    onesBD = same.astype(np.float32)
    triBD = (same & (p[:, None] <= p[None, :])).astype(np.float32)
    blk0 = np.broadcast_to((p[:, None] < 64), (128, 128)).astype(np.float32)
    blk1 = np.broadcast_to((p[:, None] >= 64), (128, 128)).astype(np.float32)
    posmask = np.where(same & (p[None, :] <= p[:, None]), 0.0, 1e4).astype(np.float32)
    posmaskT = np.where(same & (p[None, :] >= p[:, None]), 0.0, 1e4).astype(np.float32)
    negstrict = np.where(same & (p[None, :] < p[:, None]), -1.0, 0.0).astype(np.float32)
    cst = np.ascontiguousarray(np.stack([ident, onesBD, triBD, blk0, blk1, posmask, posmaskT, negstrict], 1))
    nwr = np.ascontiguousarray(np.broadcast_to(np.asarray(inp["gdn_norm_w"])[None, :], (128, 128))).astype(np.float32)
    identb = ident.astype(ml_dtypes.bfloat16)
    xTs = [xT_layout(x_full[b]) for b in range(2)]
    in_maps = []
    for core in range(NCORES):
        b, hp = divmod(core, 4)
        m = mod[b, layer]
        sh1, sc1 = m[0:1024], m[1024:2048]
        pm1 = np.ascontiguousarray(np.stack([pm(sc1), pm(sh1)], 1)).astype(np.float32)
        cols = []
        cws = []
        for h2 in range(2):
            hd = hp * 2 + h2
            for i in range(3):
                cols.append(wi[:, i * 1024 + hd * 128:i * 1024 + (hd + 1) * 128])
                cws.append(cwf[:, i * 1024 + hd * 128:i * 1024 + (hd + 1) * 128].T)
        for h2 in range(2):
            hd = hp * 2 + h2
            cols.append(wi[:, 3072 + hd * 128:3072 + (hd + 1) * 128])
        wcat = np.concatenate(cols, axis=1)
        h0, h1 = hp * 2, hp * 2 + 1
        wab = np.stack([wi[:, 4096 + h0], wi[:, 4096 + h1], wi[:, 4104 + h0], wi[:, 4104 + h1]], 1)
        hs = np.array([inp["gdn_dt_bias"][h0], inp["gdn_dt_bias"][h1], inp["gdn_a_log"][h0], inp["gdn_a_log"][h1]], np.float32)
        in_maps.append({
            "xT": xTs[b], "pm1": pm1, "w": wtile(wcat), "wab": wtile(wab),
            "cw": np.ascontiguousarray(np.stack(cws, 1)).astype(np.float32),
            "cst": cst, "hs": np.ascontiguousarray(np.broadcast_to(hs[None], (128, 4))), "nw": nwr, "identb": identb,
        })
    res = run_bass_kernel_spmd(nc, in_maps, core_ids=list(range(NCORES)))
    o_full = np.empty((2, S, 1024), ml_dtypes.bfloat16)
    for core in range(NCORES):
        b, hp = divmod(core, 4)
        o = np.asarray(res.results[core]["o"])
        o_full[b, :, hp * 256:(hp + 1) * 256] = o.transpose(1, 0, 2).reshape(S, 256)
    return oT_from_tokenmajor(o_full)


def kernel(**inputs):
    inp = {k: np.asarray(v) for k, v in inputs.items()}
    mod = run_C(inp)
    x = np.ascontiguousarray(inp["x"], dtype=np.float32)
    mixers = (run_Mgdn, run_Mret, run_Mgmlp, run_Msb)
    wouts = (inp["gdn_w_out"], inp["ret_w_out"], inp["gmlp_w_out"], inp["sb_w_out"])
    for layer in range(DEPTH):
        oT = mixers[layer](x, mod, inp)
        x = run_F(x, oT, layer, mod, inp, wouts[layer])
    return x.astype(np.float32)
```
